# Optimizing a Trainium2 kernel written in Bass

```python
import math
import jax, jax.numpy as jnp
from jax import lax
import numpy as np

D_MODEL = 2048
BATCH = 16
SEQ = 256
DEPTH = 1
DEC_BATCH = 4
DEC_SEQ = 2048
PAST_LEN = 512

GRID_W = 64
D_MIX = D_MODEL
D_HY = D_MIX // 2
HY_ORDER = 2
HY_SHORT = 3
HY_BANDS = 16
HY_EMB_DIM = 1 + 2 * HY_BANDS
HY_FILTER_HIDDEN = 64
HY_SIN_FREQ = 1.0
HY_WINDOW_SHIFT = 0.05
HY_DECAY_MIN = 3.07
HY_DECAY_MAX = 15.35
D_GDN = D_MIX - D_HY
GDN_HEADS = 8
GDN_DK = D_GDN // GDN_HEADS
GDN_DV = D_GDN // GDN_HEADS
GDN_SHORT = 3
GDN_CHUNK = 64
W_IN_COLS = 3 * D_HY + 4 * D_GDN + 4 * GDN_HEADS
D_FF = 5632
N_MOD = 9
RMS_EPS = 1e-6
POS_BASE = 10000.0

kernel_name = 'hyena_gdn_macaron_prefix_flow_step'


def rms_norm(x, g):
    xf = x.astype(jnp.float32)
    y = xf * lax.rsqrt(jnp.mean(xf * xf, axis=-1, keepdims=True) + RMS_EPS)
    return (y * g.astype(jnp.float32)).astype(x.dtype)


def l2_norm(x):
    return x * lax.rsqrt(jnp.sum(x * x, axis=-1, keepdims=True) + RMS_EPS)


def adaln(x, g, shift, scale):
    return rms_norm(x, g) * (1.0 + scale) + shift


def swiglu(h, wg, wu, wd):
    return (jax.nn.silu(h @ wg) * (h @ wu)) @ wd


def dw_conv_centred(u, w):
    k = w.shape[0]
    p = k // 2
    n = u.shape[1]
    up = jnp.pad(u, ((0, 0), (p, p), (0, 0)))
    out = up[:, 0:n, :] * w[0]
    for j in range(1, k):
        out = out + up[:, j:j + n, :] * w[j]
    return out


def grid_pos_embed(n_tok, dim):
    rows = n_tok // GRID_W
    r = jnp.broadcast_to(jnp.arange(rows, dtype=jnp.float32)[:, None], (rows, GRID_W)).reshape(-1)
    col = jnp.broadcast_to(jnp.arange(GRID_W, dtype=jnp.float32)[None, :], (rows, GRID_W)).reshape(-1)
    quarter = dim // 4
    omega = 1.0 / (POS_BASE ** (jnp.arange(quarter, dtype=jnp.float32) / quarter))
    ar = r[:, None] * omega[None]
    ac = col[:, None] * omega[None]
    return jnp.concatenate([jnp.sin(ar), jnp.cos(ar), jnp.sin(ac), jnp.cos(ac)], axis=-1)


def hyena_filter_spectra(n_tok, w1, b1, w2, b2, w3, decay):
    f32 = jnp.float32
    idx = jnp.arange(n_tok, dtype=f32)
    t = idx / (n_tok - 1)
    bands = jnp.arange(1, HY_BANDS + 1, dtype=f32)
    ang = (2.0 * math.pi / n_tok) * idx[:, None] * bands[None, :]
    feats = jnp.concatenate([t[:, None], jnp.cos(ang), jnp.sin(ang)], axis=-1)
    h = jnp.sin(HY_SIN_FREQ * (feats @ w1.astype(f32) + b1.astype(f32)))
    h = jnp.sin(HY_SIN_FREQ * (h @ w2.astype(f32) + b2.astype(f32)))
    h = (h @ w3.astype(f32)).reshape(n_tok, 2, HY_ORDER, D_HY)
    window = jnp.exp(-t[:, None, None] * jnp.abs(decay.astype(f32))[None]) + HY_WINDOW_SHIFT
    h = h * window[:, None]
    h_fwd = h[:, 0]
    h_bwd = h[:, 1]
    taps = jnp.concatenate([h_fwd, jnp.zeros_like(h_fwd[:1]), h_bwd[1:][::-1]], axis=0)
    taps = taps / jnp.sum(jnp.abs(taps), axis=0, keepdims=True)
    return jnp.fft.rfft(taps, axis=0)


def fft_long_conv(u, spec, d_bias):
    n = u.shape[1]
    uf = jnp.fft.rfft(u, n=2 * n, axis=1)
    y = jnp.fft.irfft(uf * spec[None], n=2 * n, axis=1)[:, :n]
    return y + u * d_bias


def hyena_mixer(proj, conv_w, w1, b1, w2, b2, w3, decay, d_bias):
    n = proj.shape[1]
    pc = dw_conv_centred(proj, conv_w).astype(jnp.float32)
    v, x1, x2 = jnp.split(pc, 3, axis=-1)
    spec = hyena_filter_spectra(n, w1, b1, w2, b2, w3, decay)
    d_bias = d_bias.astype(jnp.float32)
    z = x1 * fft_long_conv(v, spec[:, 0], d_bias[0])
    z = x2 * fft_long_conv(z, spec[:, 1], d_bias[1])
    return z


def chunk_gated_delta(q, k, v, g, beta, s0):
    b, n, nh, dk = q.shape
    c = GDN_CHUNK
    nc = n // c

    def chunks(t):
        t = t.reshape((b, nc, c) + t.shape[2:])
        return jnp.moveaxis(t, 3, 1)

    q, k, v, g, beta = chunks(q), chunks(k), chunks(v), chunks(g), chunks(beta)
    g_cum = jnp.cumsum(g, axis=-1)
    diff = g_cum[..., :, None] - g_cum[..., None, :]
    causal = jnp.tril(jnp.ones((c, c), dtype=bool))
    strict = jnp.tril(jnp.ones((c, c), dtype=bool), -1)
    decay = jnp.where(causal, jnp.exp(jnp.where(causal, diff, 0.0)), 0.0)
    k_beta = k * beta[..., None]
    a_low = jnp.where(strict, jnp.einsum('bhnid,bhnjd->bhnij', k_beta, k) * decay, 0.0)
    t_mat = a_low + jnp.eye(c, dtype=jnp.float32)
    rhs = jnp.concatenate([v * beta[..., None], k_beta * jnp.exp(g_cum)[..., None]], axis=-1)
    sol = lax.linalg.triangular_solve(t_mat, rhs, left_side=True, lower=True, unit_diagonal=True)
    dv = v.shape[-1]
    u = sol[..., :dv]
    w = sol[..., dv:]
    attn = jnp.einsum('bhnid,bhnjd->bhnij', q, k) * decay
    g_last = g_cum[..., -1]
    k_dec = k * jnp.exp(g_last[..., None] - g_cum)[..., None]
    q_dec = q * jnp.exp(g_cum)[..., None]
    xs = tuple(jnp.moveaxis(t, 2, 0) for t in (q_dec, k_dec, u, w, attn, g_last))

    def step(s, inp):
        q_d, k_d, u_i, w_i, a_i, gl = inp
        v_new = u_i - jnp.einsum('bhck,bhkv->bhcv', w_i, s)
        o = jnp.einsum('bhck,bhkv->bhcv', q_d, s) + jnp.einsum('bhij,bhjv->bhiv', a_i, v_new)
        s = s * jnp.exp(gl)[..., None, None] + jnp.einsum('bhck,bhcv->bhkv', k_d, v_new)
        return s, o

    s_fin, o = lax.scan(step, s0, xs)
    o = jnp.transpose(o, (1, 0, 3, 2, 4)).reshape(b, n, nh, dv)
    return o, s_fin


def gdn_mixer(qkv, z, ab, conv_w, a_log, dt_bias, o_norm, s0):
    f32 = jnp.float32
    b, n, _ = qkv.shape
    qkv = jax.nn.silu(dw_conv_centred(qkv, conv_w)).astype(f32)
    q, k, v = jnp.split(qkv, 3, axis=-1)
    q = l2_norm(q.reshape(b, n, GDN_HEADS, GDN_DK)) * (GDN_DK ** -0.5)
    k = l2_norm(k.reshape(b, n, GDN_HEADS, GDN_DK))
    v = v.reshape(b, n, GDN_HEADS, GDN_DV)
    ab = ab.astype(f32).reshape(b, n, 2, 2, GDN_HEADS)
    g = -jnp.exp(a_log.astype(f32)) * jax.nn.softplus(ab[:, :, :, 0] + dt_bias.astype(f32))
    beta = jax.nn.sigmoid(ab[:, :, :, 1])
    o_f, s_f = chunk_gated_delta(q, k, v, g[:, :, 0], beta[:, :, 0], s0[:, 0])
    o_b, s_b = chunk_gated_delta(q[:, ::-1], k[:, ::-1], v[:, ::-1], g[:, ::-1, 1], beta[:, ::-1, 1], s0[:, 1])
    o = o_f + o_b[:, ::-1]
    o = rms_norm(o, o_norm) * jax.nn.silu(z.astype(f32).reshape(b, n, GDN_HEADS, GDN_DV))
    return o.reshape(b, n, D_GDN), jnp.stack([s_f, s_b], axis=1)


def trunk_layer(x, mod, s0, norm_pre, norm_post, ffn_wg, ffn_wu, ffn_wd, w_in, w_out,
                hy_conv_w, hy_f_w1, hy_f_b1, hy_f_w2, hy_f_b2, hy_f_w3, hy_decay, hy_bias, hy_out_norm,
                gdn_conv_w, gdn_a_log, gdn_dt_bias, gdn_o_norm):
    h = adaln(x, norm_pre[0], mod[:, :, 0], mod[:, :, 1])
    x = x + 0.5 * mod[:, :, 2] * rms_norm(swiglu(h, ffn_wg[0], ffn_wu[0], ffn_wd[0]), norm_post[0])

    h = adaln(x, norm_pre[1], mod[:, :, 3], mod[:, :, 4])
    proj = h @ w_in
    o0 = 3 * D_HY
    o1 = o0 + 3 * D_GDN
    o2 = o1 + D_GDN
    y_hy = rms_norm(hyena_mixer(proj[..., :o0], hy_conv_w, hy_f_w1, hy_f_b1, hy_f_w2, hy_f_b2,
                                hy_f_w3, hy_decay, hy_bias), hy_out_norm).astype(x.dtype)
    y_gdn, s_fin = gdn_mixer(proj[..., o0:o1], proj[..., o1:o2], proj[..., o2:], gdn_conv_w,
                             gdn_a_log, gdn_dt_bias, gdn_o_norm, s0)
    y = jnp.concatenate([y_hy, y_gdn.astype(x.dtype)], axis=-1) @ w_out
    x = x + mod[:, :, 5] * rms_norm(y, norm_post[1])

    h = adaln(x, norm_pre[2], mod[:, :, 6], mod[:, :, 7])
    x = x + 0.5 * mod[:, :, 8] * rms_norm(swiglu(h, ffn_wg[1], ffn_wu[1], ffn_wd[1]), norm_post[2])
    return x, s_fin


def setup_inputs(seed: int = 0) -> dict:
    key = jax.random.key(seed)
    ks = jax.random.split(key, 32)
    f32 = jnp.float32

    def nrm(k, shape, s):
        return jax.random.normal(k, shape, f32) * s

    dt = jnp.exp(jax.random.uniform(ks[25], (DEPTH, 2, GDN_HEADS), f32, math.log(1e-3), math.log(1e-1)))
    return {
        'x_prompt': nrm(ks[0], (BATCH, SEQ, D_MODEL), 1.0),
        'x_sample': nrm(ks[1], (DEC_BATCH, DEC_SEQ, D_MODEL), 1.0),
        'state_gdn': nrm(ks[2], (DEC_BATCH, DEPTH, 2, GDN_HEADS, GDN_DK, GDN_DV), GDN_DK ** -0.5),
        'c': nrm(ks[3], (DEC_BATCH, D_MODEL), 1.0),
        'c_ctx': nrm(ks[4], (D_MODEL,), 1.0),
        'ada_w': nrm(ks[5], (DEPTH, D_MODEL, N_MOD * D_MODEL), 0.5 * D_MODEL ** -0.5),
        'ada_b': nrm(ks[6], (DEPTH, N_MOD * D_MODEL), 0.02),
        'norm_pre': 1.0 + nrm(ks[7], (DEPTH, 3, D_MODEL), 0.02),
        'norm_post': 1.0 + nrm(ks[8], (DEPTH, 3, D_MODEL), 0.02),
        'ffn_wg': nrm(ks[9], (DEPTH, 2, D_MODEL, D_FF), D_MODEL ** -0.5),
        'ffn_wu': nrm(ks[10], (DEPTH, 2, D_MODEL, D_FF), D_MODEL ** -0.5),
        'ffn_wd': nrm(ks[11], (DEPTH, 2, D_FF, D_MODEL), D_FF ** -0.5),
        'w_in': nrm(ks[12], (DEPTH, D_MODEL, W_IN_COLS), D_MODEL ** -0.5),
        'w_out': nrm(ks[13], (DEPTH, D_MIX, D_MODEL), D_MIX ** -0.5),
        'hy_conv_w': nrm(ks[14], (DEPTH, HY_SHORT, 3 * D_HY), HY_SHORT ** -0.5),
        'hy_f_w1': nrm(ks[15], (DEPTH, HY_EMB_DIM, HY_FILTER_HIDDEN), HY_EMB_DIM ** -0.5),
        'hy_f_b1': nrm(ks[16], (DEPTH, HY_FILTER_HIDDEN), 0.1),
        'hy_f_w2': nrm(ks[17], (DEPTH, HY_FILTER_HIDDEN, HY_FILTER_HIDDEN), HY_FILTER_HIDDEN ** -0.5),
        'hy_f_b2': nrm(ks[18], (DEPTH, HY_FILTER_HIDDEN), 0.1),
        'hy_f_w3': nrm(ks[19], (DEPTH, HY_FILTER_HIDDEN, 2 * HY_ORDER * D_HY), HY_FILTER_HIDDEN ** -0.5),
        'hy_decay': jax.random.uniform(ks[20], (DEPTH, HY_ORDER, D_HY), f32, HY_DECAY_MIN, HY_DECAY_MAX),
        'hy_bias': nrm(ks[21], (DEPTH, HY_ORDER, D_HY), 1.0),
        'hy_out_norm': 1.0 + nrm(ks[22], (DEPTH, D_HY), 0.02),
        'gdn_conv_w': nrm(ks[23], (DEPTH, GDN_SHORT, 3 * D_GDN), GDN_SHORT ** -0.5),
        'gdn_a_log': jnp.log(jax.random.uniform(ks[24], (DEPTH, 2, GDN_HEADS), f32, 1.0, 16.0)),
        'gdn_dt_bias': dt + jnp.log(-jnp.expm1(-dt)),
        'gdn_o_norm': 1.0 + nrm(ks[26], (DEPTH, GDN_DV), 0.02),
    }


def reference(x_prompt, x_sample, state_gdn, c, c_ctx, ada_w, ada_b, norm_pre, norm_post,
              ffn_wg, ffn_wu, ffn_wd, w_in, w_out, hy_conv_w, hy_f_w1, hy_f_b1, hy_f_w2, hy_f_b2,
              hy_f_w3, hy_decay, hy_bias, hy_out_norm, gdn_conv_w, gdn_a_log, gdn_dt_bias, gdn_o_norm):
    n_ctx_req = x_prompt.shape[0]
    n_lat_req = x_sample.shape[0]
    xp = x_prompt
    xs = x_sample + grid_pos_embed(x_sample.shape[1], D_MODEL).astype(x_sample.dtype)[None]
    ctx_states = []
    for l in range(DEPTH):
        mod_ctx = (jax.nn.silu(c_ctx)[None] @ ada_w[l] + ada_b[l]).reshape(1, 1, N_MOD, D_MODEL)
        mod_lat = (jax.nn.silu(c) @ ada_w[l] + ada_b[l]).reshape(n_lat_req, 1, N_MOD, D_MODEL)
        layer_w = (norm_pre[l], norm_post[l], ffn_wg[l], ffn_wu[l], ffn_wd[l], w_in[l], w_out[l],
                   hy_conv_w[l], hy_f_w1[l], hy_f_b1[l], hy_f_w2[l], hy_f_b2[l], hy_f_w3[l], hy_decay[l],
                   hy_bias[l], hy_out_norm[l], gdn_conv_w[l], gdn_a_log[l], gdn_dt_bias[l], gdn_o_norm[l])
        s0_ctx = jnp.zeros((n_ctx_req, 2, GDN_HEADS, GDN_DK, GDN_DV), jnp.float32)
        xp, s_ctx = trunk_layer(xp, mod_ctx, s0_ctx, *layer_w)
        xs, _ = trunk_layer(xs, mod_lat, state_gdn[:, l].astype(jnp.float32), *layer_w)
        ctx_states.append(s_ctx)
    new_state_gdn = jnp.stack(ctx_states, axis=1)
    return (xp, xs, new_state_gdn)
```

```python
import math
import numpy as np
import concourse.bass as bass
import concourse.mybir as mybir
from concourse.bass_utils import run_bass_kernel_spmd

F32 = mybir.dt.float32
BF16 = mybir.dt.bfloat16
AF = mybir.ActivationFunctionType
ALU = mybir.AluOpType

D = 2048
DFF = 5632
T = 2048
NT = 16
TB = 512
WIN = 7200
EPS = 1e-6
FFN_PARTS = 4
MIX_PARTS = 99
GDN_F32R = False
SKIP_H2 = False
GDN_INTERLEAVE = True
DEBUG = False
RUN_MIXER = True


class StopMix(Exception):
    pass


def chk(n):
    if MIX_PARTS <= n:
        raise StopMix()
EVAC_ACT = False
STAGE = 9


class Buf:
    __slots__ = ("w", "r", "dsem", "dcnt", "name")

    def __init__(self, name=""):
        self.w = {}
        self.r = {}
        self.dsem = None
        self.dcnt = 0
        self.name = name


class KB:
    def __init__(self):
        self.nc = bass.Bass("TRN2", target_bir_lowering=False)
        nc = self.nc
        self.E = {}
        for nm, e in (("pe", nc.tensor), ("act", nc.scalar), ("dve", nc.vector),
                      ("pool", nc.gpsimd), ("sp", nc.sync)):
            sem = nc.semaphore("s_" + nm).__enter__()
            self.E[nm] = dict(e=e, sem=sem, cnt=0, waited={}, name=nm)
        self.dsems = []
        self.ctx = []
        self.n_ins = 0

    def sb(self, name, shape, dt):
        self.uid = getattr(self, "uid", 0) + 1
        cm = self.nc.sbuf_tensor("%s_s%d" % (name, self.uid), shape, dt)
        t = cm.__enter__()
        self.ctx.append(cm)
        return t

    def ps(self, name, shape, dt=F32):
        self.uid = getattr(self, "uid", 0) + 1
        cm = self.nc.psum_tensor("%s_p%d" % (name, self.uid), shape, dt)
        t = cm.__enter__()
        self.ctx.append(cm)
        return t

    def mark(self):
        return len(self.ctx)

    def release(self, mark):
        self.barrier()
        while len(self.ctx) > mark:
            cm = self.ctx.pop()
            cm.__exit__(None, None, None)

    def _deps(self, reads, writes):
        deps = {}
        for b in reads:
            for k, v in b.w.items():
                if k not in deps or deps[k][1] < v[1]:
                    deps[k] = v
        for b in writes:
            for dd in (b.w, b.r):
                for k, v in dd.items():
                    if k not in deps or deps[k][1] < v[1]:
                        deps[k] = v
        return deps

    def _wait(self, E, deps):
        for k, (semobj, val) in deps.items():
            if E["name"] == "pe" and semobj is E["sem"]:
                continue
            if E["waited"].get(k, 0) >= val:
                continue
            E["e"].wait_ge(semobj, val)
            E["waited"][k] = val

    def _record(self, tag, reads, writes):
        k = id(tag[0])
        for b in reads:
            if k not in b.r or b.r[k][1] < tag[1]:
                b.r[k] = tag
        for b in writes:
            b.w = {k: tag}
            b.r = {}

    def op(self, eng, emit, reads=(), writes=(), inc=True):
        E = self.E[eng]
        self._wait(E, self._deps(reads, writes))
        ins = emit(E["e"])
        self.n_ins += 1
        if inc:
            E["cnt"] += 1
            ins.then_inc(E["sem"], 1)
            tag = (E["sem"], E["cnt"])
        else:
            tag = (E["sem"], E["cnt"] + 1)
        self._record(tag, reads, writes)

    def dma(self, q, out, in_, reads=(), writes=(), **kw):
        E = self.E[q]
        self._wait(E, self._deps(reads, writes))
        b = writes[0]
        if b.dsem is None:
            b.dsem = self.nc.semaphore("d%d" % len(self.dsems)).__enter__()
            self.dsems.append(b)
        b.dcnt += 16
        E["e"].dma_start(out=out, in_=in_, **kw).then_inc(b.dsem, 16)
        self.n_ins += 1
        self._record((b.dsem, b.dcnt), reads, writes)

    def barrier(self):
        allsem = [(E["sem"], E["cnt"]) for E in self.E.values() if E["cnt"] > 0]
        allsem += [(b.dsem, b.dcnt) for b in self.dsems]
        for E in self.E.values():
            deps = {id(s): (s, v) for s, v in allsem if s is not E["sem"]}
            self._wait(E, deps)


def bufs(n, name=""):
    return [Buf("%s%d" % (name, i)) for i in range(n)]


def build():
    kb = KB()
    nc = kb.nc
    op, dma = kb.op, kb.dma

    def din(name, shape, dt=F32):
        return nc.dram_tensor(name, list(shape), dt, kind="ExternalInput").ap()

    x_in = din("x", [T, D])
    pos_in = din("pos", [T, D])
    cvec = din("cvec", [1, D])
    ada_w = din("ada_w", [D, 9 * D])
    ada_b = din("ada_b", [1, 9 * D])
    norm_pre = din("norm_pre", [3, D])
    norm_post = din("norm_post", [3, D])
    ffn_wg = din("ffn_wg", [2, D, DFF])
    ffn_wu = din("ffn_wu", [2, D, DFF])
    ffn_wd = din("ffn_wd", [2, DFF, D])
    ident_in = din("ident", [128, 128])
    w_in = din("w_in", [D, WIN])
    w_out = din("w_out", [D, D])
    gdn_conv_w = din("gdn_conv_w", [3, 3072])
    hy_conv_w = din("hy_conv_w", [3, 3072])
    gdn_o_norm = din("gdn_o_norm", [1, 128])
    gmask_in = din("gmask", [7, 128, 128])
    keep_in = din("keep", [1, 8])
    bmask_in = din("bmask", [1, 7])
    dtb_in = din("dtb", [1, 256])
    alog_in = din("alog", [1, 256])
    s0_in = din("s0", [2, 8, 128, 128])
    hy_f_w1 = din("hy_f_w1", [33, 64])
    hy_f_b1 = din("hy_f_b1", [1, 64])
    hy_f_w2 = din("hy_f_w2", [64, 64])
    hy_f_b2 = din("hy_f_b2", [1, 64])
    hy_f_w3 = din("hy_f_w3", [64, 4096])
    hy_decay = din("hy_decay", [2, 1024])
    hy_bias = din("hy_bias", [2, 1024])
    hy_out_norm = din("hy_out_norm", [1, 1024])
    feats_in = din("feats", [33, T])
    hcol_in = din("hcol", [128, 64])
    nrep_in = din("nrep", [128, 1])
    CF_t = din("CF_t", [16, 128, 16, 128], BF16)
    SF_t = din("SF_t", [16, 128, 16, 128], BF16)
    GC_t = din("GC_t", [16, 128, 16, 128], BF16)
    GS_t = din("GS_t", [16, 128, 16, 128], BF16)
    tapsS = nc.dram_tensor("tapsS", [T, 2048], BF16, kind="Internal").ap()
    tapsD = nc.dram_tensor("tapsD", [T, 2048], BF16, kind="Internal").ap()
    tapsb = Buf("taps")
    spec_re = nc.dram_tensor("spec_re", [T, 2048], F32, kind="Internal").ap()
    spec_im = nc.dram_tensor("spec_im", [T, 2048], F32, kind="Internal").ap()
    spec_rb = nc.dram_tensor("spec_rb", [T, 2048], F32, kind="Internal").ap()
    specb = Buf("spec")
    st_out = nc.dram_tensor("st_out", [8, 2, 8, 128, 128], F32, kind="ExternalOutput").ap()
    stoutb = [Buf("stout")] * 128
    ycatT = nc.dram_tensor("ycatT", [16, 128, T], BF16, kind="Internal").ap()
    ycb = [b_ for b_ in bufs(4, "ycat") for _ in range(4)]
    dbgb = Buf("dbg")
    if DEBUG:
        dbg_gv = nc.dram_tensor("dbg_gv", [128, 12 * 256], F32, kind="ExternalOutput").ap()
        dbg_oa = nc.dram_tensor("dbg_oa", [128, T], F32, kind="ExternalOutput").ap()
        dbg_q = nc.dram_tensor("dbg_q", [128, T], F32, kind="ExternalOutput").ap()
        dbg_k = nc.dram_tensor("dbg_k", [128, T], F32, kind="ExternalOutput").ap()
        dbg_z = nc.dram_tensor("dbg_z", [128, T], F32, kind="ExternalOutput").ap()
    y_out = nc.dram_tensor("y", [T, D], F32, kind="ExternalOutput").ap()
    modrow = nc.dram_tensor("modrow", [1, 9 * D], F32, kind="Internal").ap()
    xres = nc.dram_tensor("xres", [T, D], F32, kind="Internal").ap()

    ybuf = bufs(NT, "y")
    modb = Buf("modrow")
    NB = Buf("none")

    ident = kb.sb("ident", [128, 128], F32)
    identb = Buf("ident")
    dma("sp", ident[:], ident_in[:, :], writes=[identb])
    modcol = kb.sb("modcol", [128, 144], F32)
    npre = kb.sb("npre", [128, 48], F32)
    Acol = kb.sb("Acol", [128, 48], F32)
    colb = Buf("cols")
    hyss = kb.sb("hyss", [128, 16], F32)
    hyssb = Buf("hyss")
    PSALL = kb.ps("psall", [128, 4096])
    PS = [PSALL[:, i * 512:(i + 1) * 512] for i in range(8)]
    PSB = bufs(8, "ps")

    mk = kb.mark()
    cs = kb.sb("cs", [128, 16], F32)
    css = kb.sb("css", [128, 16], F32)
    csb = Buf("cs")
    with nc.allow_non_contiguous_dma(reason="small vector relayout"):
        dma("sp", cs[:], cvec.rearrange("o (kc p) -> p (o kc)", p=128), writes=[csb])
    op("act", lambda e: e.activation(out=css[:], in_=cs[:], func=AF.Silu), reads=[csb], writes=[csb])
    aw = [kb.sb("aw%d" % i, [128, 16, 512], F32) for i in range(2)]
    awb = bufs(2, "aw")
    abr = [kb.sb("abr%d" % i, [1, 512], F32) for i in range(2)]
    abrb = bufs(2, "abr")
    mrow = [kb.sb("mrow%d" % i, [1, 512], F32) for i in range(2)]
    mrowb = bufs(2, "mrow")
    for cg in range(36):
        s = cg % 2
        dma("sp", aw[s][:], ada_w[:, cg * 512:(cg + 1) * 512].rearrange("(kc p) n -> p kc n", p=128),
            writes=[awb[s]])
        dma("sp", abr[s][:], ada_b[:, cg * 512:(cg + 1) * 512], writes=[abrb[s]])
        bk = cg % 2
        for kc in range(16):
            op("pe", lambda e, kc=kc: e.matmul(PS[bk][0:1, :], lhsT=css[:, kc:kc + 1], rhs=aw[s][:, kc, :],
                                               start=(kc == 0), stop=(kc == 15)),
               reads=[csb, awb[s]], writes=[PSB[bk]], inc=(kc == 15))
        op("dve", lambda e: e.tensor_tensor(out=mrow[s][:], in0=PS[bk][0:1, :], in1=abr[s][:], op=ALU.add),
           reads=[PSB[bk], abrb[s]], writes=[mrowb[s]])
        dma("sp", modrow[:, cg * 512:(cg + 1) * 512], mrow[s][:], reads=[mrowb[s]], writes=[modb])
    with nc.allow_non_contiguous_dma(reason="small vector relayout"):
        dma("sp", modcol[:], modrow.rearrange("o (m kc p) -> p (o m kc)", p=128, kc=16), reads=[modb], writes=[colb])
        dma("sp", npre[:], norm_pre.rearrange("s (kc p) -> p (s kc)", p=128), writes=[colb])
    for s in range(3):
        sc = modcol[:, (3 * s + 1) * 16:(3 * s + 2) * 16]
        op("dve", lambda e, s=s, sc=sc: e.scalar_tensor_tensor(
            out=Acol[:, s * 16:(s + 1) * 16], in0=sc, scalar=1.0, in1=npre[:, s * 16:(s + 1) * 16],
            op0=ALU.add, op1=ALU.mult), reads=[colb], writes=[colb])
    kb.release(mk)

    def Bcol(s, kc):
        return modcol[:, 3 * s * 16 + kc:3 * s * 16 + kc + 1]

    def rstd_from_ss(ss_ap, out_ap, tmp_ap, rb, n):
        op("dve", lambda e: e.tensor_scalar(out=tmp_ap, in0=ss_ap, scalar1=1.0 / n, scalar2=EPS,
                                            op0=ALU.mult, op1=ALU.add), reads=[rb], writes=[rb])
        op("act", lambda e: e.activation(out=tmp_ap, in_=tmp_ap, func=AF.Sqrt), reads=[rb], writes=[rb])
        op("dve", lambda e: e.reciprocal(out=out_ap, in_=tmp_ap), reads=[rb], writes=[rb])

    def ffn_stage(si, layer, xsrc, xsrc_bufs, add_pos, xdst):
        mk = kb.mark()
        hT = kb.sb("hT", [128, 16, TB], BF16)
        hTb = [[Buf() for _ in range(4)] for _ in range(16)]
        aT = kb.sb("aT", [128, 44, TB], BF16)
        aTb = bufs(44, "aT")
        wg_s = [kb.sb("wg%d" % i, [128, 16, 256], BF16) for i in range(2)]
        wu_s = [kb.sb("wu%d" % i, [128, 16, 256], BF16) for i in range(2)]
        wgb, wub = bufs(2, "wg"), bufs(2, "wu")
        wd_s = [kb.sb("wd%d" % i, [128, 2, 1024], BF16) for i in range(2)]
        wdb = bufs(2, "wd")
        xin = [kb.sb("xin%d" % i, [128, D], F32) for i in range(4)]
        xinb = bufs(4, "xin")
        xn = kb.sb("xn", [128, D], F32)
        xnb = Buf("xn")
        junk = kb.sb("junk", [128, D], BF16)
        junkb = Buf("junk")
        yacc = kb.sb("yacc", [128, 4, 1024], F32)
        yaccb = bufs(4, "yacc")
        tmp = [kb.sb("tmp%d" % i, [128, 512], F32) for i in range(2)]
        tmpb = bufs(2, "tmp")
        sg = [kb.sb("sg%d" % i, [128, 512], F32) for i in range(2)]
        sgb = bufs(2, "sg")
        Gp = kb.sb("Gp", [128, D], F32)
        gpost = kb.sb("gpost", [128, D], F32)
        Gpb = Buf("Gp")
        st = kb.sb("st", [128, 32], F32)
        stb = bufs(8, "st")
        gi = 3 * si + 2
        dma("sp", Gp[:], modrow[0, gi * D:(gi + 1) * D].partition_broadcast(128), reads=[modb], writes=[Gpb])
        dma("sp", gpost[:], norm_post[si, :].partition_broadcast(128), writes=[Gpb])
        gsc = 1.0 if si == 1 else 0.5
        op("dve", lambda e: e.scalar_tensor_tensor(out=Gp[:], in0=Gp[:], scalar=gsc, in1=gpost[:],
                                                   op0=ALU.mult, op1=ALU.mult), reads=[Gpb], writes=[Gpb])
        evac_i = [0]
        for tb in range(T // TB):
            if FFN_PARTS < 0.15:
                break
            for m in range(4):
                t = tb * 4 + m
                dma("sp", xin[m][:], xsrc[t * 128:(t + 1) * 128, :], reads=[xsrc_bufs[t]], writes=[xinb[m]])
                if add_pos:
                    dma("sp", xn[:], pos_in[t * 128:(t + 1) * 128, :], writes=[xnb])
                    op("dve", lambda e, m=m: e.tensor_tensor(out=xin[m][:], in0=xin[m][:], in1=xn[:], op=ALU.add),
                       reads=[xnb, xinb[m]], writes=[xinb[m]])
                sb_ = stb[m]
                c0 = m * 4
                op("act", lambda e, m=m, c0=c0: e.activation(out=junk[:], in_=xin[m][:], func=AF.Square,
                                                            accum_out=st[:, c0:c0 + 1]),
                   reads=[xinb[m]], writes=[junkb, sb_])
                if FFN_PARTS < 0.25:
                    continue
                rstd_from_ss(st[:, c0:c0 + 1], st[:, c0 + 1:c0 + 2], st[:, c0 + 2:c0 + 3], sb_, D)
                op("dve", lambda e, m=m, c0=c0: e.tensor_scalar(out=xn[:], in0=xin[m][:], scalar1=st[:, c0 + 1:c0 + 2],
                                                               scalar2=None, op0=ALU.mult),
                   reads=[xinb[m], sb_], writes=[xnb])
                for q in range(4):
                    if FFN_PARTS < 0.35:
                        break
                    bk = q
                    for kk in range(4):
                        kc = q * 4 + kk
                        op("pe", lambda e, kc=kc, kk=kk, bk=bk: e.transpose(
                            out=PS[bk][:, kk * 128:(kk + 1) * 128], in_=xn[:, kc * 128:(kc + 1) * 128], identity=ident[:]),
                           reads=[xnb, identb], writes=[PSB[bk]], inc=(kk == 3))
                    for kk in range(4):
                        if FFN_PARTS < 0.45:
                            break
                        kc = q * 4 + kk
                        src = PS[bk][:, kk * 128:(kk + 1) * 128]
                        dst = hT[:, kc, m * 128:(m + 1) * 128]
                        a_ap = Acol[:, si * 16 + kc:si * 16 + kc + 1]
                        b_ap = Bcol(si, kc)
                        if evac_i[0] % 2 == 0 and EVAC_ACT:
                            op("act", lambda e, src=src, dst=dst, a_ap=a_ap, b_ap=b_ap: e.activation(
                                out=dst, in_=src, func=AF.Identity, scale=a_ap, bias=b_ap),
                               reads=[PSB[bk], colb], writes=[hTb[kc][m]])
                        else:
                            op("dve", lambda e, src=src, dst=dst, a_ap=a_ap, b_ap=b_ap: e.tensor_scalar(
                                out=dst, in0=src, scalar1=a_ap, scalar2=b_ap, op0=ALU.mult, op1=ALU.add),
                               reads=[PSB[bk], colb], writes=[hTb[kc][m]])
                        evac_i[0] += 1
            if FFN_PARTS < 2:
                break
            for jp in range(22):
                s = jp % 2
                dma("pool", wg_s[s][:], ffn_wg[layer, :, jp * 256:(jp + 1) * 256].rearrange("(kc p) n -> p kc n", p=128),
                    writes=[wgb[s]])
                dma("pool", wu_s[s][:], ffn_wu[layer, :, jp * 256:(jp + 1) * 256].rearrange("(kc p) n -> p kc n", p=128),
                    writes=[wub[s]])
                for jj in range(2):
                    j = jp * 2 + jj
                    gb, ub = 4 + (j % 2), 6 + (j % 2)
                    for (wt, wb, bk) in ((wg_s[s], wgb[s], gb), (wu_s[s], wub[s], ub)):
                        for kc in range(16):
                            op("pe", lambda e, wt=wt, bk=bk, kc=kc, jj=jj: e.matmul(
                                PS[bk][:, :], lhsT=wt[:, kc, jj * 128:(jj + 1) * 128], rhs=hT[:, kc, :],
                                start=(kc == 0), stop=(kc == 15)),
                               reads=[wb] + hTb[kc], writes=[PSB[bk]], inc=(kc == 15))
                    ss_ = j % 2
                    op("act", lambda e, ss_=ss_, gb=gb: e.activation(out=sg[ss_][:], in_=PS[gb][:, :], func=AF.Silu),
                       reads=[PSB[gb]], writes=[sgb[ss_]])
                    op("dve", lambda e, ss_=ss_, ub=ub, j=j: e.tensor_tensor(out=aT[:, j, :], in0=sg[ss_][:], in1=PS[ub][:, :],
                                                                          op=ALU.mult),
                       reads=[sgb[ss_], PSB[ub]], writes=[aTb[j]])
            if FFN_PARTS < 3:
                break
            for half in range(2):
                for jp in range(22):
                    s = jp % 2
                    dma("pool", wd_s[s][:],
                        ffn_wd[layer, jp * 256:(jp + 1) * 256, half * 1024:(half + 1) * 1024].rearrange("(jj p) n -> p jj n", p=128),
                        writes=[wdb[s]])
                    for jj in range(2):
                        j = jp * 2 + jj
                        for m in range(4):
                            for nb in range(2):
                                bk = m * 2 + nb
                                op("pe", lambda e, j=j, jj=jj, m=m, nb=nb, bk=bk, s=s: e.matmul(
                                    PS[bk][:, :], lhsT=aT[:, j, m * 128:(m + 1) * 128],
                                    rhs=wd_s[s][:, jj, nb * 512:(nb + 1) * 512], start=(j == 0), stop=(j == 43)),
                                   reads=[aTb[j], wdb[s]], writes=[PSB[bk]], inc=(j == 43 or (jj == 1 and m == 3 and nb == 1)))
                if half == 0:
                    for m in range(4):
                        for nb in range(2):
                            bk = m * 2 + nb
                            eng = "act" if nb == 0 else "dve"
                            if eng == "act":
                                op("act", lambda e, m=m, nb=nb, bk=bk: e.activation(
                                    out=yacc[:, m, nb * 512:(nb + 1) * 512], in_=PS[bk][:, :], func=AF.Identity),
                                   reads=[PSB[bk]], writes=[yaccb[m]])
                            else:
                                op("dve", lambda e, m=m, nb=nb, bk=bk: e.tensor_copy(
                                    out=yacc[:, m, nb * 512:(nb + 1) * 512], in_=PS[bk][:, :]),
                                   reads=[PSB[bk]], writes=[yaccb[m]])
            if FFN_PARTS < 4:
                break
            for m in range(4):
                t = tb * 4 + m
                sb_ = stb[4 + m]
                c0 = 16 + m * 4
                op("act", lambda e, m=m, c0=c0: e.activation(out=junk[:, 0:1024], in_=yacc[:, m, :], func=AF.Square,
                                                            accum_out=st[:, c0:c0 + 1]),
                   reads=[yaccb[m]], writes=[junkb, sb_])
                for nb in range(2):
                    bk = m * 2 + nb
                    op("act", lambda e, nb=nb, bk=bk, c0=c0: e.activation(out=junk[:, 0:512], in_=PS[bk][:, :], func=AF.Square,
                                                                        accum_out=st[:, c0 + 1 + nb:c0 + 2 + nb]),
                       reads=[PSB[bk]], writes=[junkb, sb_])
                op("dve", lambda e, c0=c0: e.tensor_tensor(out=st[:, c0:c0 + 1], in0=st[:, c0:c0 + 1], in1=st[:, c0 + 1:c0 + 2],
                                                          op=ALU.add), reads=[sb_], writes=[sb_])
                op("dve", lambda e, c0=c0: e.tensor_tensor(out=st[:, c0:c0 + 1], in0=st[:, c0:c0 + 1], in1=st[:, c0 + 2:c0 + 3],
                                                          op=ALU.add), reads=[sb_], writes=[sb_])
                rstd_from_ss(st[:, c0:c0 + 1], st[:, c0 + 3:c0 + 4], st[:, c0 + 1:c0 + 2], sb_, D)
                rs = st[:, c0 + 3:c0 + 4]
                for q in range(4):
                    ts_ = q % 2
                    cols = slice(q * 512, (q + 1) * 512)
                    if q < 2:
                        src, srcb = yacc[:, m, cols], yaccb[m]
                    else:
                        bk = m * 2 + (q - 2)
                        src, srcb = PS[bk][:, :], PSB[bk]
                    op("dve", lambda e, src=src, ts_=ts_, cols=cols: e.scalar_tensor_tensor(
                        out=tmp[ts_][:], in0=src, scalar=rs, in1=Gp[:, cols], op0=ALU.mult, op1=ALU.mult),
                       reads=[srcb, sb_, Gpb], writes=[tmpb[ts_]])
                    op("dve", lambda e, ts_=ts_, cols=cols, m=m: e.tensor_tensor(
                        out=xin[m][:, cols], in0=tmp[ts_][:], in1=xin[m][:, cols], op=ALU.add),
                       reads=[tmpb[ts_], xinb[m]], writes=[xinb[m]])
                dma("sp", xdst[t * 128:(t + 1) * 128, :], xin[m][:], reads=[xinb[m]], writes=[ybuf[t]])
        kb.release(mk)

    def mixer_stage():
        si = 1
        mk = kb.mark()
        hTa = kb.sb("hTa", [128, 16, T], BF16)
        hTab = [[Buf() for _ in range(NT)] for _ in range(16)]
        amk = kb.mark()
        xin = [kb.sb("mxin%d" % i, [128, D], F32) for i in range(2)]
        xinb = bufs(2, "mxin")
        xn = kb.sb("mxn", [128, D], F32)
        xnb = Buf("mxn")
        junk = kb.sb("mjunk", [128, D], BF16)
        junkb = Buf("mjunk")
        st = kb.sb("mst", [128, 64], F32)
        stb = bufs(16, "mst")
        for t in range(NT):
            s = t % 2
            dma("sp", xin[s][:], xres[t * 128:(t + 1) * 128, :], reads=[ybuf[t]], writes=[xinb[s]])
            sb_ = stb[t]
            c0 = t * 4
            op("act", lambda e: e.activation(out=junk[:], in_=xin[s][:], func=AF.Square, accum_out=st[:, c0:c0 + 1]),
               reads=[xinb[s]], writes=[junkb, sb_])
            rstd_from_ss(st[:, c0:c0 + 1], st[:, c0 + 1:c0 + 2], st[:, c0 + 2:c0 + 3], sb_, D)
            op("dve", lambda e: e.tensor_scalar(out=xn[:], in0=xin[s][:], scalar1=st[:, c0 + 1:c0 + 2], scalar2=None,
                                                op0=ALU.mult), reads=[xinb[s], sb_], writes=[xnb])
            for q in range(4):
                bk = q
                for kk in range(4):
                    kc = q * 4 + kk
                    op("pe", lambda e: e.transpose(out=PS[bk][:, kk * 128:(kk + 1) * 128], in_=xn[:, kc * 128:(kc + 1) * 128],
                                                   identity=ident[:]), reads=[xnb, identb], writes=[PSB[bk]], inc=(kk == 3))
                for kk in range(4):
                    kc = q * 4 + kk
                    op("dve", lambda e: e.tensor_scalar(out=hTa[:, kc, t * 128:(t + 1) * 128], in0=PS[bk][:, kk * 128:(kk + 1) * 128],
                                                        scalar1=Acol[:, si * 16 + kc:si * 16 + kc + 1], scalar2=Bcol(si, kc),
                                                        op0=ALU.mult, op1=ALU.add),
                       reads=[PSB[bk], colb], writes=[hTab[kc][t]])

        kb.release(amk)
        chk(1)
        wi = [kb.sb("wi%d" % i, [128, 16, 128], BF16) for i in range(2)]
        wib = bufs(2, "wi")
        wi_n = [0]

        def proj_chunk(c0, ncol=128):
            s = wi_n[0] % 2
            wi_n[0] += 1
            dma("pool", wi[s][:, :, 0:ncol], w_in[:, c0:c0 + ncol].rearrange("(kc p) n -> p kc n", p=128), writes=[wib[s]])
            for nb in range(4):
                for kc in range(16):
                    op("pe", lambda e: e.matmul(PS[nb][0:ncol, :], lhsT=wi[s][:, kc, 0:ncol], rhs=hTa[:, kc, nb * 512:(nb + 1) * 512],
                                                start=(kc == 0), stop=(kc == 15)),
                       reads=[wib[s]] + hTab[kc][nb * 4:(nb + 1) * 4], writes=[PSB[nb]], inc=(kc == 15))

        PSlo = PSALL[:, 0:2048]
        PSloB = PSB[0:4]

        op("dve", lambda e: e.memset(hyss[:], 0.0), writes=[hyssb])
        PI = math.pi
        hm = kb.mark()
        ones_h = kb.sb("ones_h", [128, 128], F32)
        hcb = Buf("hconst")
        op("dve", lambda e: e.memset(ones_h[:], 1.0), writes=[hcb])
        hcw = kb.sb("hcw", [128, 3, 24], F32)
        bmask_h = kb.sb("bmask_h", [128, 7], F32)
        hcol = kb.sb("hcol", [128, 64], F32)
        nrep = kb.sb("nrep", [128, 1], F32)
        with nc.allow_non_contiguous_dma(reason="small vector relayout"):
            dma("sp", hcw[:], hy_conv_w.rearrange("j (c p) -> p j c", p=128), writes=[hcb])
        dma("sp", bmask_h[:], bmask_in[0, :].partition_broadcast(128), writes=[hcb])
        dma("sp", hcol[:], hcol_in[:, :], writes=[hcb])
        dma("sp", nrep[:], nrep_in[:, :], writes=[hcb])
        t7h = kb.sb("t7h", [128, 16], F32)
        t7hb = Buf("t7h")

        invm = kb.mark()
        invn = kb.sb("invn", [128, 2048], F32)
        invb = Buf("invn")
        h1m = kb.mark()
        featsT = kb.sb("featsT", [33, T], F32)
        w1s = kb.sb("w1s", [33, 64], F32)
        w2s = kb.sb("w2s", [64, 64], F32)
        w3s = kb.sb("w3s", [64, 4096], F32)
        bcs = kb.sb("bcs", [64, 2], F32)
        fb = Buf("filt")
        dma("sp", featsT[:], feats_in[:, :], writes=[fb])
        dma("sp", w1s[:], hy_f_w1[:, :], writes=[fb])
        dma("sp", w2s[:], hy_f_w2[:, :], writes=[fb])
        dma("sp", w3s[:], hy_f_w3[:, :], writes=[fb])
        with nc.allow_non_contiguous_dma(reason="small vector relayout"):
            dma("sp", bcs[:, 0:1], hy_f_b1.rearrange("o p -> p o"), writes=[fb])
            dma("sp", bcs[:, 1:2], hy_f_b2.rearrange("o p -> p o"), writes=[fb])
        h1T = kb.sb("h1T", [64, T], F32)
        h2T = kb.sb("h2T", [64, T], F32)
        h1b, h2b = Buf("h1T"), Buf("h2T")
        rw1 = kb.sb("rw1", [64, T], F32)
        rw2 = kb.sb("rw2", [64, T], F32)
        rwb = Buf("rw")
        for (src, srcb, wts, kdim, dst, dstb, bi) in ((featsT, fb, w1s, 33, h1T, h1b, 0), (h1T, h1b, w2s, 64, h2T, h2b, 1)):
            for nb in range(4):
                sl = slice(nb * 512, (nb + 1) * 512)
                op("pe", lambda e: e.matmul(PS[nb][0:64, :], lhsT=wts[0:kdim, :], rhs=src[0:kdim, sl], start=True, stop=True),
                   reads=[srcb, fb], writes=[PSB[nb]])
                op("dve", lambda e: e.tensor_scalar(out=dst[:, sl], in0=PS[nb][0:64, :], scalar1=bcs[:, bi:bi + 1], scalar2=None, op0=ALU.add),
                   reads=[PSB[nb], fb], writes=[dstb])
            op("dve", lambda e: e.tensor_scalar(out=rw1[:], in0=dst[:], scalar1=PI, scalar2=-2 * PI, op0=ALU.is_gt, op1=ALU.mult),
               reads=[dstb], writes=[rwb])
            op("dve", lambda e: e.tensor_scalar(out=rw2[:], in0=dst[:], scalar1=-PI, scalar2=2 * PI, op0=ALU.is_lt, op1=ALU.mult),
               reads=[dstb], writes=[rwb])
            op("dve", lambda e: e.tensor_tensor(out=dst[:], in0=dst[:], in1=rw1[:], op=ALU.add), reads=[dstb, rwb], writes=[dstb])
            op("dve", lambda e: e.tensor_tensor(out=dst[:], in0=dst[:], in1=rw2[:], op=ALU.add), reads=[dstb, rwb], writes=[dstb])
            op("dve", lambda e: e.tensor_scalar(out=dst[:], in0=dst[:], scalar1=-PI, scalar2=PI, op0=ALU.max, op1=ALU.min),
               reads=[dstb], writes=[dstb])
            op("act", lambda e: e.activation(out=dst[:], in_=dst[:], func=AF.Sin), reads=[dstb], writes=[dstb])
        absdec = kb.sb("absdec", [128, 2048], F32)
        adb = Buf("absdec")
        dma("sp", absdec[:], hy_decay.rearrange("o c -> (o c)").partition_broadcast(128), writes=[adb])
        op("act", lambda e: e.activation(out=absdec[:], in_=absdec[:], func=AF.Abs), reads=[adb], writes=[adb])
        absacc = kb.sb("absacc", [128, 2048], F32)
        aab = bufs(4, "absacc")
        op("dve", lambda e: e.memset(absacc[:], 0.0), writes=aab)
        wn = kb.sb("wn", [128, 2048], F32)
        wnb = Buf("wn")
        tA = [kb.sb("tA%d" % i, [128, 512], F32) for i in range(2)]
        tB = [kb.sb("tB%d" % i, [128, 512], F32) for i in range(2)]
        tAb, tBb = bufs(2, "tA"), bufs(2, "tB")
        tS = [kb.sb("tS%d" % i, [128, 2048], BF16) for i in range(2)]
        tD = [kb.sb("tD%d" % i, [128, 2048], BF16) for i in range(2)]
        tSb, tDb = bufs(2, "tS"), bufs(2, "tD")
        for lt in range(16):
            op("dve", lambda e: e.tensor_scalar(out=wn[:], in0=absdec[:], scalar1=hcol[:, lt:lt + 1], scalar2=None, op0=ALU.mult),
               reads=[adb, hcb], writes=[wnb])
            op("act", lambda e: e.activation(out=wn[:], in_=wn[:], func=AF.Exp), reads=[wnb], writes=[wnb])
            op("dve", lambda e: e.tensor_scalar(out=wn[:], in0=wn[:], scalar1=0.05, scalar2=None, op0=ALU.add), reads=[wnb], writes=[wnb])
            s2 = lt % 2
            for cb4 in range(4):
                sl = slice(cb4 * 512, (cb4 + 1) * 512)
                k_ = cb4 % 2
                pa, pb = 4 + 2 * k_, 5 + 2 * k_
                op("pe", lambda e: e.matmul(PS[pa][:, :], lhsT=h2T[:, lt * 128:(lt + 1) * 128], rhs=w3s[:, cb4 * 512:(cb4 + 1) * 512],
                                            start=True, stop=True), reads=[h2b, fb], writes=[PSB[pa]])
                op("pe", lambda e: e.matmul(PS[pb][:, :], lhsT=h2T[:, lt * 128:(lt + 1) * 128], rhs=w3s[:, 2048 + cb4 * 512:2048 + (cb4 + 1) * 512],
                                            start=True, stop=True), reads=[h2b, fb], writes=[PSB[pb]])
                op("dve", lambda e: e.tensor_tensor(out=tA[k_][:], in0=PS[pa][:, :], in1=wn[:, sl], op=ALU.mult),
                   reads=[PSB[pa], wnb], writes=[tAb[k_]])
                op("dve", lambda e: e.scalar_tensor_tensor(out=tB[k_][:], in0=PS[pb][:, :], scalar=hcol[:, 16 + lt:17 + lt], in1=wn[:, sl],
                                                           op0=ALU.mult, op1=ALU.mult), reads=[PSB[pb], wnb, hcb], writes=[tBb[k_]])
                op("pool", lambda e: e.tensor_tensor(out=tS[s2][:, sl], in0=tA[k_][:], in1=tB[k_][:], op=ALU.add),
                   reads=[tAb[k_], tBb[k_]], writes=[tSb[s2]])
                op("pool", lambda e: e.tensor_tensor(out=tD[s2][:, sl], in0=tA[k_][:], in1=tB[k_][:], op=ALU.subtract),
                   reads=[tAb[k_], tBb[k_]], writes=[tDb[s2]])
                op("act", lambda e: e.activation(out=tA[k_][:], in_=tA[k_][:], func=AF.Abs), reads=[tAb[k_]], writes=[tAb[k_]])
                op("act", lambda e: e.activation(out=tB[k_][:], in_=tB[k_][:], func=AF.Abs), reads=[tBb[k_]], writes=[tBb[k_]])
                op("pool", lambda e: e.tensor_tensor(out=absacc[:, sl], in0=absacc[:, sl], in1=tA[k_][:], op=ALU.add),
                   reads=[tAb[k_], aab[cb4]], writes=[aab[cb4]])
                op("pool", lambda e: e.tensor_tensor(out=absacc[:, sl], in0=absacc[:, sl], in1=tB[k_][:], op=ALU.add),
                   reads=[tBb[k_], aab[cb4]], writes=[aab[cb4]])
            dma("sp", tapsS[lt * 128:(lt + 1) * 128, :], tS[s2][:], reads=[tSb[s2]], writes=[tapsb])
            dma("sp", tapsD[lt * 128:(lt + 1) * 128, :], tD[s2][:], reads=[tDb[s2]], writes=[tapsb])
        for cb4 in range(4):
            sl = slice(cb4 * 512, (cb4 + 1) * 512)
            op("pe", lambda e: e.matmul(PS[cb4][:, :], lhsT=ones_h[:], rhs=absacc[:, sl], start=True, stop=True),
               reads=[aab[cb4], hcb], writes=[PSB[cb4]])
            op("dve", lambda e: e.reciprocal(out=invn[:, sl], in_=PS[cb4][:, :]), reads=[PSB[cb4]], writes=[invb])
        op("dve", lambda e: e.tensor_scalar(out=invn[:], in0=invn[:], scalar1=nrep[:, 0:1], scalar2=None, op0=ALU.mult),
           reads=[invb, hcb], writes=[invb])
        kb.release(h1m)
        h1m = kb.mark()
        tSk = kb.sb("tSk", [128, 16, 512], BF16)
        tDk = kb.sb("tDk", [128, 16, 512], BF16)
        tkb = Buf("tk")
        dfa = [kb.sb("dfa%d" % i, [128, 16, 128], BF16) for i in range(2)]
        dfb = [kb.sb("dfb%d" % i, [128, 16, 128], BF16) for i in range(2)]
        dfab, dfbb = bufs(2, "dfa"), bufs(2, "dfb")
        sp_t = [kb.sb("sp_t%d" % i, [128, 512], F32) for i in range(8)]
        sp_b = bufs(8, "sp_t")
        for cb4 in range(4):
            sl = slice(cb4 * 512, (cb4 + 1) * 512)
            dma("sp", tSk[:], tapsS[:, sl].rearrange("(lt p) n -> p lt n", p=128), reads=[tapsb], writes=[tkb])
            dma("sp", tDk[:], tapsD[:, sl].rearrange("(lt p) n -> p lt n", p=128), reads=[tapsb], writes=[tkb])
            for ft in range(16):
                s2 = ft % 2
                dma("sp", dfa[s2][:], CF_t[ft], writes=[dfab[s2]])
                dma("sp", dfb[s2][:], SF_t[ft], writes=[dfbb[s2]])
                b0 = 4 * s2
                for lt in range(16):
                    op("pe", lambda e: e.matmul(PS[b0][:, :], lhsT=dfa[s2][:, lt, :], rhs=tSk[:, lt, :], start=(lt == 0), stop=(lt == 15)),
                       reads=[dfab[s2], tkb], writes=[PSB[b0]], inc=(lt == 15))
                for lt in range(16):
                    op("pe", lambda e: e.matmul(PS[b0 + 1][:, :], lhsT=dfb[s2][:, lt, :], rhs=tDk[:, lt, :], start=(lt == 0), stop=(lt == 15)),
                       reads=[dfbb[s2], tkb], writes=[PSB[b0 + 1]], inc=(lt == 15))
                if ft % 2 == 0:
                    for lt in range(16):
                        op("pe", lambda e: e.matmul(PS[b0 + 2][:, :], lhsT=dfb[s2][:, lt, :], rhs=tSk[:, lt, :], start=(lt == 0), stop=(lt == 15)),
                           reads=[dfbb[s2], tkb], writes=[PSB[b0 + 2]], inc=(lt == 15))
                o4 = 4 * s2
                sre, sim, nq, srb = sp_t[o4], sp_t[o4 + 1], sp_t[o4 + 2], sp_t[o4 + 3]
                op("dve", lambda e: e.tensor_tensor(out=sre[:], in0=PS[b0][:, :], in1=invn[:, sl], op=ALU.mult),
                   reads=[PSB[b0], invb], writes=[sp_b[o4]])
                op("dve", lambda e: e.scalar_tensor_tensor(out=sim[:], in0=PS[b0 + 1][:, :], scalar=hcol[:, 48 + ft:49 + ft], in1=invn[:, sl],
                                                           op0=ALU.mult, op1=ALU.mult), reads=[PSB[b0 + 1], invb, hcb], writes=[sp_b[o4 + 1]])
                dma("sp", spec_re[ft * 128:(ft + 1) * 128, sl], sre[:], reads=[sp_b[o4]], writes=[specb])
                dma("sp", spec_im[ft * 128:(ft + 1) * 128, sl], sim[:], reads=[sp_b[o4 + 1]], writes=[specb])
                if ft % 2 == 0:
                    op("dve", lambda e: e.tensor_tensor(out=nq[:], in0=PS[b0 + 2][:, :], in1=invn[:, sl], op=ALU.mult),
                       reads=[PSB[b0 + 2], invb], writes=[sp_b[o4 + 2]])
                    op("pool", lambda e: e.tensor_tensor(out=nq[:], in0=nq[:], in1=sre[:], op=ALU.subtract),
                       reads=[sp_b[o4 + 2], sp_b[o4]], writes=[sp_b[o4 + 2]])
                    op("dve", lambda e: e.scalar_tensor_tensor(out=srb[:], in0=nq[:], scalar=hcol[:, 32 + ft:33 + ft], in1=sre[:],
                                                               op0=ALU.mult, op1=ALU.add), reads=[sp_b[o4 + 2], sp_b[o4], hcb], writes=[sp_b[o4 + 3]])
                    dma("sp", spec_rb[ft * 128:(ft + 1) * 128, sl], srb[:], reads=[sp_b[o4 + 3]], writes=[specb])
                else:
                    dma("sp", spec_rb[ft * 128:(ft + 1) * 128, sl], sre[:], reads=[sp_b[o4]], writes=[specb])
        kb.release(h1m)
        kb.release(invm)
        chk(20)

        CW = 256
        u32 = kb.sb("u32", [128, 16, CW], F32)
        ubf = kb.sb("ubf", [128, 16, CW], BF16)
        x1t = kb.sb("x1t", [128, 16, CW], F32)
        x2t = kb.sb("x2t", [128, 16, CW], F32)
        u32b, ubfb, x1tb, x2tb = Buf("u32"), Buf("ubf"), Buf("x1t"), Buf("x2t")
        sp3 = [kb.sb("sp3_%d" % i, [128, 3, CW], F32) for i in range(2)]
        sp3b = bufs(2, "sp3")
        Yre = kb.sb("Yre", [128, 16, CW], BF16)
        Yim = kb.sb("Yim", [128, 16, CW], BF16)
        Yb = Buf("Yf")
        dfa = [kb.sb("dga%d" % i, [128, 16, 128], BF16) for i in range(3)]
        dfb = [kb.sb("dgb%d" % i, [128, 16, 128], BF16) for i in range(3)]
        dfab, dfbb = bufs(3, "dga"), bufs(3, "dgb")
        tq = [kb.sb("tq%d" % i, [128, CW], F32) for i in range(8)]
        tqb = bufs(8, "tq")
        dbt = kb.sb("dbt", [128, CW], F32)
        hnt = kb.sb("hnt", [128, CW], F32)
        dbb = Buf("dbt")
        ztok = kb.sb("ztok", [128, 16, CW], F32)
        ztb_ = Buf("ztok")
        fmt = ztok[:, 0:8, :].rearrange("p a b -> p (a b)")
        fmtb = ztb_
        yoh = kb.sb("yoh", [128, T], BF16)
        yohb = Buf("yoh")
        ssq = kb.sb("ssq", [128, 2], F32)
        ssqb = Buf("ssq")
        jk = kb.sb("jk", [128, CW], BF16)
        jkb = Buf("jk")

        def conv3(dst, dstb, ci):
            w0, w1, w2 = (hcw[:, j, ci:ci + 1] for j in range(3))
            op("dve", lambda e: e.tensor_scalar(out=dst[:], in0=PSlo, scalar1=w1, scalar2=None, op0=ALU.mult),
               reads=PSloB + [hcb], writes=[dstb])
            op("dve", lambda e: e.scalar_tensor_tensor(out=dst[:, 1:T], in0=PSALL[:, 0:T - 1], scalar=w0, in1=dst[:, 1:T],
                                                       op0=ALU.mult, op1=ALU.add), reads=PSloB + [hcb], writes=[dstb])
            op("dve", lambda e: e.scalar_tensor_tensor(out=dst[:, 0:T - 1], in0=PSALL[:, 1:T], scalar=w2, in1=dst[:, 0:T - 1],
                                                       op0=ALU.mult, op1=ALU.add), reads=PSloB + [hcb], writes=[dstb])
            d3 = dst.rearrange("p (s t) -> p s t", t=256)
            p3 = PSlo.rearrange("p (s t) -> p s t", t=256)
            op("dve", lambda e: e.scalar_tensor_tensor(out=t7h[:, 0:7], in0=p3[:, 0:7, 255], scalar=w0, in1=bmask_h[:], op0=ALU.mult,
                                                       op1=ALU.mult), reads=PSloB + [hcb], writes=[t7hb])
            op("dve", lambda e: e.tensor_tensor(out=d3[:, 1:8, 0], in0=d3[:, 1:8, 0], in1=t7h[:, 0:7], op=ALU.subtract),
               reads=[t7hb], writes=[dstb])
            op("dve", lambda e: e.scalar_tensor_tensor(out=t7h[:, 8:15], in0=p3[:, 1:8, 0], scalar=w2, in1=bmask_h[:], op0=ALU.mult,
                                                       op1=ALU.mult), reads=PSloB + [hcb], writes=[t7hb])
            op("dve", lambda e: e.tensor_tensor(out=d3[:, 0:7, 255], in0=d3[:, 0:7, 255], in1=t7h[:, 8:15], op=ALU.subtract),
               reads=[t7hb], writes=[dstb])

        for cgp in range(0 if not SKIP_H2 else 4, 4):
            for g2 in range(2):
                cg = 2 * cgp + g2
                gs = slice(g2 * 128, (g2 + 1) * 128)
                for (which, dst, dstb) in ((0, u32, u32b), (1, x1t, x1tb), (2, x2t, x2tb)):
                    proj_chunk(which * 1024 + cg * 128)
                    conv3(fmt, fmtb, which * 8 + cg)
                    for q in range(4):
                        pbk = 4 + q
                        for kk in range(4):
                            tt = q * 4 + kk
                            op("pe", lambda e: e.transpose(out=PS[pbk][:, kk * 128:(kk + 1) * 128], in_=fmt[:, tt * 128:(tt + 1) * 128],
                                                           identity=ident[:]), reads=[fmtb, identb], writes=[PSB[pbk]], inc=(kk == 3))
                        src3 = PS[pbk].rearrange("p (a b) -> p a b", b=128)
                        op("act", lambda e: e.activation(out=dst[:, q * 4:(q + 1) * 4, gs], in_=src3, func=AF.Copy),
                           reads=[PSB[pbk]], writes=[dstb])
                        if which == 0:
                            op("act", lambda e: e.activation(out=ubf[:, q * 4:(q + 1) * 4, gs], in_=src3, func=AF.Copy),
                               reads=[PSB[pbk]], writes=[ubfb])
            csl0 = slice(cgp * CW, (cgp + 1) * CW)
            dma("sp", hnt[:], hy_out_norm[0, csl0].partition_broadcast(128), writes=[dbb])
            for o in range(2):
                csl = slice(o * 1024 + cgp * CW, o * 1024 + (cgp + 1) * CW)
                dma("sp", dbt[:], hy_bias[o, csl0].partition_broadcast(128), writes=[dbb])
                xg, xgb = (x1t, x1tb) if o == 0 else (x2t, x2tb)
                for ft in range(16):
                    s2 = ft % 2
                    s3 = ft % 3
                    dma("sp", dfa[s3][:], CF_t[ft], writes=[dfab[s3]])
                    dma("sp", dfb[s3][:], SF_t[ft], writes=[dfbb[s3]])
                    fsl = slice(ft * 128, (ft + 1) * 128)
                    dma("sp", sp3[s2][:, 0, :], spec_re[fsl, csl], reads=[specb], writes=[sp3b[s2]])
                    dma("sp", sp3[s2][:, 1, :], spec_im[fsl, csl], reads=[specb], writes=[sp3b[s2]])
                    dma("sp", sp3[s2][:, 2, :], spec_rb[fsl, csl], reads=[specb], writes=[sp3b[s2]])
                    pk = s2
                    for lt in range(16):
                        op("pe", lambda e: e.matmul(PS[pk][:, 0:CW], lhsT=dfa[s3][:, lt, :], rhs=ubf[:, lt, :], start=(lt == 0), stop=(lt == 15)),
                           reads=[dfab[s3], ubfb], writes=[PSB[pk]], inc=(lt == 15))
                    for lt in range(16):
                        op("pe", lambda e: e.matmul(PS[2 + pk][:, 0:CW], lhsT=dfb[s3][:, lt, :], rhs=ubf[:, lt, :], start=(lt == 0), stop=(lt == 15)),
                           reads=[dfbb[s3], ubfb], writes=[PSB[2 + pk]], inc=(lt == 15))
                    q4 = 4 * s2
                    ure, uim = PS[pk][:, 0:CW], PS[2 + pk][:, 0:CW]
                    S_re, S_im, S_rb = sp3[s2][:, 0, :], sp3[s2][:, 1, :], sp3[s2][:, 2, :]
                    op("dve", lambda e: e.tensor_tensor(out=tq[q4][:], in0=ure, in1=S_re, op=ALU.mult), reads=[PSB[pk], sp3b[s2]], writes=[tqb[q4]])
                    op("dve", lambda e: e.tensor_tensor(out=tq[q4 + 1][:], in0=uim, in1=S_im, op=ALU.mult), reads=[PSB[2 + pk], sp3b[s2]], writes=[tqb[q4 + 1]])
                    op("dve", lambda e: e.tensor_tensor(out=tq[q4 + 2][:], in0=ure, in1=S_im, op=ALU.mult), reads=[PSB[pk], sp3b[s2]], writes=[tqb[q4 + 2]])
                    op("dve", lambda e: e.tensor_tensor(out=tq[q4 + 3][:], in0=uim, in1=S_rb, op=ALU.mult), reads=[PSB[2 + pk], sp3b[s2]], writes=[tqb[q4 + 3]])
                    op("pool", lambda e: e.tensor_tensor(out=Yre[:, ft, :], in0=tq[q4][:], in1=tq[q4 + 1][:], op=ALU.subtract),
                       reads=[tqb[q4], tqb[q4 + 1]], writes=[Yb])
                    op("pool", lambda e: e.tensor_tensor(out=Yim[:, ft, :], in0=tq[q4 + 2][:], in1=tq[q4 + 3][:], op=ALU.add),
                       reads=[tqb[q4 + 2], tqb[q4 + 3]], writes=[Yb])
                for tt in range(16):
                    s2 = tt % 2
                    s3 = (tt + 1) % 3
                    dma("sp", dfa[s3][:], GC_t[tt], writes=[dfab[s3]])
                    dma("sp", dfb[s3][:], GS_t[tt], writes=[dfbb[s3]])
                    pk = 4 + s2
                    for ft in range(16):
                        op("pe", lambda e: e.matmul(PS[pk][:, 0:CW], lhsT=dfa[s3][:, ft, :], rhs=Yre[:, ft, :], start=(ft == 0), stop=False),
                           reads=[dfab[s3], Yb], writes=[PSB[pk]], inc=False)
                        op("pe", lambda e: e.matmul(PS[pk][:, 0:CW], lhsT=dfb[s3][:, ft, :], rhs=Yim[:, ft, :], start=False, stop=(ft == 15)),
                           reads=[dfbb[s3], Yb], writes=[PSB[pk]], inc=(ft == 15))
                    q4 = 4 * s2
                    op("dve", lambda e: e.tensor_tensor(out=tq[q4][:], in0=u32[:, tt, :], in1=dbt[:], op=ALU.mult), reads=[u32b, dbb], writes=[tqb[q4]])
                    op("dve", lambda e: e.tensor_tensor(out=tq[q4][:], in0=tq[q4][:], in1=PS[pk][:, 0:CW], op=ALU.add), reads=[PSB[pk], tqb[q4]], writes=[tqb[q4]])
                    if o == 0:
                        op("dve", lambda e: e.tensor_tensor(out=u32[:, tt, :], in0=tq[q4][:], in1=xg[:, tt, :], op=ALU.mult),
                           reads=[tqb[q4], xgb, u32b], writes=[u32b])
                        op("pool", lambda e: e.tensor_copy(out=ubf[:, tt, :], in_=u32[:, tt, :]), reads=[u32b], writes=[ubfb])
                    else:
                        op("dve", lambda e: e.tensor_tensor(out=tq[q4 + 1][:], in0=tq[q4][:], in1=xg[:, tt, :], op=ALU.mult),
                           reads=[tqb[q4], xgb], writes=[tqb[q4 + 1]])
                        op("act", lambda e: e.activation(out=jk[:], in_=tq[q4 + 1][:], func=AF.Square, accum_out=ssq[:, 0:1]),
                           reads=[tqb[q4 + 1]], writes=[jkb, ssqb])
                        op("dve", lambda e: e.tensor_tensor(out=hyss[:, tt:tt + 1], in0=hyss[:, tt:tt + 1], in1=ssq[:, 0:1], op=ALU.add),
                           reads=[ssqb, hyssb], writes=[hyssb])
                        op("pool", lambda e: e.tensor_tensor(out=ztok[:, tt, :], in0=tq[q4 + 1][:], in1=hnt[:], op=ALU.mult),
                           reads=[tqb[q4 + 1], dbb], writes=[ztb_])
            for g2 in range(2):
                gs = slice(g2 * 128, (g2 + 1) * 128)
                for q in range(4):
                    pbk = q
                    for kk in range(4):
                        tt = q * 4 + kk
                        op("pe", lambda e: e.transpose(out=PS[pbk][:, kk * 128:(kk + 1) * 128], in_=ztok[:, tt, gs], identity=ident[:]),
                           reads=[ztb_, identb], writes=[PSB[pbk]], inc=(kk == 3))
                    op("act", lambda e: e.activation(out=yoh[:, q * 512:(q + 1) * 512], in_=PS[pbk][:, :], func=AF.Copy), reads=[PSB[pbk]], writes=[yohb])
                dma("sp", ycatT[2 * cgp + g2], yoh[:], reads=[yohb], writes=[ycb[2 * cgp + g2]])
        kb.release(hm)
        chk(2)
        gm = kb.mark()
        gmask = kb.sb("gmask", [128, 7, 128], F32)
        ones = kb.sb("ones", [128, 128], F32)
        cb = Buf("gconst")
        dma("sp", gmask[:], gmask_in.rearrange("m p f -> p m f"), writes=[cb])
        op("dve", lambda e: e.memset(ones[:], 1.0), writes=[cb])
        keepc = kb.sb("keepc", [128, 8], F32)
        bmask = kb.sb("bmask", [128, 7], F32)
        dma("sp", keepc[:], keep_in[0, :].partition_broadcast(128), writes=[cb])
        dma("sp", bmask[:], bmask_in[0, :].partition_broadcast(128), writes=[cb])
        gcw = kb.sb("gcw", [128, 3, 24], F32)
        hcw = kb.sb("hcw", [128, 3, 24], F32)
        with nc.allow_non_contiguous_dma(reason="small vector relayout"):
            dma("sp", gcw[:], gdn_conv_w.rearrange("j (c p) -> p j c", p=128), writes=[cb])
            dma("sp", hcw[:], hy_conv_w.rearrange("j (c p) -> p j c", p=128), writes=[cb])
        onorm = kb.sb("onorm", [128, 1], F32)
        with nc.allow_non_contiguous_dma(reason="small vector relayout"):
            dma("sp", onorm[:], gdn_o_norm.rearrange("o p -> p o"), writes=[cb])
        chk(3)
        wab = kb.sb("wab", [128, 16, 32], BF16)
        wabb = Buf("wab")
        dma("pool", wab[:], w_in[:, 7168:7200].rearrange("(kc p) n -> p kc n", p=128), writes=[wabb])
        for tt in range(NT):
            for kc in range(16):
                op("pe", lambda e: e.matmul(PS[4][:, tt * 32:(tt + 1) * 32], lhsT=hTa[:, kc, tt * 128:(tt + 1) * 128], rhs=wab[:, kc, :],
                                            start=(kc == 0), stop=(kc == 15)),
                   reads=[wabb, hTab[kc][tt]], writes=[PSB[4]], inc=(kc == 15))
        chk(4)
        W = 256
        gv = kb.sb("gv", [128, 12, W], F32)
        gvb = Buf("gv")
        GX, GG, BETA, NEA, GCUM, TOT, NBETA, BG, KD, DTB, ALOG = range(11)
        ab4 = PS[4].rearrange("p (t d k h) -> p t d k h", t=16, d=2, k=2, h=8)

        def v4(i):
            return gv[:, i, :].rearrange("p (t d h) -> p t d h", t=16, d=2, h=8)
        dma("sp", gv[:, DTB, :], dtb_in[0, :].partition_broadcast(128), writes=[gvb])
        dma("sp", gv[:, ALOG, :], alog_in[0, :].partition_broadcast(128), writes=[gvb])
        op("dve", lambda e: e.tensor_tensor(out=v4(GX), in0=ab4[:, :, :, 0, :], in1=v4(DTB), op=ALU.add), reads=[PSB[4], gvb], writes=[gvb])
        op("act", lambda e: e.activation(out=gv[:, GX, :], in_=gv[:, GX, :], func=AF.Exp), reads=[gvb], writes=[gvb])
        op("act", lambda e: e.activation(out=gv[:, GX, :], in_=gv[:, GX, :], func=AF.Ln, bias=1.0), reads=[gvb], writes=[gvb])
        op("act", lambda e: e.activation(out=gv[:, NEA, :], in_=gv[:, ALOG, :], func=AF.Exp), reads=[gvb], writes=[gvb])
        op("dve", lambda e: e.scalar_tensor_tensor(out=gv[:, GG, :], in0=gv[:, GX, :], scalar=-1.0, in1=gv[:, NEA, :],
                                                   op0=ALU.mult, op1=ALU.mult), reads=[gvb], writes=[gvb])
        op("act", lambda e: e.activation(out=v4(BETA), in_=ab4[:, :, :, 1, :], func=AF.Sigmoid), reads=[PSB[4]], writes=[gvb])
        chk(5)
        for tt in range(NT):
            for d in range(2):
                cs_ = (tt * 2 + d) * 8
                op("pe", lambda e: e.matmul(PS[5][:, cs_:cs_ + 8], lhsT=gmask[:, 1 if d == 0 else 3, :], rhs=gv[:, GG, cs_:cs_ + 8],
                                            start=True, stop=True), reads=[gvb, cb], writes=[PSB[5]], inc=(tt == NT - 1 and d == 1))
        op("pe", lambda e: e.matmul(PS[6][:, 0:W], lhsT=ones[:], rhs=gv[:, GG, :], start=True, stop=True), reads=[gvb, cb], writes=[PSB[6]])
        op("dve", lambda e: e.tensor_copy(out=gv[:, GCUM, :], in_=PS[5][:, 0:W]), reads=[PSB[5]], writes=[gvb])
        op("dve", lambda e: e.tensor_copy(out=gv[:, TOT, :], in_=PS[6][:, 0:W]), reads=[PSB[6]], writes=[gvb])
        op("dve", lambda e: e.tensor_scalar(out=gv[:, NBETA, :], in0=gv[:, BETA, :], scalar1=-1.0, scalar2=None, op0=ALU.mult),
           reads=[gvb], writes=[gvb])
        op("act", lambda e: e.activation(out=gv[:, BG, :], in_=gv[:, GCUM, :], func=AF.Exp), reads=[gvb], writes=[gvb])
        op("dve", lambda e: e.tensor_tensor(out=gv[:, BG, :], in0=gv[:, BG, :], in1=gv[:, BETA, :], op=ALU.mult), reads=[gvb], writes=[gvb])
        op("dve", lambda e: e.tensor_tensor(out=gv[:, KD, :], in0=gv[:, TOT, :], in1=gv[:, GCUM, :], op=ALU.subtract), reads=[gvb], writes=[gvb])
        op("act", lambda e: e.activation(out=gv[:, KD, :], in_=gv[:, KD, :], func=AF.Exp), reads=[gvb], writes=[gvb])

        chk(6)
        if DEBUG:
            dma("sp", dbg_gv[:, :], gv[:].rearrange("p a b -> p (a b)"), reads=[gvb], writes=[dbgb])

        def col(i, tt, d, h):
            c = (tt * 2 + d) * 8 + h
            return gv[:, i, c:c + 1]

        FM = [kb.sb("fm%d" % i, [128, T], F32) for i in range(6)]
        FMB = bufs(6, "fm")
        QT, KT, ZT, OA, SQ, VT = range(6)
        ktok = kb.sb("ktok", [128, 16, 128], F32)
        vtok = kb.sb("vtok", [128, 16, 128], F32)
        ktokb, vtokb = Buf("ktok"), Buf("vtok")
        tl2 = [kb.sb("tl%d" % i, [128, 38, 128], F32) for i in range(2)]
        tlb2 = [bufs(38, "tl%d_" % i) for i in range(2)]
        OAd = [FM[OA], kb.sb("oab", [128, T], F32)]
        OAdb = [Buf("oaf"), Buf("oab")]
        psl = {b_: bufs(4, "psl%d_" % b_) for b_ in range(8)}
        (DG, ND, MM, EG, ML, MU, NA, NTA, NB_, NTB, RTA, RTB, VB, KBG, UU, WT, ATT, QD, KDE, VN, SS, S2, T1, T2, T3, T4,
         NBD, NTBD, E1, E1T, E2, DD, PP, XX, DD2, DT2, RTF, T5) = range(38)
        yo = kb.sb("yo", [128, T], BF16)
        yob = Buf("yo")
        t7 = kb.sb("t7", [128, 16], F32)
        t7b = Buf("t7")

        def conv_silu(dst, dstb, wtile, ci, do_conv=True, do_silu=True):
            if not do_conv:
                op("act", lambda e: e.activation(out=dst[:], in_=PSlo, func=AF.Silu), reads=PSloB, writes=[dstb])
                return
            w0, w1, w2 = (wtile[:, j, ci:ci + 1] for j in range(3))
            op("dve", lambda e: e.tensor_scalar(out=dst[:], in0=PSlo, scalar1=w1, scalar2=None, op0=ALU.mult),
               reads=PSloB + [cb], writes=[dstb])
            op("dve", lambda e: e.scalar_tensor_tensor(out=dst[:, 1:T], in0=PSALL[:, 0:T - 1], scalar=w0, in1=dst[:, 1:T],
                                                       op0=ALU.mult, op1=ALU.add), reads=PSloB + [cb], writes=[dstb])
            op("dve", lambda e: e.scalar_tensor_tensor(out=dst[:, 0:T - 1], in0=PSALL[:, 1:T], scalar=w2, in1=dst[:, 0:T - 1],
                                                       op0=ALU.mult, op1=ALU.add), reads=PSloB + [cb], writes=[dstb])
            d3 = dst.rearrange("p (s t) -> p s t", t=256)
            p3 = PSlo.rearrange("p (s t) -> p s t", t=256)
            op("dve", lambda e: e.scalar_tensor_tensor(out=t7[:, 0:7], in0=p3[:, 0:7, 255], scalar=w0, in1=bmask[:], op0=ALU.mult,
                                                       op1=ALU.mult), reads=PSloB + [cb], writes=[t7b])
            op("dve", lambda e: e.tensor_tensor(out=d3[:, 1:8, 0], in0=d3[:, 1:8, 0], in1=t7[:, 0:7], op=ALU.subtract),
               reads=[t7b], writes=[dstb])
            op("dve", lambda e: e.scalar_tensor_tensor(out=t7[:, 8:15], in0=p3[:, 1:8, 0], scalar=w2, in1=bmask[:], op0=ALU.mult,
                                                       op1=ALU.mult), reads=PSloB + [cb], writes=[t7b])
            op("dve", lambda e: e.tensor_tensor(out=d3[:, 0:7, 255], in0=d3[:, 0:7, 255], in1=t7[:, 8:15], op=ALU.subtract),
               reads=[t7b], writes=[dstb])
            if do_silu:
                op("act", lambda e: e.activation(out=dst[:], in_=dst[:], func=AF.Silu), reads=[dstb], writes=[dstb])

        def partnorm(src, srcb, scale_const):
            op("dve", lambda e: e.tensor_tensor(out=FM[SQ][:], in0=src[:], in1=src[:], op=ALU.mult), reads=[srcb], writes=[FMB[SQ]])
            for nb in range(4):
                op("pe", lambda e: e.matmul(PS[4 + nb][:, :], lhsT=ones[:], rhs=FM[SQ][:, nb * 512:(nb + 1) * 512], start=True, stop=True),
                   reads=[FMB[SQ], cb], writes=[PSB[4 + nb]])
            return

        def partnorm_apply(src, srcb, nscale, eps, extra=None):
            for nb in range(4):
                sl = slice(nb * 512, (nb + 1) * 512)
                op("dve", lambda e: e.tensor_scalar(out=FM[SQ][:, sl], in0=PS[4 + nb][:, :], scalar1=nscale, scalar2=eps,
                                                    op0=ALU.mult, op1=ALU.add), reads=[PSB[4 + nb]], writes=[FMB[SQ]])
            op("act", lambda e: e.activation(out=FM[SQ][:], in_=FM[SQ][:], func=AF.Ln), reads=[FMB[SQ]], writes=[FMB[SQ]])
            op("act", lambda e: e.activation(out=FM[SQ][:], in_=FM[SQ][:], func=AF.Exp, scale=-0.5), reads=[FMB[SQ]], writes=[FMB[SQ]])
            op("dve", lambda e: e.tensor_tensor(out=src[:], in0=src[:], in1=FM[SQ][:], op=ALU.mult), reads=[FMB[SQ], srcb], writes=[srcb])

        def T_(i):
            return tl[:, i, :]

        for hd in range(8):
            for (ci, dsti, conv) in ((hd, QT, True), (8 + hd, KT, True), (16 + hd, VT, True), (None, ZT, False)):
                c0 = 3072 + ci * 128 if ci is not None else 6144 + hd * 128
                proj_chunk(c0)
                conv_silu(FM[dsti], FMB[dsti], gcw, ci if ci is not None else 0, do_conv=conv)
            chk(7)
            for (dsti, scl) in ((QT, 128.0 ** -0.5), (KT, 1.0)):
                partnorm(FM[dsti], FMB[dsti], None)
                partnorm_apply(FM[dsti], FMB[dsti], 1.0, EPS)
                if scl != 1.0:
                    op("dve", lambda e: e.tensor_scalar(out=FM[dsti][:], in0=FM[dsti][:], scalar1=scl, scalar2=None, op0=ALU.mult),
                       reads=[FMB[dsti]], writes=[FMB[dsti]])
            chk(8)
            for (srci, dst, dstb) in ((KT, ktok, ktokb), (VT, vtok, vtokb)):
                for q in range(4):
                    for kk in range(4):
                        tt = q * 4 + kk
                        op("pe", lambda e: e.transpose(out=PS[q][:, kk * 128:(kk + 1) * 128], in_=FM[srci][:, tt * 128:(tt + 1) * 128],
                                                       identity=ident[:]), reads=[FMB[srci], identb], writes=[PSB[q]], inc=(kk == 3))
                    op("dve", lambda e: e.tensor_copy(out=dst[:, q * 4:(q + 1) * 4, :].rearrange("p a b -> p (a b)"), in_=PS[q][:, :]),
                       reads=[PSB[q]], writes=[dstb])
            chk(9)
            def mm_(e, out, lhsT, rhs, **kw):
                if GDN_F32R:
                    lhsT, rhs = lhsT.bitcast(mybir.dt.float32r), rhs.bitcast(mybir.dt.float32r)
                return e.matmul(out, lhsT=lhsT, rhs=rhs, **kw)

            def gdn_dir(d):
                tl_, tlb = tl2[d], tlb2[d]
                T_ = lambda i: tl_[:, i, :]
                bx, by = 4 + 2 * d, 5 + 2 * d
                keys_ = ((4, 0), (5, 0), (5, 1), (6, 0), (6, 1), (7, 0), (7, 1))
                SL = {k_: ((k_[0] if d == 0 else k_[0] - 4), k_[1]) for k_ in keys_}
                def P_(ob, oslot):
                    b_, s_ = SL[(ob, oslot)]
                    return PS[b_][:, s_ * 128:(s_ + 1) * 128]
                def PB_(ob, oslot):
                    b_, s_ = SL[(ob, oslot)]
                    return psl[b_][s_]
                order = list(range(NT)) if d == 0 else list(range(NT - 1, -1, -1))
                mA, mB = (0, 1) if d == 0 else (2, 3)
                dma("sp", T_(SS), s0_in[d, hd], writes=[tlb[SS]])
                for ci_, c in enumerate(order):
                    tsl = slice(c * 128, (c + 1) * 128)
                    seg = c // 2
                    seg_start = (c % 2 == 0) if d == 0 else (c % 2 == 1)
                    seg_end = not seg_start
                    if seg_start and ci_ > 0:
                        kcol = keepc[:, seg:seg + 1] if d == 0 else keepc[:, seg + 1:seg + 2]
                        yield op("dve", lambda e: e.tensor_scalar(out=T_(SS), in0=T_(SS), scalar1=kcol, scalar2=None, op0=ALU.mult),
                           reads=[tlb[SS], cb], writes=[tlb[SS]])
                    gc = col(GCUM, c, d, hd)
                    yield op("dve", lambda e: e.tensor_scalar(out=T_(DG), in0=ident[:], scalar1=gc, scalar2=None, op0=ALU.mult),
                       reads=[identb, gvb], writes=[tlb[DG]])
                    yield op("pe", lambda e: mm_(e, P_(4, 0), lhsT=ones[:], rhs=T_(DG), start=True, stop=True),
                       reads=[tlb[DG], cb], writes=[PB_(4, 0)])
                    yield op("dve", lambda e: e.tensor_scalar(out=T_(ND), in0=P_(4, 0), scalar1=gc, scalar2=None, op0=ALU.subtract),
                       reads=[PB_(4, 0), gvb], writes=[tlb[ND]])
                    yield op("act", lambda e: e.activation(out=T_(ND), in_=T_(ND), func=AF.Abs), reads=[tlb[ND]], writes=[tlb[ND]])
                    yield op("act", lambda e: e.activation(out=T_(MM), in_=T_(ND), func=AF.Exp, scale=-1.0), reads=[tlb[ND]], writes=[tlb[MM]])
                    yield op("act", lambda e: e.activation(out=T_(EG), in_=P_(4, 0), func=AF.Exp), reads=[PB_(4, 0)], writes=[tlb[EG]])
                    yield op("dve", lambda e: e.tensor_tensor(out=T_(ML), in0=T_(MM), in1=gmask[:, mA, :], op=ALU.mult),
                       reads=[tlb[MM], cb], writes=[tlb[ML]])
                    yield op("dve", lambda e: e.tensor_tensor(out=T_(MU), in0=T_(MM), in1=gmask[:, mB, :], op=ALU.mult),
                       reads=[tlb[MM], cb], writes=[tlb[MU]])
                    chk(10)
                    yield op("pe", lambda e: mm_(e, P_(5, 0), lhsT=FM[KT][:, tsl], rhs=FM[KT][:, tsl], start=True, stop=True),
                       reads=[FMB[KT]], writes=[PB_(5, 0)])
                    yield op("dve", lambda e: e.scalar_tensor_tensor(out=T_(NA), in0=P_(5, 0), scalar=col(NBETA, c, d, hd), in1=T_(ML),
                                                               op0=ALU.mult, op1=ALU.mult), reads=[PB_(5, 0), gvb, tlb[ML]], writes=[tlb[NA]])
                    yield op("pe", lambda e: e.transpose(out=P_(5, 1), in_=T_(NA), identity=ident[:]), reads=[tlb[NA], identb], writes=[PB_(5, 1)])
                    yield op("dve", lambda e: e.tensor_copy(out=T_(NTA), in_=P_(5, 1)), reads=[PB_(5, 1)], writes=[tlb[NTA]])
                    for (dst_, src_, mi) in ((NBD, NA, 4), (NTBD, NTA, 4), (E1, NA, 5), (E1T, NTA, 5), (E2, NA, 6)):
                        yield op("pool", lambda e: e.tensor_tensor(out=T_(dst_), in0=T_(src_), in1=gmask[:, mi, :], op=ALU.mult),
                           reads=[tlb[src_], cb], writes=[tlb[dst_]])
                    yield op("dve", lambda e: e.tensor_tensor(out=T_(RTA), in0=T_(NTBD), in1=ident[:], op=ALU.add), reads=[tlb[NTBD], identb], writes=[tlb[RTA]])
                    cur = (NBD, NTBD, RTA)
                    nxt = (NB_, NTB, RTB)
                    for lvl in range(4):
                        n_, nt_, rt_ = cur
                        n2, nt2, rt2 = nxt
                        yield op("pe", lambda e: mm_(e, P_(6, 0), lhsT=T_(nt_), rhs=T_(n_), start=True, stop=True),
                           reads=[tlb[nt_], tlb[n_]], writes=[PB_(6, 0)])
                        yield op("act", lambda e: e.activation(out=T_(n2), in_=P_(6, 0), func=AF.Copy), reads=[PB_(6, 0)], writes=[tlb[n2]])
                        if lvl < 3:
                            yield op("pe", lambda e: mm_(e, P_(7, 0), lhsT=T_(n_), rhs=T_(nt_), start=True, stop=True),
                               reads=[tlb[nt_], tlb[n_]], writes=[PB_(7, 0)])
                            yield op("act", lambda e: e.activation(out=T_(nt2), in_=P_(7, 0), func=AF.Copy), reads=[PB_(7, 0)], writes=[tlb[nt2]])
                        yield op("pe", lambda e: mm_(e, P_(5, 0), lhsT=T_(n2), rhs=T_(rt_), start=True, stop=True),
                           reads=[tlb[n2], tlb[rt_]], writes=[PB_(5, 0)])
                        yield op("dve", lambda e: e.tensor_tensor(out=T_(rt2), in0=P_(5, 0), in1=T_(rt_), op=ALU.add),
                           reads=[PB_(5, 0), tlb[rt_]], writes=[tlb[rt2]])
                        cur, nxt = nxt, cur
                    DT0 = cur[2]
                    yield op("pe", lambda e: e.transpose(out=P_(5, 1), in_=T_(DT0), identity=ident[:]), reads=[tlb[DT0], identb], writes=[PB_(5, 1)])
                    yield op("act", lambda e: e.activation(out=T_(DD), in_=P_(5, 1), func=AF.Copy), reads=[PB_(5, 1)], writes=[tlb[DD]])
                    yield op("pe", lambda e: mm_(e, P_(6, 0), lhsT=T_(E1T), rhs=T_(DD), start=True, stop=True),
                       reads=[tlb[E1T], tlb[DD]], writes=[PB_(6, 0)])
                    yield op("act", lambda e: e.activation(out=T_(PP), in_=P_(6, 0), func=AF.Copy), reads=[PB_(6, 0)], writes=[tlb[PP]])
                    yield op("pe", lambda e: mm_(e, P_(7, 0), lhsT=T_(DT0), rhs=T_(PP), start=True, stop=True),
                       reads=[tlb[DT0], tlb[PP]], writes=[PB_(7, 0)])
                    yield op("dve", lambda e: e.tensor_tensor(out=T_(DD2), in0=P_(7, 0), in1=T_(DD), op=ALU.add),
                       reads=[PB_(7, 0), tlb[DD]], writes=[tlb[DD2]])
                    yield op("pe", lambda e: mm_(e, P_(6, 1), lhsT=T_(E1), rhs=T_(DT0), start=True, stop=True),
                       reads=[tlb[E1], tlb[DT0]], writes=[PB_(6, 1)])
                    yield op("act", lambda e: e.activation(out=T_(XX), in_=P_(6, 1), func=AF.Copy), reads=[PB_(6, 1)], writes=[tlb[XX]])
                    yield op("pe", lambda e: mm_(e, P_(7, 1), lhsT=T_(DD), rhs=T_(XX), start=True, stop=True),
                       reads=[tlb[DD], tlb[XX]], writes=[PB_(7, 1)])
                    yield op("dve", lambda e: e.tensor_tensor(out=T_(DT2), in0=P_(7, 1), in1=T_(DT0), op=ALU.add),
                       reads=[PB_(7, 1), tlb[DT0]], writes=[tlb[DT2]])
                    yield op("pe", lambda e: mm_(e, P_(6, 0), lhsT=T_(E2), rhs=T_(DT2), start=True, stop=True),
                       reads=[tlb[E2], tlb[DT2]], writes=[PB_(6, 0)])
                    yield op("act", lambda e: e.activation(out=T_(XX), in_=P_(6, 0), func=AF.Copy), reads=[PB_(6, 0)], writes=[tlb[XX]])
                    yield op("pe", lambda e: mm_(e, P_(7, 0), lhsT=T_(DD2), rhs=T_(XX), start=True, stop=True),
                       reads=[tlb[DD2], tlb[XX]], writes=[PB_(7, 0)])
                    yield op("dve", lambda e: e.tensor_tensor(out=T_(RTF), in0=P_(7, 0), in1=T_(DT2), op=ALU.add),
                       reads=[PB_(7, 0), tlb[DT2]], writes=[tlb[RTF]])
                    chk(11)
                    RT = RTF
                    yield op("dve", lambda e: e.tensor_scalar(out=T_(VB), in0=vtok[:, c, :], scalar1=col(BETA, c, d, hd), scalar2=None, op0=ALU.mult),
                       reads=[vtokb, gvb], writes=[tlb[VB]])
                    yield op("dve", lambda e: e.tensor_scalar(out=T_(KBG), in0=ktok[:, c, :], scalar1=col(BG, c, d, hd), scalar2=None, op0=ALU.mult),
                       reads=[ktokb, gvb], writes=[tlb[KBG]])
                    yield op("dve", lambda e: e.tensor_scalar(out=T_(KDE), in0=ktok[:, c, :], scalar1=col(KD, c, d, hd), scalar2=None, op0=ALU.mult),
                       reads=[ktokb, gvb], writes=[tlb[KDE]])
                    yield op("pe", lambda e: mm_(e, P_(6, 0), lhsT=T_(RT), rhs=T_(VB), start=True, stop=True),
                       reads=[tlb[RT], tlb[VB]], writes=[PB_(6, 0)])
                    yield op("act", lambda e: e.activation(out=T_(UU), in_=P_(6, 0), func=AF.Copy), reads=[PB_(6, 0)], writes=[tlb[UU]])
                    yield op("pe", lambda e: mm_(e, P_(7, 0), lhsT=T_(KBG), rhs=T_(RT), start=True, stop=True),
                       reads=[tlb[RT], tlb[KBG]], writes=[PB_(7, 0)])
                    yield op("act", lambda e: e.activation(out=T_(WT), in_=P_(7, 0), func=AF.Copy), reads=[PB_(7, 0)], writes=[tlb[WT]])
                    yield op("pe", lambda e: mm_(e, P_(5, 0), lhsT=FM[KT][:, tsl], rhs=FM[QT][:, tsl], start=True, stop=True),
                       reads=[FMB[KT], FMB[QT]], writes=[PB_(5, 0)])
                    yield op("dve", lambda e: e.tensor_tensor(out=T_(ATT), in0=P_(5, 0), in1=T_(MU), op=ALU.mult),
                       reads=[PB_(5, 0), tlb[MU]], writes=[tlb[ATT]])
                    yield op("dve", lambda e: e.tensor_tensor(out=T_(QD), in0=FM[QT][:, tsl], in1=T_(EG), op=ALU.mult),
                       reads=[FMB[QT], tlb[EG]], writes=[tlb[QD]])
                    chk(12)
                    yield op("pe", lambda e: mm_(e, P_(6, 0), lhsT=T_(WT), rhs=T_(SS), start=True, stop=True),
                       reads=[tlb[WT], tlb[SS]], writes=[PB_(6, 0)])
                    yield op("dve", lambda e: e.tensor_tensor(out=T_(VN), in0=T_(UU), in1=P_(6, 0), op=ALU.subtract),
                       reads=[PB_(6, 0), tlb[UU]], writes=[tlb[VN]])
                    op("pe", lambda e: mm_(e, P_(7, 0), lhsT=T_(SS), rhs=T_(QD), start=True, stop=False),
                       reads=[tlb[SS], tlb[QD]], writes=[PB_(7, 0)], inc=False)
                    yield op("pe", lambda e: mm_(e, P_(7, 0), lhsT=T_(VN), rhs=T_(ATT), start=False, stop=True),
                       reads=[tlb[VN], tlb[ATT]], writes=[PB_(7, 0)])
                    yield op("act", lambda e: e.activation(out=OAd[d][:, tsl], in_=P_(7, 0), func=AF.Copy), reads=[PB_(7, 0)], writes=([OAdb[d], FMB[OA]] if d == 0 else [OAdb[d]]))
                    yield op("pe", lambda e: mm_(e, P_(6, 1), lhsT=T_(KDE), rhs=T_(VN), start=True, stop=True),
                       reads=[tlb[KDE], tlb[VN]], writes=[PB_(6, 1)])
                    egl = tl_[:, EG, 127:128] if d == 0 else tl_[:, EG, 0:1]
                    yield op("dve", lambda e: e.scalar_tensor_tensor(out=T_(SS), in0=T_(SS), scalar=egl, in1=P_(6, 1), op0=ALU.mult,
                                                               op1=ALU.add), reads=[PB_(6, 1), tlb[EG], tlb[SS]], writes=[tlb[SS]])
                    if seg_end:
                        dma("sp", st_out[seg, d, hd], T_(SS), reads=[tlb[SS]], writes=[stoutb[(seg * 2 + d) * 8 + hd]])
            for b_ in range(8):
                for s_ in range(4):
                    psl[b_][s_].w = dict(PSB[b_].w)
                    psl[b_][s_].r = dict(PSB[b_].r)
            gens = [gdn_dir(0), gdn_dir(1)]
            if not GDN_INTERLEAVE:
                for g_ in gens:
                    for _ in g_:
                        pass
                gens = []
            while gens:
                for g_ in list(gens):
                    try:
                        next(g_)
                    except StopIteration:
                        gens.remove(g_)
            for b_ in range(8):
                for s_ in range(4):
                    for (dst_, src_) in ((PSB[b_].w, psl[b_][s_].w), (PSB[b_].r, psl[b_][s_].r)):
                        for k_, v_ in src_.items():
                            if k_ not in dst_ or dst_[k_][1] < v_[1]:
                                dst_[k_] = v_
            op("dve", lambda e: e.tensor_tensor(out=FM[OA][:], in0=OAd[0][:], in1=OAd[1][:], op=ALU.add), reads=OAdb, writes=[FMB[OA]])
            chk(13)
            if DEBUG and hd == 0:
                dma("sp", dbg_q[:, :], FM[OA][:], reads=[FMB[OA]], writes=[dbgb])
            partnorm(FM[OA], FMB[OA], None)
            partnorm_apply(FM[OA], FMB[OA], 1.0 / 128.0, EPS)
            if DEBUG and hd == 0:
                dma("sp", dbg_k[:, :], FM[SQ][:], reads=[FMB[SQ]], writes=[dbgb])
            op("dve", lambda e: e.scalar_tensor_tensor(out=yo[:], in0=FM[OA][:], scalar=onorm[:, 0:1], in1=FM[ZT][:], op0=ALU.mult,
                                                       op1=ALU.mult), reads=[FMB[OA], FMB[ZT], cb], writes=[yob])
            dma("sp", ycatT[8 + hd], yo[:], reads=[yob], writes=[ycb[8 + hd]])
            if DEBUG and hd == 0:
                dma("sp", dbg_oa[:, :], FM[OA][:], reads=[FMB[OA]], writes=[dbgb])
                dma("sp", dbg_z[:, :], FM[ZT][:], reads=[FMB[ZT]], writes=[dbgb])
        kb.release(gm)
        kb.release(mk)

        chk(14)
        mk2 = kb.mark()
        wo = kb.sb("wo", [128, 16, D], BF16)
        wob = Buf("wo")
        for q in range(4):
            dma("pool", wo[:, :, q * 512:(q + 1) * 512], w_out[:, q * 512:(q + 1) * 512].rearrange("(c p) n -> p c n", p=128), writes=[wob])
        yT = [kb.sb("yT%d" % i, [128, 16, 128], BF16) for i in range(2)]
        yTb = bufs(2, "yT")
        xr = [kb.sb("xr%d" % i, [128, D], F32) for i in range(2)]
        xrb = bufs(2, "xr")
        hy = kb.sb("hyacc", [128, D], F32)
        hyb = Buf("hyacc")
        tmp = [kb.sb("mtmp%d" % i, [128, 512], F32) for i in range(2)]
        tmpb = bufs(2, "mtmp")
        junk2 = kb.sb("junk2", [128, 512], BF16)
        junk2b = Buf("junk2")
        st2 = kb.sb("st2", [128, 8], F32)
        st2b = Buf("st2")
        Gp = kb.sb("mGp", [128, D], F32)
        gpost = kb.sb("mgpost", [128, D], F32)
        Gpb = Buf("mGp")
        hyg = kb.sb("hyrs", [128, 16], F32)
        dma("sp", Gp[:], modrow[0, 5 * D:6 * D].partition_broadcast(128), reads=[modb], writes=[Gpb])
        dma("sp", gpost[:], norm_post[1, :].partition_broadcast(128), writes=[Gpb])
        op("dve", lambda e: e.tensor_tensor(out=Gp[:], in0=Gp[:], in1=gpost[:], op=ALU.mult), reads=[Gpb], writes=[Gpb])
        op("dve", lambda e: e.tensor_scalar(out=hyg[:], in0=hyss[:], scalar1=1.0 / 1024.0, scalar2=EPS, op0=ALU.mult, op1=ALU.add),
           reads=[hyssb], writes=[hyssb])
        op("act", lambda e: e.activation(out=hyg[:], in_=hyg[:], func=AF.Sqrt), reads=[hyssb], writes=[hyssb])
        op("dve", lambda e: e.reciprocal(out=hyg[:], in_=hyg[:]), reads=[hyssb], writes=[hyssb])
        for t in range(NT):
            s = t % 2
            with nc.allow_non_contiguous_dma(reason="256B rows"):
                dma("sp", yT[s][:], ycatT[:, :, t * 128:(t + 1) * 128].rearrange("c p n -> p c n"), reads=ycb, writes=[yTb[s]])
            dma("sp", xr[s][:], xres[t * 128:(t + 1) * 128, :], reads=[ybuf[t]], writes=[xrb[s]])
            for nbk in range(4):
                for c in range(8):
                    op("pe", lambda e: e.matmul(PS[nbk][:, :], lhsT=yT[s][:, c, :], rhs=wo[:, c, nbk * 512:(nbk + 1) * 512],
                                                start=(c == 0), stop=(c == 7)), reads=[yTb[s], wob], writes=[PSB[nbk]], inc=(c == 7))
                for c in range(8, 16):
                    op("pe", lambda e: e.matmul(PS[4 + nbk][:, :], lhsT=yT[s][:, c, :], rhs=wo[:, c, nbk * 512:(nbk + 1) * 512],
                                                start=(c == 8), stop=(c == 15)), reads=[yTb[s], wob], writes=[PSB[4 + nbk]], inc=(c == 15))
                sl = slice(nbk * 512, (nbk + 1) * 512)
                op("act", lambda e: e.activation(out=hy[:, sl], in_=PS[4 + nbk][:, :], func=AF.Copy), reads=[PSB[4 + nbk]], writes=[hyb])
                op("dve", lambda e: e.scalar_tensor_tensor(out=hy[:, sl], in0=PS[nbk][:, :], scalar=hyg[:, t:t + 1], in1=hy[:, sl],
                                                           op0=ALU.mult, op1=ALU.add), reads=[PSB[nbk], hyssb, hyb], writes=[hyb])
                op("act", lambda e: e.activation(out=junk2[:], in_=hy[:, sl], func=AF.Square, accum_out=st2[:, nbk:nbk + 1]),
                   reads=[hyb], writes=[junk2b, st2b])
            op("dve", lambda e: e.tensor_tensor(out=st2[:, 0:2], in0=st2[:, 0:2], in1=st2[:, 2:4], op=ALU.add), reads=[st2b], writes=[st2b])
            op("dve", lambda e: e.tensor_tensor(out=st2[:, 0:1], in0=st2[:, 0:1], in1=st2[:, 1:2], op=ALU.add), reads=[st2b], writes=[st2b])
            rstd_from_ss(st2[:, 0:1], st2[:, 4:5], st2[:, 5:6], st2b, D)
            for q in range(4):
                sl = slice(q * 512, (q + 1) * 512)
                ts_ = q % 2
                op("dve", lambda e: e.scalar_tensor_tensor(out=tmp[ts_][:], in0=hy[:, sl], scalar=st2[:, 4:5], in1=Gp[:, sl],
                                                           op0=ALU.mult, op1=ALU.mult), reads=[hyb, st2b, Gpb], writes=[tmpb[ts_]])
                op("dve", lambda e: e.tensor_tensor(out=xr[s][:, sl], in0=tmp[ts_][:], in1=xr[s][:, sl], op=ALU.add),
                   reads=[tmpb[ts_], xrb[s]], writes=[xrb[s]])
            dma("sp", (xres if STAGE >= 3 else y_out)[t * 128:(t + 1) * 128, :], xr[s][:], reads=[xrb[s]], writes=[ybuf[t]])
        kb.release(mk2)

    xinbufs = bufs(NT, "xsrc")
    if STAGE == 0:
        dma("sp", y_out[0:128, 0:144], modcol[:], reads=[colb], writes=[ybuf[0]])
        dma("sp", y_out[128:256, 0:48], Acol[:], reads=[colb], writes=[ybuf[1]])
    if STAGE >= 1:
        ffn_stage(0, 0, x_in, xinbufs, True, xres if STAGE >= 2 else y_out)
    if STAGE >= 2 and RUN_MIXER:
        try:
            mixer_stage()
        except StopMix:
            pass
    if STAGE >= 3:
        ffn_stage(2, 1, xres, ybuf, False, y_out)
    kb.barrier()
    return kb


def grid_pos():
    rows = T // 64
    r = np.broadcast_to(np.arange(rows, dtype=np.float32)[:, None], (rows, 64)).reshape(-1)
    col = np.broadcast_to(np.arange(64, dtype=np.float32)[None, :], (rows, 64)).reshape(-1)
    quarter = D // 4
    omega = (1.0 / (np.float32(10000.0) ** (np.arange(quarter, dtype=np.float32) / np.float32(quarter)))).astype(np.float32)
    ar = r[:, None] * omega[None]
    ac = col[:, None] * omega[None]
    return np.concatenate([np.sin(ar), np.cos(ar), np.sin(ac), np.cos(ac)], axis=-1).astype(np.float32)


_HC = {}


def hyena_consts(L):
    if L in _HC:
        return _HC[L]
    import ml_dtypes
    f = np.float32
    nrep = T // L
    idx = np.arange(L, dtype=np.float64)
    tt = (idx / (L - 1)).astype(f)
    bands = np.arange(1, 17, dtype=np.float64)
    ang = (2.0 * math.pi / L) * idx[:, None] * bands[None, :]
    feats = np.concatenate([tt[:, None].astype(np.float64), np.cos(ang), np.sin(ang)], axis=-1).astype(f)
    featsT = np.ascontiguousarray(np.tile(feats, (nrep, 1)).T)
    lagp = np.tile(np.arange(L), nrep)
    col = np.zeros((128, 64), f)
    lag2 = lagp.reshape(16, 128).T
    col[:, 0:16] = -(lag2 / (L - 1.0))
    col[:, 16:32] = (lag2 != 0)
    col[:, 32:48] = (lag2 == 0)
    col[:, 48:64] = (lag2 != 0)
    ph = math.pi * np.outer(idx, idx) / L
    sgn = np.where(idx % 2 == 0, 1.0, -1.0)
    cf = np.cos(ph)
    sf = -np.sin(ph)
    sf[:, 0] = sgn
    wf = np.full(L, 1.0 / L)
    wf[0] = 0.5 / L
    gc = (np.cos(ph) * wf[None, :]).T
    gs = (-np.sin(ph) / L).T
    gs[0, :] = 0.5 / L * sgn

    def tiles(blk):
        full = np.zeros((T, T), f)
        for r in range(nrep):
            full[r * L:(r + 1) * L, r * L:(r + 1) * L] = blk
        return np.ascontiguousarray(full.reshape(16, 128, 16, 128).transpose(2, 1, 0, 3)).astype(ml_dtypes.bfloat16)
    out = {"feats": featsT, "hcol": col, "nrep": np.full((128, 1), float(nrep), f),
           "CF_t": tiles(cf), "SF_t": tiles(sf), "GC_t": tiles(gc), "GS_t": tiles(gs)}
    _HC[L] = out
    return out


def core_inputs(core, inp):
    f = np.float32
    m = {}
    if core < 4:
        m["x"] = np.ascontiguousarray(inp["x_sample"][core])
        m["pos"] = grid_pos()
        m["cvec"] = np.ascontiguousarray(inp["c"][core:core + 1])
    else:
        xp = np.zeros((T, D), f)
        xp[:1024] = inp["x_prompt"][(core - 4) * 4:(core - 4) * 4 + 4].reshape(1024, D)
        m["x"] = xp
        m["pos"] = np.zeros((T, D), f)
        m["cvec"] = np.ascontiguousarray(inp["c_ctx"].reshape(1, D))
    m["ada_w"] = np.ascontiguousarray(inp["ada_w"][0])
    m["ada_b"] = np.ascontiguousarray(inp["ada_b"][0].reshape(1, -1))
    m["norm_pre"] = np.ascontiguousarray(inp["norm_pre"][0])
    m["norm_post"] = np.ascontiguousarray(inp["norm_post"][0])
    m["ffn_wg"] = np.ascontiguousarray(inp["ffn_wg"][0])
    m["ffn_wu"] = np.ascontiguousarray(inp["ffn_wu"][0])
    m["ffn_wd"] = np.ascontiguousarray(inp["ffn_wd"][0])
    m["ident"] = np.eye(128, dtype=f)
    m["w_in"] = np.ascontiguousarray(inp["w_in"][0])
    m["w_out"] = np.ascontiguousarray(inp["w_out"][0])
    m["gdn_conv_w"] = np.ascontiguousarray(inp["gdn_conv_w"][0])
    m["hy_conv_w"] = np.ascontiguousarray(inp["hy_conv_w"][0])
    m["gdn_o_norm"] = np.ascontiguousarray(inp["gdn_o_norm"][0].reshape(1, 128))
    p = np.arange(128)[:, None]
    q = np.arange(128)[None, :]
    def bm(b):
        return (p // b) == (q // b)
    m["gmask"] = np.stack([(p > q), (q >= p), (q > p), (q <= p), bm(32), bm(64) & ~bm(32), ~bm(64)]).astype(f)
    sample = core < 4
    keep = np.ones((1, 8), f) if sample else np.zeros((1, 8), f)
    m["keep"] = keep
    m["bmask"] = np.zeros((1, 7), f) if sample else np.ones((1, 7), f)
    m["dtb"] = np.ascontiguousarray(np.tile(inp["gdn_dt_bias"][0].reshape(1, 16), (1, 16)))
    m["alog"] = np.ascontiguousarray(np.tile(inp["gdn_a_log"][0].reshape(1, 16), (1, 16)))
    for nm in ("hy_f_w1", "hy_f_w2", "hy_f_w3", "hy_decay", "hy_bias"):
        m[nm] = np.ascontiguousarray(inp[nm][0])
    m["hy_f_b1"] = np.ascontiguousarray(inp["hy_f_b1"][0].reshape(1, 64))
    m["hy_f_b2"] = np.ascontiguousarray(inp["hy_f_b2"][0].reshape(1, 64))
    m["hy_out_norm"] = np.ascontiguousarray(inp["hy_out_norm"][0].reshape(1, 1024))
    m.update(hyena_consts(2048 if sample else 256))
    m["s0"] = np.ascontiguousarray(inp["state_gdn"][core, 0]) if sample else np.zeros((2, 8, 128, 128), f)
    return m


def kernel(**inputs):
    inp = {k: np.asarray(v) for k, v in inputs.items()}
    kb = build()
    in_maps = [core_inputs(c, inp) for c in range(8)]
    res = run_bass_kernel_spmd(kb.nc, in_maps, core_ids=list(range(8)))
    ys = [r["y"] for r in res.results]
    y_sample = np.stack(ys[:4], axis=0).astype(np.float32)
    y_prompt = np.concatenate([ys[c][:1024].reshape(4, 256, D) for c in range(4, 8)], axis=0).astype(np.float32)
    new_state = np.zeros((16, 1, 2, 8, 128, 128), np.float32)
    for c in range(4, 8):
        so = res.results[c]["st_out"]
        for sq in range(4):
            new_state[(c - 4) * 4 + sq, 0] = so[sq]
    return (y_prompt, y_sample, new_state)
```

```python
import math
import numpy as np
import concourse.bass as bass
import concourse.mybir as mybir
from concourse.bass_utils import run_bass_kernel_spmd

F32 = mybir.dt.float32
BF16 = mybir.dt.bfloat16
AF = mybir.ActivationFunctionType
ALU = mybir.AluOpType

D = 2048
DFF = 5632
T = 2048
NT = 16
TB = 512
WIN = 7200
EPS = 1e-6
FFN_PARTS = 4
MIX_PARTS = 99
GDN_F32R = False
SKIP_H2 = False
GDN_INTERLEAVE = True
DEBUG = False
RUN_MIXER = True


class StopMix(Exception):
    pass


def chk(n):
    if MIX_PARTS <= n:
        raise StopMix()
EVAC_ACT = False
STAGE = 9


class Buf:
    __slots__ = ("w", "r", "dsem", "dcnt", "name")

    def __init__(self, name=""):
        self.w = {}
        self.r = {}
        self.dsem = None
        self.dcnt = 0
        self.name = name


class KB:
    def __init__(self):
        self.nc = bass.Bass("TRN2", target_bir_lowering=False)
        nc = self.nc
        self.E = {}
        for nm, e in (("pe", nc.tensor), ("act", nc.scalar), ("dve", nc.vector),
                      ("pool", nc.gpsimd), ("sp", nc.sync)):
            sem = nc.semaphore("s_" + nm).__enter__()
            self.E[nm] = dict(e=e, sem=sem, cnt=0, waited={}, name=nm)
        self.dsems = []
        self.ctx = []
        self.n_ins = 0

    def sb(self, name, shape, dt):
        self.uid = getattr(self, "uid", 0) + 1
        cm = self.nc.sbuf_tensor("%s_s%d" % (name, self.uid), shape, dt)
        t = cm.__enter__()
        self.ctx.append(cm)
        return t

    def ps(self, name, shape, dt=F32):
        self.uid = getattr(self, "uid", 0) + 1
        cm = self.nc.psum_tensor("%s_p%d" % (name, self.uid), shape, dt)
        t = cm.__enter__()
        self.ctx.append(cm)
        return t

    def mark(self):
        return len(self.ctx)

    def release(self, mark):
        self.barrier()
        while len(self.ctx) > mark:
            cm = self.ctx.pop()
            cm.__exit__(None, None, None)

    def _deps(self, reads, writes):
        deps = {}
        for b in reads:
            for k, v in b.w.items():
                if k not in deps or deps[k][1] < v[1]:
                    deps[k] = v
        for b in writes:
            for dd in (b.w, b.r):
                for k, v in dd.items():
                    if k not in deps or deps[k][1] < v[1]:
                        deps[k] = v
        return deps

    def _wait(self, E, deps):
        for k, (semobj, val) in deps.items():
            if E["name"] == "pe" and semobj is E["sem"]:
                continue
            if E["waited"].get(k, 0) >= val:
                continue
            E["e"].wait_ge(semobj, val)
            E["waited"][k] = val

    def _record(self, tag, reads, writes):
        k = id(tag[0])
        for b in reads:
            if k not in b.r or b.r[k][1] < tag[1]:
                b.r[k] = tag
        for b in writes:
            b.w = {k: tag}
            b.r = {}

    def op(self, eng, emit, reads=(), writes=(), inc=True):
        E = self.E[eng]
        self._wait(E, self._deps(reads, writes))
        ins = emit(E["e"])
        self.n_ins += 1
        if inc:
            E["cnt"] += 1
            ins.then_inc(E["sem"], 1)
            tag = (E["sem"], E["cnt"])
        else:
            tag = (E["sem"], E["cnt"] + 1)
        self._record(tag, reads, writes)

    def dma(self, q, out, in_, reads=(), writes=(), **kw):
        E = self.E[q]
        self._wait(E, self._deps(reads, writes))
        b = writes[0]
        if b.dsem is None:
            b.dsem = self.nc.semaphore("d%d" % len(self.dsems)).__enter__()
            self.dsems.append(b)
        b.dcnt += 16
        E["e"].dma_start(out=out, in_=in_, **kw).then_inc(b.dsem, 16)
        self.n_ins += 1
        self._record((b.dsem, b.dcnt), reads, writes)

    def barrier(self):
        allsem = [(E["sem"], E["cnt"]) for E in self.E.values() if E["cnt"] > 0]
        allsem += [(b.dsem, b.dcnt) for b in self.dsems]
        for E in self.E.values():
            deps = {id(s): (s, v) for s, v in allsem if s is not E["sem"]}
            self._wait(E, deps)


def bufs(n, name=""):
    return [Buf("%s%d" % (name, i)) for i in range(n)]


def build():
    kb = KB()
    nc = kb.nc
    op, dma = kb.op, kb.dma

    def din(name, shape, dt=F32):
        return nc.dram_tensor(name, list(shape), dt, kind="ExternalInput").ap()

    x_in = din("x", [T, D])
    pos_in = din("pos", [T, D])
    cvec = din("cvec", [1, D])
    ada_w = din("ada_w", [D, 9 * D])
    ada_b = din("ada_b", [1, 9 * D])
    norm_pre = din("norm_pre", [3, D])
    norm_post = din("norm_post", [3, D])
    ffn_wg = din("ffn_wg", [2, D, DFF])
    ffn_wu = din("ffn_wu", [2, D, DFF])
    ffn_wd = din("ffn_wd", [2, DFF, D])
    ident_in = din("ident", [128, 128])
    w_in = din("w_in", [D, WIN])
    w_out = din("w_out", [D, D])
    gdn_conv_w = din("gdn_conv_w", [3, 3072])
    hy_conv_w = din("hy_conv_w", [3, 3072])
    gdn_o_norm = din("gdn_o_norm", [1, 128])
    gmask_in = din("gmask", [7, 128, 128])
    keep_in = din("keep", [1, 8])
    bmask_in = din("bmask", [1, 7])
    dtb_in = din("dtb", [1, 256])
    alog_in = din("alog", [1, 256])
    s0_in = din("s0", [2, 8, 128, 128])
    hy_f_w1 = din("hy_f_w1", [33, 64])
    hy_f_b1 = din("hy_f_b1", [1, 64])
    hy_f_w2 = din("hy_f_w2", [64, 64])
    hy_f_b2 = din("hy_f_b2", [1, 64])
    hy_f_w3 = din("hy_f_w3", [64, 4096])
    hy_decay = din("hy_decay", [2, 1024])
    hy_bias = din("hy_bias", [2, 1024])
    hy_out_norm = din("hy_out_norm", [1, 1024])
    feats_in = din("feats", [33, T])
    hcol_in = din("hcol", [128, 64])
    nrep_in = din("nrep", [128, 1])
    CF_t = din("CF_t", [16, 128, 16, 128], BF16)
    SF_t = din("SF_t", [16, 128, 16, 128], BF16)
    GC_t = din("GC_t", [16, 128, 16, 128], BF16)
    GS_t = din("GS_t", [16, 128, 16, 128], BF16)
    tapsS = nc.dram_tensor("tapsS", [T, 2048], BF16, kind="Internal").ap()
    tapsD = nc.dram_tensor("tapsD", [T, 2048], BF16, kind="Internal").ap()
    tapsb = Buf("taps")
    spec_re = nc.dram_tensor("spec_re", [T, 2048], F32, kind="Internal").ap()
    spec_im = nc.dram_tensor("spec_im", [T, 2048], F32, kind="Internal").ap()
    spec_rb = nc.dram_tensor("spec_rb", [T, 2048], F32, kind="Internal").ap()
    specb = Buf("spec")
    st_out = nc.dram_tensor("st_out", [8, 2, 8, 128, 128], F32, kind="ExternalOutput").ap()
    stoutb = [Buf("stout")] * 128
    ycatT = nc.dram_tensor("ycatT", [16, 128, T], BF16, kind="Internal").ap()
    ycb = [b_ for b_ in bufs(4, "ycat") for _ in range(4)]
    dbgb = Buf("dbg")
    if DEBUG:
        dbg_gv = nc.dram_tensor("dbg_gv", [128, 12 * 256], F32, kind="ExternalOutput").ap()
        dbg_oa = nc.dram_tensor("dbg_oa", [128, T], F32, kind="ExternalOutput").ap()
        dbg_q = nc.dram_tensor("dbg_q", [128, T], F32, kind="ExternalOutput").ap()
        dbg_k = nc.dram_tensor("dbg_k", [128, T], F32, kind="ExternalOutput").ap()
        dbg_z = nc.dram_tensor("dbg_z", [128, T], F32, kind="ExternalOutput").ap()
    y_out = nc.dram_tensor("y", [T, D], F32, kind="ExternalOutput").ap()
    modrow = nc.dram_tensor("modrow", [1, 9 * D], F32, kind="Internal").ap()
    xres = nc.dram_tensor("xres", [T, D], F32, kind="Internal").ap()

    ybuf = bufs(NT, "y")
    modb = Buf("modrow")
    NB = Buf("none")

    ident = kb.sb("ident", [128, 128], F32)
    identb = Buf("ident")
    dma("sp", ident[:], ident_in[:, :], writes=[identb])
    modcol = kb.sb("modcol", [128, 144], F32)
    npre = kb.sb("npre", [128, 48], F32)
    Acol = kb.sb("Acol", [128, 48], F32)
    colb = Buf("cols")
    hyss = kb.sb("hyss", [128, 16], F32)
    hyssb = Buf("hyss")
    PSALL = kb.ps("psall", [128, 4096])
    PS = [PSALL[:, i * 512:(i + 1) * 512] for i in range(8)]
    PSB = bufs(8, "ps")

    mk = kb.mark()
    cs = kb.sb("cs", [128, 16], F32)
    css = kb.sb("css", [128, 16], F32)
    csb = Buf("cs")
    with nc.allow_non_contiguous_dma(reason="small vector relayout"):
        dma("sp", cs[:], cvec.rearrange("o (kc p) -> p (o kc)", p=128), writes=[csb])
    op("act", lambda e: e.activation(out=css[:], in_=cs[:], func=AF.Silu), reads=[csb], writes=[csb])
    aw = [kb.sb("aw%d" % i, [128, 16, 512], F32) for i in range(3)]
    awb = bufs(3, "aw")
    abr = [kb.sb("abr%d" % i, [1, 512], F32) for i in range(3)]
    abrb = bufs(3, "abr")
    mrow = [kb.sb("mrow%d" % i, [1, 512], F32) for i in range(3)]
    mrowb = bufs(3, "mrow")
    for cg in range(36):
        s = cg % 3
        dma("sp", aw[s][:], ada_w[:, cg * 512:(cg + 1) * 512].rearrange("(kc p) n -> p kc n", p=128),
            writes=[awb[s]])
        dma("sp", abr[s][:], ada_b[:, cg * 512:(cg + 1) * 512], writes=[abrb[s]])
        bk = cg % 2
        for kc in range(16):
            op("pe", lambda e, kc=kc: e.matmul(PS[bk][0:1, :], lhsT=css[:, kc:kc + 1], rhs=aw[s][:, kc, :],
                                               start=(kc == 0), stop=(kc == 15)),
               reads=[csb, awb[s]], writes=[PSB[bk]], inc=(kc == 15))
        op("dve", lambda e: e.tensor_tensor(out=mrow[s][:], in0=PS[bk][0:1, :], in1=abr[s][:], op=ALU.add),
           reads=[PSB[bk], abrb[s]], writes=[mrowb[s]])
        dma("sp", modrow[:, cg * 512:(cg + 1) * 512], mrow[s][:], reads=[mrowb[s]], writes=[modb])
    with nc.allow_non_contiguous_dma(reason="small vector relayout"):
        dma("sp", modcol[:], modrow.rearrange("o (m kc p) -> p (o m kc)", p=128, kc=16), reads=[modb], writes=[colb])
        dma("sp", npre[:], norm_pre.rearrange("s (kc p) -> p (s kc)", p=128), writes=[colb])
    for s in range(3):
        sc = modcol[:, (3 * s + 1) * 16:(3 * s + 2) * 16]
        op("dve", lambda e, s=s, sc=sc: e.scalar_tensor_tensor(
            out=Acol[:, s * 16:(s + 1) * 16], in0=sc, scalar=1.0, in1=npre[:, s * 16:(s + 1) * 16],
            op0=ALU.add, op1=ALU.mult), reads=[colb], writes=[colb])
    kb.release(mk)

    def Bcol(s, kc):
        return modcol[:, 3 * s * 16 + kc:3 * s * 16 + kc + 1]

    def rstd_from_ss(ss_ap, out_ap, tmp_ap, rb, n):
        op("dve", lambda e: e.tensor_scalar(out=tmp_ap, in0=ss_ap, scalar1=1.0 / n, scalar2=EPS,
                                            op0=ALU.mult, op1=ALU.add), reads=[rb], writes=[rb])
        op("act", lambda e: e.activation(out=tmp_ap, in_=tmp_ap, func=AF.Sqrt), reads=[rb], writes=[rb])
        op("dve", lambda e: e.reciprocal(out=out_ap, in_=tmp_ap), reads=[rb], writes=[rb])

    def ffn_stage(si, layer, xsrc, xsrc_bufs, add_pos, xdst):
        mk = kb.mark()
        hT = kb.sb("hT", [128, 16, TB], BF16)
        hTb = [[Buf() for _ in range(4)] for _ in range(16)]
        aT = kb.sb("aT", [128, 44, TB], BF16)
        aTb = bufs(44, "aT")
        wg_s = [kb.sb("wg%d" % i, [128, 16, 256], BF16) for i in range(2)]
        wu_s = [kb.sb("wu%d" % i, [128, 16, 256], BF16) for i in range(2)]
        wgb, wub = bufs(2, "wg"), bufs(2, "wu")
        wd_s = [kb.sb("wd%d" % i, [128, 2, 1024], BF16) for i in range(2)]
        wdb = bufs(2, "wd")
        xin = [kb.sb("xin%d" % i, [128, D], F32) for i in range(4)]
        xinb = bufs(4, "xin")
        xn = kb.sb("xn", [128, D], F32)
        xnb = Buf("xn")
        junk = kb.sb("junk", [128, D], BF16)
        junkb = Buf("junk")
        yacc = kb.sb("yacc", [128, 4, 1024], F32)
        yaccb = bufs(4, "yacc")
        tmp = [kb.sb("tmp%d" % i, [128, 512], F32) for i in range(2)]
        tmpb = bufs(2, "tmp")
        sg = [kb.sb("sg%d" % i, [128, 512], F32) for i in range(2)]
        sgb = bufs(2, "sg")
        Gp = kb.sb("Gp", [128, D], F32)
        gpost = kb.sb("gpost", [128, D], F32)
        Gpb = Buf("Gp")
        st = kb.sb("st", [128, 32], F32)
        stb = bufs(8, "st")
        gi = 3 * si + 2
        dma("sp", Gp[:], modrow[0, gi * D:(gi + 1) * D].partition_broadcast(128), reads=[modb], writes=[Gpb])
        dma("sp", gpost[:], norm_post[si, :].partition_broadcast(128), writes=[Gpb])
        gsc = 1.0 if si == 1 else 0.5
        op("dve", lambda e: e.scalar_tensor_tensor(out=Gp[:], in0=Gp[:], scalar=gsc, in1=gpost[:],
                                                   op0=ALU.mult, op1=ALU.mult), reads=[Gpb], writes=[Gpb])
        evac_i = [0]
        for tb in range(T // TB):
            if FFN_PARTS < 0.15:
                break
            for m in range(4):
                t = tb * 4 + m
                dma("sp", xin[m][:], xsrc[t * 128:(t + 1) * 128, :], reads=[xsrc_bufs[t]], writes=[xinb[m]])
                if add_pos:
                    dma("sp", xn[:], pos_in[t * 128:(t + 1) * 128, :], writes=[xnb])
                    op("dve", lambda e, m=m: e.tensor_tensor(out=xin[m][:], in0=xin[m][:], in1=xn[:], op=ALU.add),
                       reads=[xnb, xinb[m]], writes=[xinb[m]])
                sb_ = stb[m]
                c0 = m * 4
                op("act", lambda e, m=m, c0=c0: e.activation(out=junk[:], in_=xin[m][:], func=AF.Square,
                                                            accum_out=st[:, c0:c0 + 1]),
                   reads=[xinb[m]], writes=[junkb, sb_])
                if FFN_PARTS < 0.25:
                    continue
                rstd_from_ss(st[:, c0:c0 + 1], st[:, c0 + 1:c0 + 2], st[:, c0 + 2:c0 + 3], sb_, D)
                op("dve", lambda e, m=m, c0=c0: e.tensor_scalar(out=xn[:], in0=xin[m][:], scalar1=st[:, c0 + 1:c0 + 2],
                                                               scalar2=None, op0=ALU.mult),
                   reads=[xinb[m], sb_], writes=[xnb])
                for q in range(4):
                    if FFN_PARTS < 0.35:
                        break
                    bk = q
                    for kk in range(4):
                        kc = q * 4 + kk
                        op("pe", lambda e, kc=kc, kk=kk, bk=bk: e.transpose(
                            out=PS[bk][:, kk * 128:(kk + 1) * 128], in_=xn[:, kc * 128:(kc + 1) * 128], identity=ident[:]),
                           reads=[xnb, identb], writes=[PSB[bk]], inc=(kk == 3))
                    for kk in range(4):
                        if FFN_PARTS < 0.45:
                            break
                        kc = q * 4 + kk
                        src = PS[bk][:, kk * 128:(kk + 1) * 128]
                        dst = hT[:, kc, m * 128:(m + 1) * 128]
                        a_ap = Acol[:, si * 16 + kc:si * 16 + kc + 1]
                        b_ap = Bcol(si, kc)
                        if evac_i[0] % 2 == 0 and EVAC_ACT:
                            op("act", lambda e, src=src, dst=dst, a_ap=a_ap, b_ap=b_ap: e.activation(
                                out=dst, in_=src, func=AF.Identity, scale=a_ap, bias=b_ap),
                               reads=[PSB[bk], colb], writes=[hTb[kc][m]])
                        else:
                            op("dve", lambda e, src=src, dst=dst, a_ap=a_ap, b_ap=b_ap: e.tensor_scalar(
                                out=dst, in0=src, scalar1=a_ap, scalar2=b_ap, op0=ALU.mult, op1=ALU.add),
                               reads=[PSB[bk], colb], writes=[hTb[kc][m]])
                        evac_i[0] += 1
            if FFN_PARTS < 2:
                break
            for jp in range(22):
                s = jp % 2
                dma("pool", wg_s[s][:], ffn_wg[layer, :, jp * 256:(jp + 1) * 256].rearrange("(kc p) n -> p kc n", p=128),
                    writes=[wgb[s]])
                dma("pool", wu_s[s][:], ffn_wu[layer, :, jp * 256:(jp + 1) * 256].rearrange("(kc p) n -> p kc n", p=128),
                    writes=[wub[s]])
                for jj in range(2):
                    j = jp * 2 + jj
                    gb, ub = 4 + (j % 2), 6 + (j % 2)
                    for (wt, wb, bk) in ((wg_s[s], wgb[s], gb), (wu_s[s], wub[s], ub)):
                        for kc in range(16):
                            op("pe", lambda e, wt=wt, bk=bk, kc=kc, jj=jj: e.matmul(
                                PS[bk][:, :], lhsT=wt[:, kc, jj * 128:(jj + 1) * 128], rhs=hT[:, kc, :],
                                start=(kc == 0), stop=(kc == 15)),
                               reads=[wb] + hTb[kc], writes=[PSB[bk]], inc=(kc == 15))
                    ss_ = j % 2
                    op("act", lambda e, ss_=ss_, gb=gb: e.activation(out=sg[ss_][:], in_=PS[gb][:, :], func=AF.Silu),
                       reads=[PSB[gb]], writes=[sgb[ss_]])
                    op("dve", lambda e, ss_=ss_, ub=ub, j=j: e.tensor_tensor(out=aT[:, j, :], in0=sg[ss_][:], in1=PS[ub][:, :],
                                                                          op=ALU.mult),
                       reads=[sgb[ss_], PSB[ub]], writes=[aTb[j]])
            if FFN_PARTS < 3:
                break
            for half in range(2):
                for jp in range(22):
                    s = jp % 2
                    dma("pool", wd_s[s][:],
                        ffn_wd[layer, jp * 256:(jp + 1) * 256, half * 1024:(half + 1) * 1024].rearrange("(jj p) n -> p jj n", p=128),
                        writes=[wdb[s]])
                    for jj in range(2):
                        j = jp * 2 + jj
                        for m in range(4):
                            for nb in range(2):
                                bk = m * 2 + nb
                                op("pe", lambda e, j=j, jj=jj, m=m, nb=nb, bk=bk, s=s: e.matmul(
                                    PS[bk][:, :], lhsT=aT[:, j, m * 128:(m + 1) * 128],
                                    rhs=wd_s[s][:, jj, nb * 512:(nb + 1) * 512], start=(j == 0), stop=(j == 43)),
                                   reads=[aTb[j], wdb[s]], writes=[PSB[bk]], inc=(j == 43 or (jj == 1 and m == 3 and nb == 1)))
                if half == 0:
                    for m in range(4):
                        for nb in range(2):
                            bk = m * 2 + nb
                            eng = "act" if nb == 0 else "dve"
                            if eng == "act":
                                op("act", lambda e, m=m, nb=nb, bk=bk: e.activation(
                                    out=yacc[:, m, nb * 512:(nb + 1) * 512], in_=PS[bk][:, :], func=AF.Identity),
                                   reads=[PSB[bk]], writes=[yaccb[m]])
                            else:
                                op("dve", lambda e, m=m, nb=nb, bk=bk: e.tensor_copy(
                                    out=yacc[:, m, nb * 512:(nb + 1) * 512], in_=PS[bk][:, :]),
                                   reads=[PSB[bk]], writes=[yaccb[m]])
            if FFN_PARTS < 4:
                break
            for m in range(4):
                t = tb * 4 + m
                sb_ = stb[4 + m]
                c0 = 16 + m * 4
                op("act", lambda e, m=m, c0=c0: e.activation(out=junk[:, 0:1024], in_=yacc[:, m, :], func=AF.Square,
                                                            accum_out=st[:, c0:c0 + 1]),
                   reads=[yaccb[m]], writes=[junkb, sb_])
                for nb in range(2):
                    bk = m * 2 + nb
                    op("act", lambda e, nb=nb, bk=bk, c0=c0: e.activation(out=junk[:, 0:512], in_=PS[bk][:, :], func=AF.Square,
                                                                        accum_out=st[:, c0 + 1 + nb:c0 + 2 + nb]),
                       reads=[PSB[bk]], writes=[junkb, sb_])
                op("dve", lambda e, c0=c0: e.tensor_tensor(out=st[:, c0:c0 + 1], in0=st[:, c0:c0 + 1], in1=st[:, c0 + 1:c0 + 2],
                                                          op=ALU.add), reads=[sb_], writes=[sb_])
                op("dve", lambda e, c0=c0: e.tensor_tensor(out=st[:, c0:c0 + 1], in0=st[:, c0:c0 + 1], in1=st[:, c0 + 2:c0 + 3],
                                                          op=ALU.add), reads=[sb_], writes=[sb_])
                rstd_from_ss(st[:, c0:c0 + 1], st[:, c0 + 3:c0 + 4], st[:, c0 + 1:c0 + 2], sb_, D)
                rs = st[:, c0 + 3:c0 + 4]
                for q in range(4):
                    ts_ = q % 2
                    cols = slice(q * 512, (q + 1) * 512)
                    if q < 2:
                        src, srcb = yacc[:, m, cols], yaccb[m]
                    else:
                        bk = m * 2 + (q - 2)
                        src, srcb = PS[bk][:, :], PSB[bk]
                    op("dve", lambda e, src=src, ts_=ts_, cols=cols: e.scalar_tensor_tensor(
                        out=tmp[ts_][:], in0=src, scalar=rs, in1=Gp[:, cols], op0=ALU.mult, op1=ALU.mult),
                       reads=[srcb, sb_, Gpb], writes=[tmpb[ts_]])
                    op("dve", lambda e, ts_=ts_, cols=cols, m=m: e.tensor_tensor(
                        out=xin[m][:, cols], in0=tmp[ts_][:], in1=xin[m][:, cols], op=ALU.add),
                       reads=[tmpb[ts_], xinb[m]], writes=[xinb[m]])
                dma("sp", xdst[t * 128:(t + 1) * 128, :], xin[m][:], reads=[xinb[m]], writes=[ybuf[t]])
        kb.release(mk)

    def mixer_stage():
        si = 1
        mk = kb.mark()
        hTa = kb.sb("hTa", [128, 16, T], BF16)
        hTab = [[Buf() for _ in range(NT)] for _ in range(16)]
        amk = kb.mark()
        xin = [kb.sb("mxin%d" % i, [128, D], F32) for i in range(2)]
        xinb = bufs(2, "mxin")
        xn = kb.sb("mxn", [128, D], F32)
        xnb = Buf("mxn")
        junk = kb.sb("mjunk", [128, D], BF16)
        junkb = Buf("mjunk")
        st = kb.sb("mst", [128, 64], F32)
        stb = bufs(16, "mst")
        for t in range(NT):
            s = t % 2
            dma("sp", xin[s][:], xres[t * 128:(t + 1) * 128, :], reads=[ybuf[t]], writes=[xinb[s]])
            sb_ = stb[t]
            c0 = t * 4
            op("act", lambda e: e.activation(out=junk[:], in_=xin[s][:], func=AF.Square, accum_out=st[:, c0:c0 + 1]),
               reads=[xinb[s]], writes=[junkb, sb_])
            rstd_from_ss(st[:, c0:c0 + 1], st[:, c0 + 1:c0 + 2], st[:, c0 + 2:c0 + 3], sb_, D)
            op("dve", lambda e: e.tensor_scalar(out=xn[:], in0=xin[s][:], scalar1=st[:, c0 + 1:c0 + 2], scalar2=None,
                                                op0=ALU.mult), reads=[xinb[s], sb_], writes=[xnb])
            for q in range(4):
                bk = q
                for kk in range(4):
                    kc = q * 4 + kk
                    op("pe", lambda e: e.transpose(out=PS[bk][:, kk * 128:(kk + 1) * 128], in_=xn[:, kc * 128:(kc + 1) * 128],
                                                   identity=ident[:]), reads=[xnb, identb], writes=[PSB[bk]], inc=(kk == 3))
                for kk in range(4):
                    kc = q * 4 + kk
                    op("dve", lambda e: e.tensor_scalar(out=hTa[:, kc, t * 128:(t + 1) * 128], in0=PS[bk][:, kk * 128:(kk + 1) * 128],
                                                        scalar1=Acol[:, si * 16 + kc:si * 16 + kc + 1], scalar2=Bcol(si, kc),
                                                        op0=ALU.mult, op1=ALU.add),
                       reads=[PSB[bk], colb], writes=[hTab[kc][t]])

        kb.release(amk)
        chk(1)
        wi = [kb.sb("wi%d" % i, [128, 16, 128], BF16) for i in range(2)]
        wib = bufs(2, "wi")
        wi_n = [0]

        def proj_chunk(c0, ncol=128):
            s = wi_n[0] % 2
            wi_n[0] += 1
            dma("pool", wi[s][:, :, 0:ncol], w_in[:, c0:c0 + ncol].rearrange("(kc p) n -> p kc n", p=128), writes=[wib[s]])
            for nb in range(4):
                for kc in range(16):
                    op("pe", lambda e: e.matmul(PS[nb][0:ncol, :], lhsT=wi[s][:, kc, 0:ncol], rhs=hTa[:, kc, nb * 512:(nb + 1) * 512],
                                                start=(kc == 0), stop=(kc == 15)),
                       reads=[wib[s]] + hTab[kc][nb * 4:(nb + 1) * 4], writes=[PSB[nb]], inc=(kc == 15))

        PSlo = PSALL[:, 0:2048]
        PSloB = PSB[0:4]

        op("dve", lambda e: e.memset(hyss[:], 0.0), writes=[hyssb])
        PI = math.pi
        hm = kb.mark()
        ones_h = kb.sb("ones_h", [128, 128], F32)
        hcb = Buf("hconst")
        op("dve", lambda e: e.memset(ones_h[:], 1.0), writes=[hcb])
        hcw = kb.sb("hcw", [128, 3, 24], F32)
        bmask_h = kb.sb("bmask_h", [128, 7], F32)
        hcol = kb.sb("hcol", [128, 64], F32)
        nrep = kb.sb("nrep", [128, 1], F32)
        with nc.allow_non_contiguous_dma(reason="small vector relayout"):
            dma("sp", hcw[:], hy_conv_w.rearrange("j (c p) -> p j c", p=128), writes=[hcb])
        dma("sp", bmask_h[:], bmask_in[0, :].partition_broadcast(128), writes=[hcb])
        dma("sp", hcol[:], hcol_in[:, :], writes=[hcb])
        dma("sp", nrep[:], nrep_in[:, :], writes=[hcb])
        t7h = kb.sb("t7h", [128, 16], F32)
        t7hb = Buf("t7h")

        invm = kb.mark()
        invn = kb.sb("invn", [128, 2048], F32)
        invb = Buf("invn")
        h1m = kb.mark()
        featsT = kb.sb("featsT", [33, T], F32)
        w1s = kb.sb("w1s", [33, 64], F32)
        w2s = kb.sb("w2s", [64, 64], F32)
        w3s = kb.sb("w3s", [64, 4096], F32)
        bcs = kb.sb("bcs", [64, 2], F32)
        fb = Buf("filt")
        dma("sp", featsT[:], feats_in[:, :], writes=[fb])
        dma("sp", w1s[:], hy_f_w1[:, :], writes=[fb])
        dma("sp", w2s[:], hy_f_w2[:, :], writes=[fb])
        dma("sp", w3s[:], hy_f_w3[:, :], writes=[fb])
        with nc.allow_non_contiguous_dma(reason="small vector relayout"):
            dma("sp", bcs[:, 0:1], hy_f_b1.rearrange("o p -> p o"), writes=[fb])
            dma("sp", bcs[:, 1:2], hy_f_b2.rearrange("o p -> p o"), writes=[fb])
        h1T = kb.sb("h1T", [64, T], F32)
        h2T = kb.sb("h2T", [64, T], F32)
        h1b, h2b = Buf("h1T"), Buf("h2T")
        rw1 = kb.sb("rw1", [64, T], F32)
        rw2 = kb.sb("rw2", [64, T], F32)
        rwb = Buf("rw")
        for (src, srcb, wts, kdim, dst, dstb, bi) in ((featsT, fb, w1s, 33, h1T, h1b, 0), (h1T, h1b, w2s, 64, h2T, h2b, 1)):
            for nb in range(4):
                sl = slice(nb * 512, (nb + 1) * 512)
                op("pe", lambda e: e.matmul(PS[nb][0:64, :], lhsT=wts[0:kdim, :], rhs=src[0:kdim, sl], start=True, stop=True),
                   reads=[srcb, fb], writes=[PSB[nb]])
                op("dve", lambda e: e.tensor_scalar(out=dst[:, sl], in0=PS[nb][0:64, :], scalar1=bcs[:, bi:bi + 1], scalar2=None, op0=ALU.add),
                   reads=[PSB[nb], fb], writes=[dstb])
            op("dve", lambda e: e.tensor_scalar(out=rw1[:], in0=dst[:], scalar1=PI, scalar2=-2 * PI, op0=ALU.is_gt, op1=ALU.mult),
               reads=[dstb], writes=[rwb])
            op("dve", lambda e: e.tensor_scalar(out=rw2[:], in0=dst[:], scalar1=-PI, scalar2=2 * PI, op0=ALU.is_lt, op1=ALU.mult),
               reads=[dstb], writes=[rwb])
            op("dve", lambda e: e.tensor_tensor(out=dst[:], in0=dst[:], in1=rw1[:], op=ALU.add), reads=[dstb, rwb], writes=[dstb])
            op("dve", lambda e: e.tensor_tensor(out=dst[:], in0=dst[:], in1=rw2[:], op=ALU.add), reads=[dstb, rwb], writes=[dstb])
            op("dve", lambda e: e.tensor_scalar(out=dst[:], in0=dst[:], scalar1=-PI, scalar2=PI, op0=ALU.max, op1=ALU.min),
               reads=[dstb], writes=[dstb])
            op("act", lambda e: e.activation(out=dst[:], in_=dst[:], func=AF.Sin), reads=[dstb], writes=[dstb])
        absdec = kb.sb("absdec", [128, 2048], F32)
        adb = Buf("absdec")
        dma("sp", absdec[:], hy_decay.rearrange("o c -> (o c)").partition_broadcast(128), writes=[adb])
        op("act", lambda e: e.activation(out=absdec[:], in_=absdec[:], func=AF.Abs), reads=[adb], writes=[adb])
        absacc = kb.sb("absacc", [128, 2048], F32)
        aab = bufs(4, "absacc")
        op("dve", lambda e: e.memset(absacc[:], 0.0), writes=aab)
        wn = kb.sb("wn", [128, 2048], F32)
        wnb = Buf("wn")
        tA = [kb.sb("tA%d" % i, [128, 512], F32) for i in range(2)]
        tB = [kb.sb("tB%d" % i, [128, 512], F32) for i in range(2)]
        tAb, tBb = bufs(2, "tA"), bufs(2, "tB")
        tS = [kb.sb("tS%d" % i, [128, 2048], BF16) for i in range(2)]
        tD = [kb.sb("tD%d" % i, [128, 2048], BF16) for i in range(2)]
        tSb, tDb = bufs(2, "tS"), bufs(2, "tD")
        for lt in range(16):
            op("dve", lambda e: e.tensor_scalar(out=wn[:], in0=absdec[:], scalar1=hcol[:, lt:lt + 1], scalar2=None, op0=ALU.mult),
               reads=[adb, hcb], writes=[wnb])
            op("act", lambda e: e.activation(out=wn[:], in_=wn[:], func=AF.Exp), reads=[wnb], writes=[wnb])
            op("dve", lambda e: e.tensor_scalar(out=wn[:], in0=wn[:], scalar1=0.05, scalar2=None, op0=ALU.add), reads=[wnb], writes=[wnb])
            s2 = lt % 2
            for cb4 in range(4):
                sl = slice(cb4 * 512, (cb4 + 1) * 512)
                k_ = cb4 % 2
                pa, pb = 4 + 2 * k_, 5 + 2 * k_
                op("pe", lambda e: e.matmul(PS[pa][:, :], lhsT=h2T[:, lt * 128:(lt + 1) * 128], rhs=w3s[:, cb4 * 512:(cb4 + 1) * 512],
                                            start=True, stop=True), reads=[h2b, fb], writes=[PSB[pa]])
                op("pe", lambda e: e.matmul(PS[pb][:, :], lhsT=h2T[:, lt * 128:(lt + 1) * 128], rhs=w3s[:, 2048 + cb4 * 512:2048 + (cb4 + 1) * 512],
                                            start=True, stop=True), reads=[h2b, fb], writes=[PSB[pb]])
                op("dve", lambda e: e.tensor_tensor(out=tA[k_][:], in0=PS[pa][:, :], in1=wn[:, sl], op=ALU.mult),
                   reads=[PSB[pa], wnb], writes=[tAb[k_]])
                op("dve", lambda e: e.scalar_tensor_tensor(out=tB[k_][:], in0=PS[pb][:, :], scalar=hcol[:, 16 + lt:17 + lt], in1=wn[:, sl],
                                                           op0=ALU.mult, op1=ALU.mult), reads=[PSB[pb], wnb, hcb], writes=[tBb[k_]])
                op("pool", lambda e: e.tensor_tensor(out=tS[s2][:, sl], in0=tA[k_][:], in1=tB[k_][:], op=ALU.add),
                   reads=[tAb[k_], tBb[k_]], writes=[tSb[s2]])
                op("pool", lambda e: e.tensor_tensor(out=tD[s2][:, sl], in0=tA[k_][:], in1=tB[k_][:], op=ALU.subtract),
                   reads=[tAb[k_], tBb[k_]], writes=[tDb[s2]])
                op("act", lambda e: e.activation(out=tA[k_][:], in_=tA[k_][:], func=AF.Abs), reads=[tAb[k_]], writes=[tAb[k_]])
                op("act", lambda e: e.activation(out=tB[k_][:], in_=tB[k_][:], func=AF.Abs), reads=[tBb[k_]], writes=[tBb[k_]])
                op("pool", lambda e: e.tensor_tensor(out=absacc[:, sl], in0=absacc[:, sl], in1=tA[k_][:], op=ALU.add),
                   reads=[tAb[k_], aab[cb4]], writes=[aab[cb4]])
                op("pool", lambda e: e.tensor_tensor(out=absacc[:, sl], in0=absacc[:, sl], in1=tB[k_][:], op=ALU.add),
                   reads=[tBb[k_], aab[cb4]], writes=[aab[cb4]])
            dma("sp", tapsS[lt * 128:(lt + 1) * 128, :], tS[s2][:], reads=[tSb[s2]], writes=[tapsb])
            dma("sp", tapsD[lt * 128:(lt + 1) * 128, :], tD[s2][:], reads=[tDb[s2]], writes=[tapsb])
        for cb4 in range(4):
            sl = slice(cb4 * 512, (cb4 + 1) * 512)
            op("pe", lambda e: e.matmul(PS[cb4][:, :], lhsT=ones_h[:], rhs=absacc[:, sl], start=True, stop=True),
               reads=[aab[cb4], hcb], writes=[PSB[cb4]])
            op("dve", lambda e: e.reciprocal(out=invn[:, sl], in_=PS[cb4][:, :]), reads=[PSB[cb4]], writes=[invb])
        op("dve", lambda e: e.tensor_scalar(out=invn[:], in0=invn[:], scalar1=nrep[:, 0:1], scalar2=None, op0=ALU.mult),
           reads=[invb, hcb], writes=[invb])
        kb.release(h1m)
        h1m = kb.mark()
        tSk = kb.sb("tSk", [128, 16, 512], BF16)
        tDk = kb.sb("tDk", [128, 16, 512], BF16)
        tkb = Buf("tk")
        dfa = [kb.sb("dfa%d" % i, [128, 16, 128], BF16) for i in range(2)]
        dfb = [kb.sb("dfb%d" % i, [128, 16, 128], BF16) for i in range(2)]
        dfab, dfbb = bufs(2, "dfa"), bufs(2, "dfb")
        sp_t = [kb.sb("sp_t%d" % i, [128, 512], F32) for i in range(8)]
        sp_b = bufs(8, "sp_t")
        for cb4 in range(4):
            sl = slice(cb4 * 512, (cb4 + 1) * 512)
            dma("sp", tSk[:], tapsS[:, sl].rearrange("(lt p) n -> p lt n", p=128), reads=[tapsb], writes=[tkb])
            dma("sp", tDk[:], tapsD[:, sl].rearrange("(lt p) n -> p lt n", p=128), reads=[tapsb], writes=[tkb])
            for ft in range(16):
                s2 = ft % 2
                dma("sp", dfa[s2][:], CF_t[ft], writes=[dfab[s2]])
                dma("sp", dfb[s2][:], SF_t[ft], writes=[dfbb[s2]])
                b0 = 4 * s2
                for lt in range(16):
                    op("pe", lambda e: e.matmul(PS[b0][:, :], lhsT=dfa[s2][:, lt, :], rhs=tSk[:, lt, :], start=(lt == 0), stop=(lt == 15)),
                       reads=[dfab[s2], tkb], writes=[PSB[b0]], inc=(lt == 15))
                for lt in range(16):
                    op("pe", lambda e: e.matmul(PS[b0 + 1][:, :], lhsT=dfb[s2][:, lt, :], rhs=tDk[:, lt, :], start=(lt == 0), stop=(lt == 15)),
                       reads=[dfbb[s2], tkb], writes=[PSB[b0 + 1]], inc=(lt == 15))
                if ft % 2 == 0:
                    for lt in range(16):
                        op("pe", lambda e: e.matmul(PS[b0 + 2][:, :], lhsT=dfb[s2][:, lt, :], rhs=tSk[:, lt, :], start=(lt == 0), stop=(lt == 15)),
                           reads=[dfbb[s2], tkb], writes=[PSB[b0 + 2]], inc=(lt == 15))
                o4 = 4 * s2
                sre, sim, nq, srb = sp_t[o4], sp_t[o4 + 1], sp_t[o4 + 2], sp_t[o4 + 3]
                op("dve", lambda e: e.tensor_tensor(out=sre[:], in0=PS[b0][:, :], in1=invn[:, sl], op=ALU.mult),
                   reads=[PSB[b0], invb], writes=[sp_b[o4]])
                op("dve", lambda e: e.scalar_tensor_tensor(out=sim[:], in0=PS[b0 + 1][:, :], scalar=hcol[:, 48 + ft:49 + ft], in1=invn[:, sl],
                                                           op0=ALU.mult, op1=ALU.mult), reads=[PSB[b0 + 1], invb, hcb], writes=[sp_b[o4 + 1]])
                dma("sp", spec_re[ft * 128:(ft + 1) * 128, sl], sre[:], reads=[sp_b[o4]], writes=[specb])
                dma("sp", spec_im[ft * 128:(ft + 1) * 128, sl], sim[:], reads=[sp_b[o4 + 1]], writes=[specb])
                if ft % 2 == 0:
                    op("dve", lambda e: e.tensor_tensor(out=nq[:], in0=PS[b0 + 2][:, :], in1=invn[:, sl], op=ALU.mult),
                       reads=[PSB[b0 + 2], invb], writes=[sp_b[o4 + 2]])
                    op("pool", lambda e: e.tensor_tensor(out=nq[:], in0=nq[:], in1=sre[:], op=ALU.subtract),
                       reads=[sp_b[o4 + 2], sp_b[o4]], writes=[sp_b[o4 + 2]])
                    op("dve", lambda e: e.scalar_tensor_tensor(out=srb[:], in0=nq[:], scalar=hcol[:, 32 + ft:33 + ft], in1=sre[:],
                                                               op0=ALU.mult, op1=ALU.add), reads=[sp_b[o4 + 2], sp_b[o4], hcb], writes=[sp_b[o4 + 3]])
                    dma("sp", spec_rb[ft * 128:(ft + 1) * 128, sl], srb[:], reads=[sp_b[o4 + 3]], writes=[specb])
                else:
                    dma("sp", spec_rb[ft * 128:(ft + 1) * 128, sl], sre[:], reads=[sp_b[o4]], writes=[specb])
        kb.release(h1m)
        kb.release(invm)
        chk(20)

        CW = 256
        u32 = kb.sb("u32", [128, 16, CW], F32)
        ubf = kb.sb("ubf", [128, 16, CW], BF16)
        x1t = kb.sb("x1t", [128, 16, CW], F32)
        x2t = kb.sb("x2t", [128, 16, CW], F32)
        u32b, ubfb, x1tb, x2tb = Buf("u32"), Buf("ubf"), Buf("x1t"), Buf("x2t")
        sp3 = [kb.sb("sp3_%d" % i, [128, 3, CW], F32) for i in range(2)]
        sp3b = bufs(2, "sp3")
        Yre = kb.sb("Yre", [128, 16, CW], BF16)
        Yim = kb.sb("Yim", [128, 16, CW], BF16)
        Yb = Buf("Yf")
        dfa = [kb.sb("dga%d" % i, [128, 16, 128], BF16) for i in range(3)]
        dfb = [kb.sb("dgb%d" % i, [128, 16, 128], BF16) for i in range(3)]
        dfab, dfbb = bufs(3, "dga"), bufs(3, "dgb")
        tq = [kb.sb("tq%d" % i, [128, CW], F32) for i in range(8)]
        tqb = bufs(8, "tq")
        dbt = kb.sb("dbt", [128, CW], F32)
        hnt = kb.sb("hnt", [128, CW], F32)
        dbb = Buf("dbt")
        ztok = kb.sb("ztok", [128, 16, CW], F32)
        ztb_ = Buf("ztok")
        fmt = ztok[:, 0:8, :].rearrange("p a b -> p (a b)")
        fmtb = ztb_
        yoh = kb.sb("yoh", [128, T], BF16)
        yohb = Buf("yoh")
        ssq = kb.sb("ssq", [128, 2], F32)
        ssqb = Buf("ssq")
        jk = kb.sb("jk", [128, CW], BF16)
        jkb = Buf("jk")

        def conv3(dst, dstb, ci):
            w0, w1, w2 = (hcw[:, j, ci:ci + 1] for j in range(3))
            op("dve", lambda e: e.tensor_scalar(out=dst[:], in0=PSlo, scalar1=w1, scalar2=None, op0=ALU.mult),
               reads=PSloB + [hcb], writes=[dstb])
            op("dve", lambda e: e.scalar_tensor_tensor(out=dst[:, 1:T], in0=PSALL[:, 0:T - 1], scalar=w0, in1=dst[:, 1:T],
                                                       op0=ALU.mult, op1=ALU.add), reads=PSloB + [hcb], writes=[dstb])
            op("dve", lambda e: e.scalar_tensor_tensor(out=dst[:, 0:T - 1], in0=PSALL[:, 1:T], scalar=w2, in1=dst[:, 0:T - 1],
                                                       op0=ALU.mult, op1=ALU.add), reads=PSloB + [hcb], writes=[dstb])
            d3 = dst.rearrange("p (s t) -> p s t", t=256)
            p3 = PSlo.rearrange("p (s t) -> p s t", t=256)
            op("dve", lambda e: e.scalar_tensor_tensor(out=t7h[:, 0:7], in0=p3[:, 0:7, 255], scalar=w0, in1=bmask_h[:], op0=ALU.mult,
                                                       op1=ALU.mult), reads=PSloB + [hcb], writes=[t7hb])
            op("dve", lambda e: e.tensor_tensor(out=d3[:, 1:8, 0], in0=d3[:, 1:8, 0], in1=t7h[:, 0:7], op=ALU.subtract),
               reads=[t7hb], writes=[dstb])
            op("dve", lambda e: e.scalar_tensor_tensor(out=t7h[:, 8:15], in0=p3[:, 1:8, 0], scalar=w2, in1=bmask_h[:], op0=ALU.mult,
                                                       op1=ALU.mult), reads=PSloB + [hcb], writes=[t7hb])
            op("dve", lambda e: e.tensor_tensor(out=d3[:, 0:7, 255], in0=d3[:, 0:7, 255], in1=t7h[:, 8:15], op=ALU.subtract),
               reads=[t7hb], writes=[dstb])

        for cgp in range(0 if not SKIP_H2 else 4, 4):
            for g2 in range(2):
                cg = 2 * cgp + g2
                gs = slice(g2 * 128, (g2 + 1) * 128)
                for (which, dst, dstb) in ((0, u32, u32b), (1, x1t, x1tb), (2, x2t, x2tb)):
                    proj_chunk(which * 1024 + cg * 128)
                    conv3(fmt, fmtb, which * 8 + cg)
                    for q in range(4):
                        pbk = 4 + q
                        for kk in range(4):
                            tt = q * 4 + kk
                            op("pe", lambda e: e.transpose(out=PS[pbk][:, kk * 128:(kk + 1) * 128], in_=fmt[:, tt * 128:(tt + 1) * 128],
                                                           identity=ident[:]), reads=[fmtb, identb], writes=[PSB[pbk]], inc=(kk == 3))
                        src3 = PS[pbk].rearrange("p (a b) -> p a b", b=128)
                        op("act", lambda e: e.activation(out=dst[:, q * 4:(q + 1) * 4, gs], in_=src3, func=AF.Copy),
                           reads=[PSB[pbk]], writes=[dstb])
                        if which == 0:
                            op("act", lambda e: e.activation(out=ubf[:, q * 4:(q + 1) * 4, gs], in_=src3, func=AF.Copy),
                               reads=[PSB[pbk]], writes=[ubfb])
            csl0 = slice(cgp * CW, (cgp + 1) * CW)
            dma("sp", hnt[:], hy_out_norm[0, csl0].partition_broadcast(128), writes=[dbb])
            for o in range(2):
                csl = slice(o * 1024 + cgp * CW, o * 1024 + (cgp + 1) * CW)
                dma("sp", dbt[:], hy_bias[o, csl0].partition_broadcast(128), writes=[dbb])
                xg, xgb = (x1t, x1tb) if o == 0 else (x2t, x2tb)
                for ft in range(16):
                    s2 = ft % 2
                    s3 = ft % 3
                    dma("sp", dfa[s3][:], CF_t[ft], writes=[dfab[s3]])
                    dma("sp", dfb[s3][:], SF_t[ft], writes=[dfbb[s3]])
                    fsl = slice(ft * 128, (ft + 1) * 128)
                    dma("sp", sp3[s2][:, 0, :], spec_re[fsl, csl], reads=[specb], writes=[sp3b[s2]])
                    dma("sp", sp3[s2][:, 1, :], spec_im[fsl, csl], reads=[specb], writes=[sp3b[s2]])
                    dma("sp", sp3[s2][:, 2, :], spec_rb[fsl, csl], reads=[specb], writes=[sp3b[s2]])
                    pk = s2
                    for lt in range(16):
                        op("pe", lambda e: e.matmul(PS[pk][:, 0:CW], lhsT=dfa[s3][:, lt, :], rhs=ubf[:, lt, :], start=(lt == 0), stop=(lt == 15)),
                           reads=[dfab[s3], ubfb], writes=[PSB[pk]], inc=(lt == 15))
                    for lt in range(16):
                        op("pe", lambda e: e.matmul(PS[2 + pk][:, 0:CW], lhsT=dfb[s3][:, lt, :], rhs=ubf[:, lt, :], start=(lt == 0), stop=(lt == 15)),
                           reads=[dfbb[s3], ubfb], writes=[PSB[2 + pk]], inc=(lt == 15))
                    q4 = 4 * s2
                    ure, uim = PS[pk][:, 0:CW], PS[2 + pk][:, 0:CW]
                    S_re, S_im, S_rb = sp3[s2][:, 0, :], sp3[s2][:, 1, :], sp3[s2][:, 2, :]
                    op("dve", lambda e: e.tensor_tensor(out=tq[q4][:], in0=ure, in1=S_re, op=ALU.mult), reads=[PSB[pk], sp3b[s2]], writes=[tqb[q4]])
                    op("dve", lambda e: e.tensor_tensor(out=tq[q4 + 1][:], in0=uim, in1=S_im, op=ALU.mult), reads=[PSB[2 + pk], sp3b[s2]], writes=[tqb[q4 + 1]])
                    op("dve", lambda e: e.tensor_tensor(out=tq[q4 + 2][:], in0=ure, in1=S_im, op=ALU.mult), reads=[PSB[pk], sp3b[s2]], writes=[tqb[q4 + 2]])
                    op("dve", lambda e: e.tensor_tensor(out=tq[q4 + 3][:], in0=uim, in1=S_rb, op=ALU.mult), reads=[PSB[2 + pk], sp3b[s2]], writes=[tqb[q4 + 3]])
                    op("pool", lambda e: e.tensor_tensor(out=Yre[:, ft, :], in0=tq[q4][:], in1=tq[q4 + 1][:], op=ALU.subtract),
                       reads=[tqb[q4], tqb[q4 + 1]], writes=[Yb])
                    op("pool", lambda e: e.tensor_tensor(out=Yim[:, ft, :], in0=tq[q4 + 2][:], in1=tq[q4 + 3][:], op=ALU.add),
                       reads=[tqb[q4 + 2], tqb[q4 + 3]], writes=[Yb])
                for tt in range(16):
                    s2 = tt % 2
                    s3 = (tt + 1) % 3
                    dma("sp", dfa[s3][:], GC_t[tt], writes=[dfab[s3]])
                    dma("sp", dfb[s3][:], GS_t[tt], writes=[dfbb[s3]])
                    pk = 4 + s2
                    for ft in range(16):
                        op("pe", lambda e: e.matmul(PS[pk][:, 0:CW], lhsT=dfa[s3][:, ft, :], rhs=Yre[:, ft, :], start=(ft == 0), stop=False),
                           reads=[dfab[s3], Yb], writes=[PSB[pk]], inc=False)
                        op("pe", lambda e: e.matmul(PS[pk][:, 0:CW], lhsT=dfb[s3][:, ft, :], rhs=Yim[:, ft, :], start=False, stop=(ft == 15)),
                           reads=[dfbb[s3], Yb], writes=[PSB[pk]], inc=(ft == 15))
                    q4 = 4 * s2
                    op("dve", lambda e: e.tensor_tensor(out=tq[q4][:], in0=u32[:, tt, :], in1=dbt[:], op=ALU.mult), reads=[u32b, dbb], writes=[tqb[q4]])
                    op("dve", lambda e: e.tensor_tensor(out=tq[q4][:], in0=tq[q4][:], in1=PS[pk][:, 0:CW], op=ALU.add), reads=[PSB[pk], tqb[q4]], writes=[tqb[q4]])
                    if o == 0:
                        op("dve", lambda e: e.tensor_tensor(out=u32[:, tt, :], in0=tq[q4][:], in1=xg[:, tt, :], op=ALU.mult),
                           reads=[tqb[q4], xgb, u32b], writes=[u32b])
                        op("pool", lambda e: e.tensor_copy(out=ubf[:, tt, :], in_=u32[:, tt, :]), reads=[u32b], writes=[ubfb])
                    else:
                        op("dve", lambda e: e.tensor_tensor(out=tq[q4 + 1][:], in0=tq[q4][:], in1=xg[:, tt, :], op=ALU.mult),
                           reads=[tqb[q4], xgb], writes=[tqb[q4 + 1]])
                        op("act", lambda e: e.activation(out=jk[:], in_=tq[q4 + 1][:], func=AF.Square, accum_out=ssq[:, 0:1]),
                           reads=[tqb[q4 + 1]], writes=[jkb, ssqb])
                        op("dve", lambda e: e.tensor_tensor(out=hyss[:, tt:tt + 1], in0=hyss[:, tt:tt + 1], in1=ssq[:, 0:1], op=ALU.add),
                           reads=[ssqb, hyssb], writes=[hyssb])
                        op("pool", lambda e: e.tensor_tensor(out=ztok[:, tt, :], in0=tq[q4 + 1][:], in1=hnt[:], op=ALU.mult),
                           reads=[tqb[q4 + 1], dbb], writes=[ztb_])
            for g2 in range(2):
                gs = slice(g2 * 128, (g2 + 1) * 128)
                for q in range(4):
                    pbk = q
                    for kk in range(4):
                        tt = q * 4 + kk
                        op("pe", lambda e: e.transpose(out=PS[pbk][:, kk * 128:(kk + 1) * 128], in_=ztok[:, tt, gs], identity=ident[:]),
                           reads=[ztb_, identb], writes=[PSB[pbk]], inc=(kk == 3))
                    op("act", lambda e: e.activation(out=yoh[:, q * 512:(q + 1) * 512], in_=PS[pbk][:, :], func=AF.Copy), reads=[PSB[pbk]], writes=[yohb])
                dma("sp", ycatT[2 * cgp + g2], yoh[:], reads=[yohb], writes=[ycb[2 * cgp + g2]])
        kb.release(hm)
        chk(2)
        gm = kb.mark()
        gmask = kb.sb("gmask", [128, 7, 128], F32)
        ones = kb.sb("ones", [128, 128], F32)
        cb = Buf("gconst")
        dma("sp", gmask[:], gmask_in.rearrange("m p f -> p m f"), writes=[cb])
        op("dve", lambda e: e.memset(ones[:], 1.0), writes=[cb])
        keepc = kb.sb("keepc", [128, 8], F32)
        bmask = kb.sb("bmask", [128, 7], F32)
        dma("sp", keepc[:], keep_in[0, :].partition_broadcast(128), writes=[cb])
        dma("sp", bmask[:], bmask_in[0, :].partition_broadcast(128), writes=[cb])
        gcw = kb.sb("gcw", [128, 3, 24], F32)
        hcw = kb.sb("hcw", [128, 3, 24], F32)
        with nc.allow_non_contiguous_dma(reason="small vector relayout"):
            dma("sp", gcw[:], gdn_conv_w.rearrange("j (c p) -> p j c", p=128), writes=[cb])
            dma("sp", hcw[:], hy_conv_w.rearrange("j (c p) -> p j c", p=128), writes=[cb])
        onorm = kb.sb("onorm", [128, 1], F32)
        with nc.allow_non_contiguous_dma(reason="small vector relayout"):
            dma("sp", onorm[:], gdn_o_norm.rearrange("o p -> p o"), writes=[cb])
        chk(3)
        wab = kb.sb("wab", [128, 16, 32], BF16)
        wabb = Buf("wab")
        dma("pool", wab[:], w_in[:, 7168:7200].rearrange("(kc p) n -> p kc n", p=128), writes=[wabb])
        for tt in range(NT):
            for kc in range(16):
                op("pe", lambda e: e.matmul(PS[4][:, tt * 32:(tt + 1) * 32], lhsT=hTa[:, kc, tt * 128:(tt + 1) * 128], rhs=wab[:, kc, :],
                                            start=(kc == 0), stop=(kc == 15)),
                   reads=[wabb, hTab[kc][tt]], writes=[PSB[4]], inc=(kc == 15))
        chk(4)
        W = 256
        gv = kb.sb("gv", [128, 12, W], F32)
        gvb = Buf("gv")
        GX, GG, BETA, NEA, GCUM, TOT, NBETA, BG, KD, DTB, ALOG = range(11)
        ab4 = PS[4].rearrange("p (t d k h) -> p t d k h", t=16, d=2, k=2, h=8)

        def v4(i):
            return gv[:, i, :].rearrange("p (t d h) -> p t d h", t=16, d=2, h=8)
        dma("sp", gv[:, DTB, :], dtb_in[0, :].partition_broadcast(128), writes=[gvb])
        dma("sp", gv[:, ALOG, :], alog_in[0, :].partition_broadcast(128), writes=[gvb])
        op("dve", lambda e: e.tensor_tensor(out=v4(GX), in0=ab4[:, :, :, 0, :], in1=v4(DTB), op=ALU.add), reads=[PSB[4], gvb], writes=[gvb])
        op("act", lambda e: e.activation(out=gv[:, GX, :], in_=gv[:, GX, :], func=AF.Exp), reads=[gvb], writes=[gvb])
        op("act", lambda e: e.activation(out=gv[:, GX, :], in_=gv[:, GX, :], func=AF.Ln, bias=1.0), reads=[gvb], writes=[gvb])
        op("act", lambda e: e.activation(out=gv[:, NEA, :], in_=gv[:, ALOG, :], func=AF.Exp), reads=[gvb], writes=[gvb])
        op("dve", lambda e: e.scalar_tensor_tensor(out=gv[:, GG, :], in0=gv[:, GX, :], scalar=-1.0, in1=gv[:, NEA, :],
                                                   op0=ALU.mult, op1=ALU.mult), reads=[gvb], writes=[gvb])
        op("act", lambda e: e.activation(out=v4(BETA), in_=ab4[:, :, :, 1, :], func=AF.Sigmoid), reads=[PSB[4]], writes=[gvb])
        chk(5)
        for tt in range(NT):
            for d in range(2):
                cs_ = (tt * 2 + d) * 8
                op("pe", lambda e: e.matmul(PS[5][:, cs_:cs_ + 8], lhsT=gmask[:, 1 if d == 0 else 3, :], rhs=gv[:, GG, cs_:cs_ + 8],
                                            start=True, stop=True), reads=[gvb, cb], writes=[PSB[5]], inc=(tt == NT - 1 and d == 1))
        op("pe", lambda e: e.matmul(PS[6][:, 0:W], lhsT=ones[:], rhs=gv[:, GG, :], start=True, stop=True), reads=[gvb, cb], writes=[PSB[6]])
        op("dve", lambda e: e.tensor_copy(out=gv[:, GCUM, :], in_=PS[5][:, 0:W]), reads=[PSB[5]], writes=[gvb])
        op("dve", lambda e: e.tensor_copy(out=gv[:, TOT, :], in_=PS[6][:, 0:W]), reads=[PSB[6]], writes=[gvb])
        op("dve", lambda e: e.tensor_scalar(out=gv[:, NBETA, :], in0=gv[:, BETA, :], scalar1=-1.0, scalar2=None, op0=ALU.mult),
           reads=[gvb], writes=[gvb])
        op("act", lambda e: e.activation(out=gv[:, BG, :], in_=gv[:, GCUM, :], func=AF.Exp), reads=[gvb], writes=[gvb])
        op("dve", lambda e: e.tensor_tensor(out=gv[:, BG, :], in0=gv[:, BG, :], in1=gv[:, BETA, :], op=ALU.mult), reads=[gvb], writes=[gvb])
        op("dve", lambda e: e.tensor_tensor(out=gv[:, KD, :], in0=gv[:, TOT, :], in1=gv[:, GCUM, :], op=ALU.subtract), reads=[gvb], writes=[gvb])
        op("act", lambda e: e.activation(out=gv[:, KD, :], in_=gv[:, KD, :], func=AF.Exp), reads=[gvb], writes=[gvb])

        chk(6)
        if DEBUG:
            dma("sp", dbg_gv[:, :], gv[:].rearrange("p a b -> p (a b)"), reads=[gvb], writes=[dbgb])

        def col(i, tt, d, h):
            c = (tt * 2 + d) * 8 + h
            return gv[:, i, c:c + 1]

        FM = [kb.sb("fm%d" % i, [128, T], F32) for i in range(6)]
        FMB = bufs(6, "fm")
        QT, KT, ZT, OA, SQ, VT = range(6)
        ktok = kb.sb("ktok", [128, 16, 128], F32)
        vtok = kb.sb("vtok", [128, 16, 128], F32)
        ktokb, vtokb = Buf("ktok"), Buf("vtok")
        tl2 = [kb.sb("tl%d" % i, [128, 38, 128], F32) for i in range(2)]
        tlb2 = [bufs(38, "tl%d_" % i) for i in range(2)]
        OAd = [FM[OA], kb.sb("oab", [128, T], F32)]
        OAdb = [Buf("oaf"), Buf("oab")]
        psl = {b_: bufs(4, "psl%d_" % b_) for b_ in range(8)}
        (DG, ND, MM, EG, ML, MU, NA, NTA, NB_, NTB, RTA, RTB, VB, KBG, UU, WT, ATT, QD, KDE, VN, SS, S2, T1, T2, T3, T4,
         NBD, NTBD, E1, E1T, E2, DD, PP, XX, DD2, DT2, RTF, T5) = range(38)
        yo = kb.sb("yo", [128, T], BF16)
        yob = Buf("yo")
        t7 = kb.sb("t7", [128, 16], F32)
        t7b = Buf("t7")

        def conv_silu(dst, dstb, wtile, ci, do_conv=True, do_silu=True):
            if not do_conv:
                op("act", lambda e: e.activation(out=dst[:], in_=PSlo, func=AF.Silu), reads=PSloB, writes=[dstb])
                return
            w0, w1, w2 = (wtile[:, j, ci:ci + 1] for j in range(3))
            op("dve", lambda e: e.tensor_scalar(out=dst[:], in0=PSlo, scalar1=w1, scalar2=None, op0=ALU.mult),
               reads=PSloB + [cb], writes=[dstb])
            op("dve", lambda e: e.scalar_tensor_tensor(out=dst[:, 1:T], in0=PSALL[:, 0:T - 1], scalar=w0, in1=dst[:, 1:T],
                                                       op0=ALU.mult, op1=ALU.add), reads=PSloB + [cb], writes=[dstb])
            op("dve", lambda e: e.scalar_tensor_tensor(out=dst[:, 0:T - 1], in0=PSALL[:, 1:T], scalar=w2, in1=dst[:, 0:T - 1],
                                                       op0=ALU.mult, op1=ALU.add), reads=PSloB + [cb], writes=[dstb])
            d3 = dst.rearrange("p (s t) -> p s t", t=256)
            p3 = PSlo.rearrange("p (s t) -> p s t", t=256)
            op("dve", lambda e: e.scalar_tensor_tensor(out=t7[:, 0:7], in0=p3[:, 0:7, 255], scalar=w0, in1=bmask[:], op0=ALU.mult,
                                                       op1=ALU.mult), reads=PSloB + [cb], writes=[t7b])
            op("dve", lambda e: e.tensor_tensor(out=d3[:, 1:8, 0], in0=d3[:, 1:8, 0], in1=t7[:, 0:7], op=ALU.subtract),
               reads=[t7b], writes=[dstb])
            op("dve", lambda e: e.scalar_tensor_tensor(out=t7[:, 8:15], in0=p3[:, 1:8, 0], scalar=w2, in1=bmask[:], op0=ALU.mult,
                                                       op1=ALU.mult), reads=PSloB + [cb], writes=[t7b])
            op("dve", lambda e: e.tensor_tensor(out=d3[:, 0:7, 255], in0=d3[:, 0:7, 255], in1=t7[:, 8:15], op=ALU.subtract),
               reads=[t7b], writes=[dstb])
            if do_silu:
                op("act", lambda e: e.activation(out=dst[:], in_=dst[:], func=AF.Silu), reads=[dstb], writes=[dstb])

        def partnorm(src, srcb, scale_const):
            op("dve", lambda e: e.tensor_tensor(out=FM[SQ][:], in0=src[:], in1=src[:], op=ALU.mult), reads=[srcb], writes=[FMB[SQ]])
            for nb in range(4):
                op("pe", lambda e: e.matmul(PS[4 + nb][:, :], lhsT=ones[:], rhs=FM[SQ][:, nb * 512:(nb + 1) * 512], start=True, stop=True),
                   reads=[FMB[SQ], cb], writes=[PSB[4 + nb]])
            return

        def partnorm_apply(src, srcb, nscale, eps, extra=None):
            for nb in range(4):
                sl = slice(nb * 512, (nb + 1) * 512)
                op("dve", lambda e: e.tensor_scalar(out=FM[SQ][:, sl], in0=PS[4 + nb][:, :], scalar1=nscale, scalar2=eps,
                                                    op0=ALU.mult, op1=ALU.add), reads=[PSB[4 + nb]], writes=[FMB[SQ]])
            op("act", lambda e: e.activation(out=FM[SQ][:], in_=FM[SQ][:], func=AF.Ln), reads=[FMB[SQ]], writes=[FMB[SQ]])
            op("act", lambda e: e.activation(out=FM[SQ][:], in_=FM[SQ][:], func=AF.Exp, scale=-0.5), reads=[FMB[SQ]], writes=[FMB[SQ]])
            op("dve", lambda e: e.tensor_tensor(out=src[:], in0=src[:], in1=FM[SQ][:], op=ALU.mult), reads=[FMB[SQ], srcb], writes=[srcb])

        def T_(i):
            return tl[:, i, :]

        for hd in range(8):
            for (ci, dsti, conv) in ((hd, QT, True), (8 + hd, KT, True), (16 + hd, VT, True), (None, ZT, False)):
                c0 = 3072 + ci * 128 if ci is not None else 6144 + hd * 128
                proj_chunk(c0)
                conv_silu(FM[dsti], FMB[dsti], gcw, ci if ci is not None else 0, do_conv=conv)
            chk(7)
            for (dsti, scl) in ((QT, 128.0 ** -0.5), (KT, 1.0)):
                partnorm(FM[dsti], FMB[dsti], None)
                partnorm_apply(FM[dsti], FMB[dsti], 1.0, EPS)
                if scl != 1.0:
                    op("dve", lambda e: e.tensor_scalar(out=FM[dsti][:], in0=FM[dsti][:], scalar1=scl, scalar2=None, op0=ALU.mult),
                       reads=[FMB[dsti]], writes=[FMB[dsti]])
            chk(8)
            for (srci, dst, dstb) in ((KT, ktok, ktokb), (VT, vtok, vtokb)):
                for q in range(4):
                    for kk in range(4):
                        tt = q * 4 + kk
                        op("pe", lambda e: e.transpose(out=PS[q][:, kk * 128:(kk + 1) * 128], in_=FM[srci][:, tt * 128:(tt + 1) * 128],
                                                       identity=ident[:]), reads=[FMB[srci], identb], writes=[PSB[q]], inc=(kk == 3))
                    op("dve", lambda e: e.tensor_copy(out=dst[:, q * 4:(q + 1) * 4, :].rearrange("p a b -> p (a b)"), in_=PS[q][:, :]),
                       reads=[PSB[q]], writes=[dstb])
            chk(9)
            def mm_(e, out, lhsT, rhs, **kw):
                if GDN_F32R:
                    lhsT, rhs = lhsT.bitcast(mybir.dt.float32r), rhs.bitcast(mybir.dt.float32r)
                return e.matmul(out, lhsT=lhsT, rhs=rhs, **kw)

            def gdn_dir(d):
                tl_, tlb = tl2[d], tlb2[d]
                T_ = lambda i: tl_[:, i, :]
                bx, by = 4 + 2 * d, 5 + 2 * d
                keys_ = ((4, 0), (5, 0), (5, 1), (6, 0), (6, 1), (7, 0), (7, 1))
                SL = {k_: ((k_[0] if d == 0 else k_[0] - 4), k_[1]) for k_ in keys_}
                def P_(ob, oslot):
                    b_, s_ = SL[(ob, oslot)]
                    return PS[b_][:, s_ * 128:(s_ + 1) * 128]
                def PB_(ob, oslot):
                    b_, s_ = SL[(ob, oslot)]
                    return psl[b_][s_]
                order = list(range(NT)) if d == 0 else list(range(NT - 1, -1, -1))
                mA, mB = (0, 1) if d == 0 else (2, 3)
                dma("sp", T_(SS), s0_in[d, hd], writes=[tlb[SS]])
                for ci_, c in enumerate(order):
                    tsl = slice(c * 128, (c + 1) * 128)
                    seg = c // 2
                    seg_start = (c % 2 == 0) if d == 0 else (c % 2 == 1)
                    seg_end = not seg_start
                    if seg_start and ci_ > 0:
                        kcol = keepc[:, seg:seg + 1] if d == 0 else keepc[:, seg + 1:seg + 2]
                        yield op("dve", lambda e: e.tensor_scalar(out=T_(SS), in0=T_(SS), scalar1=kcol, scalar2=None, op0=ALU.mult),
                           reads=[tlb[SS], cb], writes=[tlb[SS]])
                    gc = col(GCUM, c, d, hd)
                    yield op("dve", lambda e: e.tensor_scalar(out=T_(DG), in0=ident[:], scalar1=gc, scalar2=None, op0=ALU.mult),
                       reads=[identb, gvb], writes=[tlb[DG]])
                    yield op("pe", lambda e: mm_(e, P_(4, 0), lhsT=ones[:], rhs=T_(DG), start=True, stop=True),
                       reads=[tlb[DG], cb], writes=[PB_(4, 0)])
                    yield op("dve", lambda e: e.tensor_scalar(out=T_(ND), in0=P_(4, 0), scalar1=gc, scalar2=None, op0=ALU.subtract),
                       reads=[PB_(4, 0), gvb], writes=[tlb[ND]])
                    yield op("act", lambda e: e.activation(out=T_(ND), in_=T_(ND), func=AF.Abs), reads=[tlb[ND]], writes=[tlb[ND]])
                    yield op("act", lambda e: e.activation(out=T_(MM), in_=T_(ND), func=AF.Exp, scale=-1.0), reads=[tlb[ND]], writes=[tlb[MM]])
                    yield op("act", lambda e: e.activation(out=T_(EG), in_=P_(4, 0), func=AF.Exp), reads=[PB_(4, 0)], writes=[tlb[EG]])
                    yield op("dve", lambda e: e.tensor_tensor(out=T_(ML), in0=T_(MM), in1=gmask[:, mA, :], op=ALU.mult),
                       reads=[tlb[MM], cb], writes=[tlb[ML]])
                    yield op("dve", lambda e: e.tensor_tensor(out=T_(MU), in0=T_(MM), in1=gmask[:, mB, :], op=ALU.mult),
                       reads=[tlb[MM], cb], writes=[tlb[MU]])
                    chk(10)
                    yield op("pe", lambda e: mm_(e, P_(5, 0), lhsT=FM[KT][:, tsl], rhs=FM[KT][:, tsl], start=True, stop=True),
                       reads=[FMB[KT]], writes=[PB_(5, 0)])
                    yield op("dve", lambda e: e.scalar_tensor_tensor(out=T_(NA), in0=P_(5, 0), scalar=col(NBETA, c, d, hd), in1=T_(ML),
                                                               op0=ALU.mult, op1=ALU.mult), reads=[PB_(5, 0), gvb, tlb[ML]], writes=[tlb[NA]])
                    yield op("pe", lambda e: e.transpose(out=P_(5, 1), in_=T_(NA), identity=ident[:]), reads=[tlb[NA], identb], writes=[PB_(5, 1)])
                    yield op("dve", lambda e: e.tensor_copy(out=T_(NTA), in_=P_(5, 1)), reads=[PB_(5, 1)], writes=[tlb[NTA]])
                    for (dst_, src_, mi) in ((NBD, NA, 4), (NTBD, NTA, 4), (E1, NA, 5), (E1T, NTA, 5), (E2, NA, 6)):
                        yield op("pool", lambda e: e.tensor_tensor(out=T_(dst_), in0=T_(src_), in1=gmask[:, mi, :], op=ALU.mult),
                           reads=[tlb[src_], cb], writes=[tlb[dst_]])
                    yield op("dve", lambda e: e.tensor_tensor(out=T_(RTA), in0=T_(NTBD), in1=ident[:], op=ALU.add), reads=[tlb[NTBD], identb], writes=[tlb[RTA]])
                    cur = (NBD, NTBD, RTA)
                    nxt = (NB_, NTB, RTB)
                    for lvl in range(4):
                        n_, nt_, rt_ = cur
                        n2, nt2, rt2 = nxt
                        yield op("pe", lambda e: mm_(e, P_(6, 0), lhsT=T_(nt_), rhs=T_(n_), start=True, stop=True),
                           reads=[tlb[nt_], tlb[n_]], writes=[PB_(6, 0)])
                        yield op("act", lambda e: e.activation(out=T_(n2), in_=P_(6, 0), func=AF.Copy), reads=[PB_(6, 0)], writes=[tlb[n2]])
                        if lvl < 3:
                            yield op("pe", lambda e: mm_(e, P_(7, 0), lhsT=T_(n_), rhs=T_(nt_), start=True, stop=True),
                               reads=[tlb[nt_], tlb[n_]], writes=[PB_(7, 0)])
                            yield op("act", lambda e: e.activation(out=T_(nt2), in_=P_(7, 0), func=AF.Copy), reads=[PB_(7, 0)], writes=[tlb[nt2]])
                        yield op("pe", lambda e: mm_(e, P_(5, 0), lhsT=T_(n2), rhs=T_(rt_), start=True, stop=True),
                           reads=[tlb[n2], tlb[rt_]], writes=[PB_(5, 0)])
                        yield op("dve", lambda e: e.tensor_tensor(out=T_(rt2), in0=P_(5, 0), in1=T_(rt_), op=ALU.add),
                           reads=[PB_(5, 0), tlb[rt_]], writes=[tlb[rt2]])
                        cur, nxt = nxt, cur
                    DT0 = cur[2]
                    yield op("pe", lambda e: e.transpose(out=P_(5, 1), in_=T_(DT0), identity=ident[:]), reads=[tlb[DT0], identb], writes=[PB_(5, 1)])
                    yield op("act", lambda e: e.activation(out=T_(DD), in_=P_(5, 1), func=AF.Copy), reads=[PB_(5, 1)], writes=[tlb[DD]])
                    yield op("pe", lambda e: mm_(e, P_(6, 0), lhsT=T_(E1T), rhs=T_(DD), start=True, stop=True),
                       reads=[tlb[E1T], tlb[DD]], writes=[PB_(6, 0)])
                    yield op("act", lambda e: e.activation(out=T_(PP), in_=P_(6, 0), func=AF.Copy), reads=[PB_(6, 0)], writes=[tlb[PP]])
                    yield op("pe", lambda e: mm_(e, P_(7, 0), lhsT=T_(DT0), rhs=T_(PP), start=True, stop=True),
                       reads=[tlb[DT0], tlb[PP]], writes=[PB_(7, 0)])
                    yield op("dve", lambda e: e.tensor_tensor(out=T_(DD2), in0=P_(7, 0), in1=T_(DD), op=ALU.add),
                       reads=[PB_(7, 0), tlb[DD]], writes=[tlb[DD2]])
                    yield op("pe", lambda e: mm_(e, P_(6, 1), lhsT=T_(E1), rhs=T_(DT0), start=True, stop=True),
                       reads=[tlb[E1], tlb[DT0]], writes=[PB_(6, 1)])
                    yield op("act", lambda e: e.activation(out=T_(XX), in_=P_(6, 1), func=AF.Copy), reads=[PB_(6, 1)], writes=[tlb[XX]])
                    yield op("pe", lambda e: mm_(e, P_(7, 1), lhsT=T_(DD), rhs=T_(XX), start=True, stop=True),
                       reads=[tlb[DD], tlb[XX]], writes=[PB_(7, 1)])
                    yield op("dve", lambda e: e.tensor_tensor(out=T_(DT2), in0=P_(7, 1), in1=T_(DT0), op=ALU.add),
                       reads=[PB_(7, 1), tlb[DT0]], writes=[tlb[DT2]])
                    yield op("pe", lambda e: mm_(e, P_(6, 0), lhsT=T_(E2), rhs=T_(DT2), start=True, stop=True),
                       reads=[tlb[E2], tlb[DT2]], writes=[PB_(6, 0)])
                    yield op("act", lambda e: e.activation(out=T_(XX), in_=P_(6, 0), func=AF.Copy), reads=[PB_(6, 0)], writes=[tlb[XX]])
                    yield op("pe", lambda e: mm_(e, P_(7, 0), lhsT=T_(DD2), rhs=T_(XX), start=True, stop=True),
                       reads=[tlb[DD2], tlb[XX]], writes=[PB_(7, 0)])
                    yield op("dve", lambda e: e.tensor_tensor(out=T_(RTF), in0=P_(7, 0), in1=T_(DT2), op=ALU.add),
                       reads=[PB_(7, 0), tlb[DT2]], writes=[tlb[RTF]])
                    chk(11)
                    RT = RTF
                    yield op("dve", lambda e: e.tensor_scalar(out=T_(VB), in0=vtok[:, c, :], scalar1=col(BETA, c, d, hd), scalar2=None, op0=ALU.mult),
                       reads=[vtokb, gvb], writes=[tlb[VB]])
                    yield op("dve", lambda e: e.tensor_scalar(out=T_(KBG), in0=ktok[:, c, :], scalar1=col(BG, c, d, hd), scalar2=None, op0=ALU.mult),
                       reads=[ktokb, gvb], writes=[tlb[KBG]])
                    yield op("dve", lambda e: e.tensor_scalar(out=T_(KDE), in0=ktok[:, c, :], scalar1=col(KD, c, d, hd), scalar2=None, op0=ALU.mult),
                       reads=[ktokb, gvb], writes=[tlb[KDE]])
                    yield op("pe", lambda e: mm_(e, P_(6, 0), lhsT=T_(RT), rhs=T_(VB), start=True, stop=True),
                       reads=[tlb[RT], tlb[VB]], writes=[PB_(6, 0)])
                    yield op("act", lambda e: e.activation(out=T_(UU), in_=P_(6, 0), func=AF.Copy), reads=[PB_(6, 0)], writes=[tlb[UU]])
                    yield op("pe", lambda e: mm_(e, P_(7, 0), lhsT=T_(KBG), rhs=T_(RT), start=True, stop=True),
                       reads=[tlb[RT], tlb[KBG]], writes=[PB_(7, 0)])
                    yield op("act", lambda e: e.activation(out=T_(WT), in_=P_(7, 0), func=AF.Copy), reads=[PB_(7, 0)], writes=[tlb[WT]])
                    yield op("pe", lambda e: mm_(e, P_(5, 0), lhsT=FM[KT][:, tsl], rhs=FM[QT][:, tsl], start=True, stop=True),
                       reads=[FMB[KT], FMB[QT]], writes=[PB_(5, 0)])
                    yield op("dve", lambda e: e.tensor_tensor(out=T_(ATT), in0=P_(5, 0), in1=T_(MU), op=ALU.mult),
                       reads=[PB_(5, 0), tlb[MU]], writes=[tlb[ATT]])
                    yield op("dve", lambda e: e.tensor_tensor(out=T_(QD), in0=FM[QT][:, tsl], in1=T_(EG), op=ALU.mult),
                       reads=[FMB[QT], tlb[EG]], writes=[tlb[QD]])
                    chk(12)
                    yield op("pe", lambda e: mm_(e, P_(6, 0), lhsT=T_(WT), rhs=T_(SS), start=True, stop=True),
                       reads=[tlb[WT], tlb[SS]], writes=[PB_(6, 0)])
                    yield op("dve", lambda e: e.tensor_tensor(out=T_(VN), in0=T_(UU), in1=P_(6, 0), op=ALU.subtract),
                       reads=[PB_(6, 0), tlb[UU]], writes=[tlb[VN]])
                    op("pe", lambda e: mm_(e, P_(7, 0), lhsT=T_(SS), rhs=T_(QD), start=True, stop=False),
                       reads=[tlb[SS], tlb[QD]], writes=[PB_(7, 0)], inc=False)
                    yield op("pe", lambda e: mm_(e, P_(7, 0), lhsT=T_(VN), rhs=T_(ATT), start=False, stop=True),
                       reads=[tlb[VN], tlb[ATT]], writes=[PB_(7, 0)])
                    yield op("act", lambda e: e.activation(out=OAd[d][:, tsl], in_=P_(7, 0), func=AF.Copy), reads=[PB_(7, 0)], writes=([OAdb[d], FMB[OA]] if d == 0 else [OAdb[d]]))
                    yield op("pe", lambda e: mm_(e, P_(6, 1), lhsT=T_(KDE), rhs=T_(VN), start=True, stop=True),
                       reads=[tlb[KDE], tlb[VN]], writes=[PB_(6, 1)])
                    egl = tl_[:, EG, 127:128] if d == 0 else tl_[:, EG, 0:1]
                    yield op("dve", lambda e: e.scalar_tensor_tensor(out=T_(SS), in0=T_(SS), scalar=egl, in1=P_(6, 1), op0=ALU.mult,
                                                               op1=ALU.add), reads=[PB_(6, 1), tlb[EG], tlb[SS]], writes=[tlb[SS]])
                    if seg_end:
                        dma("sp", st_out[seg, d, hd], T_(SS), reads=[tlb[SS]], writes=[stoutb[(seg * 2 + d) * 8 + hd]])
            for b_ in range(8):
                for s_ in range(4):
                    psl[b_][s_].w = dict(PSB[b_].w)
                    psl[b_][s_].r = dict(PSB[b_].r)
            gens = [gdn_dir(0), gdn_dir(1)]
            if not GDN_INTERLEAVE:
                for g_ in gens:
                    for _ in g_:
                        pass
                gens = []
            while gens:
                for g_ in list(gens):
                    try:
                        next(g_)
                    except StopIteration:
                        gens.remove(g_)
            for b_ in range(8):
                for s_ in range(4):
                    for (dst_, src_) in ((PSB[b_].w, psl[b_][s_].w), (PSB[b_].r, psl[b_][s_].r)):
                        for k_, v_ in src_.items():
                            if k_ not in dst_ or dst_[k_][1] < v_[1]:
                                dst_[k_] = v_
            op("dve", lambda e: e.tensor_tensor(out=FM[OA][:], in0=OAd[0][:], in1=OAd[1][:], op=ALU.add), reads=OAdb, writes=[FMB[OA]])
            chk(13)
            if DEBUG and hd == 0:
                dma("sp", dbg_q[:, :], FM[OA][:], reads=[FMB[OA]], writes=[dbgb])
            partnorm(FM[OA], FMB[OA], None)
            partnorm_apply(FM[OA], FMB[OA], 1.0 / 128.0, EPS)
            if DEBUG and hd == 0:
                dma("sp", dbg_k[:, :], FM[SQ][:], reads=[FMB[SQ]], writes=[dbgb])
            op("dve", lambda e: e.scalar_tensor_tensor(out=yo[:], in0=FM[OA][:], scalar=onorm[:, 0:1], in1=FM[ZT][:], op0=ALU.mult,
                                                       op1=ALU.mult), reads=[FMB[OA], FMB[ZT], cb], writes=[yob])
            dma("sp", ycatT[8 + hd], yo[:], reads=[yob], writes=[ycb[8 + hd]])
            if DEBUG and hd == 0:
                dma("sp", dbg_oa[:, :], FM[OA][:], reads=[FMB[OA]], writes=[dbgb])
                dma("sp", dbg_z[:, :], FM[ZT][:], reads=[FMB[ZT]], writes=[dbgb])
        kb.release(gm)
        kb.release(mk)

        chk(14)
        mk2 = kb.mark()
        wo = kb.sb("wo", [128, 16, D], BF16)
        wob = Buf("wo")
        for q in range(4):
            dma("pool", wo[:, :, q * 512:(q + 1) * 512], w_out[:, q * 512:(q + 1) * 512].rearrange("(c p) n -> p c n", p=128), writes=[wob])
        yT = [kb.sb("yT%d" % i, [128, 16, 128], BF16) for i in range(2)]
        yTb = bufs(2, "yT")
        xr = [kb.sb("xr%d" % i, [128, D], F32) for i in range(2)]
        xrb = bufs(2, "xr")
        hy = kb.sb("hyacc", [128, D], F32)
        hyb = Buf("hyacc")
        tmp = [kb.sb("mtmp%d" % i, [128, 512], F32) for i in range(2)]
        tmpb = bufs(2, "mtmp")
        junk2 = kb.sb("junk2", [128, 512], BF16)
        junk2b = Buf("junk2")
        st2 = kb.sb("st2", [128, 8], F32)
        st2b = Buf("st2")
        Gp = kb.sb("mGp", [128, D], F32)
        gpost = kb.sb("mgpost", [128, D], F32)
        Gpb = Buf("mGp")
        hyg = kb.sb("hyrs", [128, 16], F32)
        dma("sp", Gp[:], modrow[0, 5 * D:6 * D].partition_broadcast(128), reads=[modb], writes=[Gpb])
        dma("sp", gpost[:], norm_post[1, :].partition_broadcast(128), writes=[Gpb])
        op("dve", lambda e: e.tensor_tensor(out=Gp[:], in0=Gp[:], in1=gpost[:], op=ALU.mult), reads=[Gpb], writes=[Gpb])
        op("dve", lambda e: e.tensor_scalar(out=hyg[:], in0=hyss[:], scalar1=1.0 / 1024.0, scalar2=EPS, op0=ALU.mult, op1=ALU.add),
           reads=[hyssb], writes=[hyssb])
        op("act", lambda e: e.activation(out=hyg[:], in_=hyg[:], func=AF.Sqrt), reads=[hyssb], writes=[hyssb])
        op("dve", lambda e: e.reciprocal(out=hyg[:], in_=hyg[:]), reads=[hyssb], writes=[hyssb])
        for t in range(NT):
            s = t % 2
            with nc.allow_non_contiguous_dma(reason="256B rows"):
                dma("sp", yT[s][:], ycatT[:, :, t * 128:(t + 1) * 128].rearrange("c p n -> p c n"), reads=ycb, writes=[yTb[s]])
            dma("sp", xr[s][:], xres[t * 128:(t + 1) * 128, :], reads=[ybuf[t]], writes=[xrb[s]])
            for nbk in range(4):
                for c in range(8):
                    op("pe", lambda e: e.matmul(PS[nbk][:, :], lhsT=yT[s][:, c, :], rhs=wo[:, c, nbk * 512:(nbk + 1) * 512],
                                                start=(c == 0), stop=(c == 7)), reads=[yTb[s], wob], writes=[PSB[nbk]], inc=(c == 7))
                for c in range(8, 16):
                    op("pe", lambda e: e.matmul(PS[4 + nbk][:, :], lhsT=yT[s][:, c, :], rhs=wo[:, c, nbk * 512:(nbk + 1) * 512],
                                                start=(c == 8), stop=(c == 15)), reads=[yTb[s], wob], writes=[PSB[4 + nbk]], inc=(c == 15))
                sl = slice(nbk * 512, (nbk + 1) * 512)
                op("act", lambda e: e.activation(out=hy[:, sl], in_=PS[4 + nbk][:, :], func=AF.Copy), reads=[PSB[4 + nbk]], writes=[hyb])
                op("dve", lambda e: e.scalar_tensor_tensor(out=hy[:, sl], in0=PS[nbk][:, :], scalar=hyg[:, t:t + 1], in1=hy[:, sl],
                                                           op0=ALU.mult, op1=ALU.add), reads=[PSB[nbk], hyssb, hyb], writes=[hyb])
                op("act", lambda e: e.activation(out=junk2[:], in_=hy[:, sl], func=AF.Square, accum_out=st2[:, nbk:nbk + 1]),
                   reads=[hyb], writes=[junk2b, st2b])
            op("dve", lambda e: e.tensor_tensor(out=st2[:, 0:2], in0=st2[:, 0:2], in1=st2[:, 2:4], op=ALU.add), reads=[st2b], writes=[st2b])
            op("dve", lambda e: e.tensor_tensor(out=st2[:, 0:1], in0=st2[:, 0:1], in1=st2[:, 1:2], op=ALU.add), reads=[st2b], writes=[st2b])
            rstd_from_ss(st2[:, 0:1], st2[:, 4:5], st2[:, 5:6], st2b, D)
            for q in range(4):
                sl = slice(q * 512, (q + 1) * 512)
                ts_ = q % 2
                op("dve", lambda e: e.scalar_tensor_tensor(out=tmp[ts_][:], in0=hy[:, sl], scalar=st2[:, 4:5], in1=Gp[:, sl],
                                                           op0=ALU.mult, op1=ALU.mult), reads=[hyb, st2b, Gpb], writes=[tmpb[ts_]])
                op("dve", lambda e: e.tensor_tensor(out=xr[s][:, sl], in0=tmp[ts_][:], in1=xr[s][:, sl], op=ALU.add),
                   reads=[tmpb[ts_], xrb[s]], writes=[xrb[s]])
            dma("sp", (xres if STAGE >= 3 else y_out)[t * 128:(t + 1) * 128, :], xr[s][:], reads=[xrb[s]], writes=[ybuf[t]])
        kb.release(mk2)

    xinbufs = bufs(NT, "xsrc")
    if STAGE == 0:
        dma("sp", y_out[0:128, 0:144], modcol[:], reads=[colb], writes=[ybuf[0]])
        dma("sp", y_out[128:256, 0:48], Acol[:], reads=[colb], writes=[ybuf[1]])
    if STAGE >= 1:
        ffn_stage(0, 0, x_in, xinbufs, True, xres if STAGE >= 2 else y_out)
    if STAGE >= 2 and RUN_MIXER:
        try:
            mixer_stage()
        except StopMix:
            pass
    if STAGE >= 3:
        ffn_stage(2, 1, xres, ybuf, False, y_out)
    kb.barrier()
    return kb


def grid_pos():
    rows = T // 64
    r = np.broadcast_to(np.arange(rows, dtype=np.float32)[:, None], (rows, 64)).reshape(-1)
    col = np.broadcast_to(np.arange(64, dtype=np.float32)[None, :], (rows, 64)).reshape(-1)
    quarter = D // 4
    omega = (1.0 / (np.float32(10000.0) ** (np.arange(quarter, dtype=np.float32) / np.float32(quarter)))).astype(np.float32)
    ar = r[:, None] * omega[None]
    ac = col[:, None] * omega[None]
    return np.concatenate([np.sin(ar), np.cos(ar), np.sin(ac), np.cos(ac)], axis=-1).astype(np.float32)


_HC = {}


def hyena_consts(L):
    if L in _HC:
        return _HC[L]
    import ml_dtypes
    f = np.float32
    nrep = T // L
    idx = np.arange(L, dtype=np.float64)
    tt = (idx / (L - 1)).astype(f)
    bands = np.arange(1, 17, dtype=np.float64)
    ang = (2.0 * math.pi / L) * idx[:, None] * bands[None, :]
    feats = np.concatenate([tt[:, None].astype(np.float64), np.cos(ang), np.sin(ang)], axis=-1).astype(f)
    featsT = np.ascontiguousarray(np.tile(feats, (nrep, 1)).T)
    lagp = np.tile(np.arange(L), nrep)
    col = np.zeros((128, 64), f)
    lag2 = lagp.reshape(16, 128).T
    col[:, 0:16] = -(lag2 / (L - 1.0))
    col[:, 16:32] = (lag2 != 0)
    col[:, 32:48] = (lag2 == 0)
    col[:, 48:64] = (lag2 != 0)
    ph = math.pi * np.outer(idx, idx) / L
    sgn = np.where(idx % 2 == 0, 1.0, -1.0)
    cf = np.cos(ph)
    sf = -np.sin(ph)
    sf[:, 0] = sgn
    wf = np.full(L, 1.0 / L)
    wf[0] = 0.5 / L
    gc = (np.cos(ph) * wf[None, :]).T
    gs = (-np.sin(ph) / L).T
    gs[0, :] = 0.5 / L * sgn

    def tiles(blk):
        full = np.zeros((T, T), f)
        for r in range(nrep):
            full[r * L:(r + 1) * L, r * L:(r + 1) * L] = blk
        return np.ascontiguousarray(full.reshape(16, 128, 16, 128).transpose(2, 1, 0, 3)).astype(ml_dtypes.bfloat16)
    out = {"feats": featsT, "hcol": col, "nrep": np.full((128, 1), float(nrep), f),
           "CF_t": tiles(cf), "SF_t": tiles(sf), "GC_t": tiles(gc), "GS_t": tiles(gs)}
    _HC[L] = out
    return out


def core_inputs(core, inp):
    f = np.float32
    m = {}
    if core < 4:
        m["x"] = np.ascontiguousarray(inp["x_sample"][core])
        m["pos"] = grid_pos()
        m["cvec"] = np.ascontiguousarray(inp["c"][core:core + 1])
    else:
        xp = np.zeros((T, D), f)
        xp[:1024] = inp["x_prompt"][(core - 4) * 4:(core - 4) * 4 + 4].reshape(1024, D)
        m["x"] = xp
        m["pos"] = np.zeros((T, D), f)
        m["cvec"] = np.ascontiguousarray(inp["c_ctx"].reshape(1, D))
    m["ada_w"] = np.ascontiguousarray(inp["ada_w"][0])
    m["ada_b"] = np.ascontiguousarray(inp["ada_b"][0].reshape(1, -1))
    m["norm_pre"] = np.ascontiguousarray(inp["norm_pre"][0])
    m["norm_post"] = np.ascontiguousarray(inp["norm_post"][0])
    m["ffn_wg"] = np.ascontiguousarray(inp["ffn_wg"][0])
    m["ffn_wu"] = np.ascontiguousarray(inp["ffn_wu"][0])
    m["ffn_wd"] = np.ascontiguousarray(inp["ffn_wd"][0])
    m["ident"] = np.eye(128, dtype=f)
    m["w_in"] = np.ascontiguousarray(inp["w_in"][0])
    m["w_out"] = np.ascontiguousarray(inp["w_out"][0])
    m["gdn_conv_w"] = np.ascontiguousarray(inp["gdn_conv_w"][0])
    m["hy_conv_w"] = np.ascontiguousarray(inp["hy_conv_w"][0])
    m["gdn_o_norm"] = np.ascontiguousarray(inp["gdn_o_norm"][0].reshape(1, 128))
    p = np.arange(128)[:, None]
    q = np.arange(128)[None, :]
    def bm(b):
        return (p // b) == (q // b)
    m["gmask"] = np.stack([(p > q), (q >= p), (q > p), (q <= p), bm(32), bm(64) & ~bm(32), ~bm(64)]).astype(f)
    sample = core < 4
    keep = np.ones((1, 8), f) if sample else np.zeros((1, 8), f)
    m["keep"] = keep
    m["bmask"] = np.zeros((1, 7), f) if sample else np.ones((1, 7), f)
    m["dtb"] = np.ascontiguousarray(np.tile(inp["gdn_dt_bias"][0].reshape(1, 16), (1, 16)))
    m["alog"] = np.ascontiguousarray(np.tile(inp["gdn_a_log"][0].reshape(1, 16), (1, 16)))
    for nm in ("hy_f_w1", "hy_f_w2", "hy_f_w3", "hy_decay", "hy_bias"):
        m[nm] = np.ascontiguousarray(inp[nm][0])
    m["hy_f_b1"] = np.ascontiguousarray(inp["hy_f_b1"][0].reshape(1, 64))
    m["hy_f_b2"] = np.ascontiguousarray(inp["hy_f_b2"][0].reshape(1, 64))
    m["hy_out_norm"] = np.ascontiguousarray(inp["hy_out_norm"][0].reshape(1, 1024))
    m.update(hyena_consts(2048 if sample else 256))
    m["s0"] = np.ascontiguousarray(inp["state_gdn"][core, 0]) if sample else np.zeros((2, 8, 128, 128), f)
    return m


def kernel(**inputs):
    inp = {k: np.asarray(v) for k, v in inputs.items()}
    kb = build()
    in_maps = [core_inputs(c, inp) for c in range(8)]
    res = run_bass_kernel_spmd(kb.nc, in_maps, core_ids=list(range(8)))
    ys = [r["y"] for r in res.results]
    y_sample = np.stack(ys[:4], axis=0).astype(np.float32)
    y_prompt = np.concatenate([ys[c][:1024].reshape(4, 256, D) for c in range(4, 8)], axis=0).astype(np.float32)
    new_state = np.zeros((16, 1, 2, 8, 128, 128), np.float32)
    for c in range(4, 8):
        so = res.results[c]["st_out"]
        for sq in range(4):
            new_state[(c - 4) * 4 + sq, 0] = so[sq]
    return (y_prompt, y_sample, new_state)
```

```python
import math
import numpy as np
import concourse.bass as bass
import concourse.mybir as mybir
from concourse.bass_utils import run_bass_kernel_spmd

F32 = mybir.dt.float32
BF16 = mybir.dt.bfloat16
AF = mybir.ActivationFunctionType
ALU = mybir.AluOpType

D = 2048
DFF = 5632
T = 2048
NT = 16
TB = 512
WIN = 7200
EPS = 1e-6
FFN_PARTS = 4
MIX_PARTS = 99
GDN_F32R = False
SKIP_H2 = False
GDN_INTERLEAVE = True
DEBUG = False
RUN_MIXER = True


class StopMix(Exception):
    pass


def chk(n):
    if MIX_PARTS <= n:
        raise StopMix()
EVAC_ACT = False
STAGE = 9


class Buf:
    __slots__ = ("w", "r", "dsem", "dcnt", "name")

    def __init__(self, name=""):
        self.w = {}
        self.r = {}
        self.dsem = None
        self.dcnt = 0
        self.name = name


class KB:
    def __init__(self):
        self.nc = bass.Bass("TRN2", target_bir_lowering=False)
        nc = self.nc
        self.E = {}
        for nm, e in (("pe", nc.tensor), ("act", nc.scalar), ("dve", nc.vector),
                      ("pool", nc.gpsimd), ("sp", nc.sync)):
            sem = nc.semaphore("s_" + nm).__enter__()
            self.E[nm] = dict(e=e, sem=sem, cnt=0, waited={}, name=nm)
        self.dsems = []
        self.ctx = []
        self.n_ins = 0

    def sb(self, name, shape, dt):
        self.uid = getattr(self, "uid", 0) + 1
        cm = self.nc.sbuf_tensor("%s_s%d" % (name, self.uid), shape, dt)
        t = cm.__enter__()
        self.ctx.append(cm)
        return t

    def ps(self, name, shape, dt=F32):
        self.uid = getattr(self, "uid", 0) + 1
        cm = self.nc.psum_tensor("%s_p%d" % (name, self.uid), shape, dt)
        t = cm.__enter__()
        self.ctx.append(cm)
        return t

    def mark(self):
        return len(self.ctx)

    def release(self, mark):
        self.barrier()
        while len(self.ctx) > mark:
            cm = self.ctx.pop()
            cm.__exit__(None, None, None)

    def _deps(self, reads, writes):
        deps = {}
        for b in reads:
            for k, v in b.w.items():
                if k not in deps or deps[k][1] < v[1]:
                    deps[k] = v
        for b in writes:
            for dd in (b.w, b.r):
                for k, v in dd.items():
                    if k not in deps or deps[k][1] < v[1]:
                        deps[k] = v
        return deps

    def _wait(self, E, deps):
        for k, (semobj, val) in deps.items():
            if E["name"] == "pe" and semobj is E["sem"]:
                continue
            if E["waited"].get(k, 0) >= val:
                continue
            E["e"].wait_ge(semobj, val)
            E["waited"][k] = val

    def _record(self, tag, reads, writes):
        k = id(tag[0])
        for b in reads:
            if k not in b.r or b.r[k][1] < tag[1]:
                b.r[k] = tag
        for b in writes:
            b.w = {k: tag}
            b.r = {}

    def op(self, eng, emit, reads=(), writes=(), inc=True):
        E = self.E[eng]
        self._wait(E, self._deps(reads, writes))
        ins = emit(E["e"])
        self.n_ins += 1
        if inc:
            E["cnt"] += 1
            ins.then_inc(E["sem"], 1)
            tag = (E["sem"], E["cnt"])
        else:
            tag = (E["sem"], E["cnt"] + 1)
        self._record(tag, reads, writes)

    def dma(self, q, out, in_, reads=(), writes=(), **kw):
        E = self.E[q]
        self._wait(E, self._deps(reads, writes))
        b = writes[0]
        if b.dsem is None:
            b.dsem = self.nc.semaphore("d%d" % len(self.dsems)).__enter__()
            self.dsems.append(b)
        b.dcnt += 16
        E["e"].dma_start(out=out, in_=in_, **kw).then_inc(b.dsem, 16)
        self.n_ins += 1
        self._record((b.dsem, b.dcnt), reads, writes)

    def barrier(self):
        allsem = [(E["sem"], E["cnt"]) for E in self.E.values() if E["cnt"] > 0]
        allsem += [(b.dsem, b.dcnt) for b in self.dsems]
        for E in self.E.values():
            deps = {id(s): (s, v) for s, v in allsem if s is not E["sem"]}
            self._wait(E, deps)


def bufs(n, name=""):
    return [Buf("%s%d" % (name, i)) for i in range(n)]


def build():
    kb = KB()
    nc = kb.nc
    op, dma = kb.op, kb.dma

    def din(name, shape, dt=F32):
        return nc.dram_tensor(name, list(shape), dt, kind="ExternalInput").ap()

    x_in = din("x", [T, D])
    pos_in = din("pos", [T, D])
    cvec = din("cvec", [1, D])
    ada_w = din("ada_w", [D, 9 * D])
    ada_b = din("ada_b", [1, 9 * D])
    norm_pre = din("norm_pre", [3, D])
    norm_post = din("norm_post", [3, D])
    ffn_wg = din("ffn_wg", [2, D, DFF])
    ffn_wu = din("ffn_wu", [2, D, DFF])
    ffn_wd = din("ffn_wd", [2, DFF, D])
    ident_in = din("ident", [128, 128])
    w_in = din("w_in", [D, WIN])
    w_out = din("w_out", [D, D])
    gdn_conv_w = din("gdn_conv_w", [3, 3072])
    hy_conv_w = din("hy_conv_w", [3, 3072])
    gdn_o_norm = din("gdn_o_norm", [1, 128])
    gmask_in = din("gmask", [7, 128, 128])
    keep_in = din("keep", [1, 8])
    bmask_in = din("bmask", [1, 7])
    dtb_in = din("dtb", [1, 256])
    alog_in = din("alog", [1, 256])
    s0_in = din("s0", [2, 8, 128, 128])
    hy_f_w1 = din("hy_f_w1", [33, 64])
    hy_f_b1 = din("hy_f_b1", [1, 64])
    hy_f_w2 = din("hy_f_w2", [64, 64])
    hy_f_b2 = din("hy_f_b2", [1, 64])
    hy_f_w3 = din("hy_f_w3", [64, 4096])
    hy_decay = din("hy_decay", [2, 1024])
    hy_bias = din("hy_bias", [2, 1024])
    hy_out_norm = din("hy_out_norm", [1, 1024])
    feats_in = din("feats", [33, T])
    hcol_in = din("hcol", [128, 64])
    nrep_in = din("nrep", [128, 1])
    CF_t = din("CF_t", [16, 128, 16, 128], BF16)
    SF_t = din("SF_t", [16, 128, 16, 128], BF16)
    GC_t = din("GC_t", [16, 128, 16, 128], BF16)
    GS_t = din("GS_t", [16, 128, 16, 128], BF16)
    tapsS = nc.dram_tensor("tapsS", [T, 2048], BF16, kind="Internal").ap()
    tapsD = nc.dram_tensor("tapsD", [T, 2048], BF16, kind="Internal").ap()
    tapsb = Buf("taps")
    spec_re = nc.dram_tensor("spec_re", [T, 2048], F32, kind="Internal").ap()
    spec_im = nc.dram_tensor("spec_im", [T, 2048], F32, kind="Internal").ap()
    spec_rb = nc.dram_tensor("spec_rb", [T, 2048], F32, kind="Internal").ap()
    specb = Buf("spec")
    st_out = nc.dram_tensor("st_out", [8, 2, 8, 128, 128], F32, kind="ExternalOutput").ap()
    stoutb = [Buf("stout")] * 128
    ycatT = nc.dram_tensor("ycatT", [16, 128, T], BF16, kind="Internal").ap()
    ycb = [b_ for b_ in bufs(4, "ycat") for _ in range(4)]
    dbgb = Buf("dbg")
    if DEBUG:
        dbg_gv = nc.dram_tensor("dbg_gv", [128, 12 * 256], F32, kind="ExternalOutput").ap()
        dbg_oa = nc.dram_tensor("dbg_oa", [128, T], F32, kind="ExternalOutput").ap()
        dbg_q = nc.dram_tensor("dbg_q", [128, T], F32, kind="ExternalOutput").ap()
        dbg_k = nc.dram_tensor("dbg_k", [128, T], F32, kind="ExternalOutput").ap()
        dbg_z = nc.dram_tensor("dbg_z", [128, T], F32, kind="ExternalOutput").ap()
    y_out = nc.dram_tensor("y", [T, D], F32, kind="ExternalOutput").ap()
    modrow = nc.dram_tensor("modrow", [1, 9 * D], F32, kind="Internal").ap()
    xres = nc.dram_tensor("xres", [T, D], F32, kind="Internal").ap()

    ybuf = bufs(NT, "y")
    modb = Buf("modrow")
    NB = Buf("none")

    ident = kb.sb("ident", [128, 128], F32)
    identb = Buf("ident")
    dma("sp", ident[:], ident_in[:, :], writes=[identb])
    modcol = kb.sb("modcol", [128, 144], F32)
    npre = kb.sb("npre", [128, 48], F32)
    Acol = kb.sb("Acol", [128, 48], F32)
    colb = Buf("cols")
    hyss = kb.sb("hyss", [128, 16], F32)
    hyssb = Buf("hyss")
    PSALL = kb.ps("psall", [128, 4096])
    PS = [PSALL[:, i * 512:(i + 1) * 512] for i in range(8)]
    PSB = bufs(8, "ps")

    mk = kb.mark()
    cs = kb.sb("cs", [128, 16], F32)
    css = kb.sb("css", [128, 16], F32)
    csb = Buf("cs")
    with nc.allow_non_contiguous_dma(reason="small vector relayout"):
        dma("sp", cs[:], cvec.rearrange("o (kc p) -> p (o kc)", p=128), writes=[csb])
    op("act", lambda e: e.activation(out=css[:], in_=cs[:], func=AF.Silu), reads=[csb], writes=[csb])
    aw = [kb.sb("aw%d" % i, [128, 16, 512], F32) for i in range(2)]
    awb = bufs(2, "aw")
    abr = [kb.sb("abr%d" % i, [1, 512], F32) for i in range(2)]
    abrb = bufs(2, "abr")
    mrow = [kb.sb("mrow%d" % i, [1, 512], F32) for i in range(2)]
    mrowb = bufs(2, "mrow")
    for cg in range(36):
        s = cg % 2
        dma("sp", aw[s][:], ada_w[:, cg * 512:(cg + 1) * 512].rearrange("(kc p) n -> p kc n", p=128),
            writes=[awb[s]])
        dma("sp", abr[s][:], ada_b[:, cg * 512:(cg + 1) * 512], writes=[abrb[s]])
        bk = cg % 2
        for kc in range(16):
            op("pe", lambda e, kc=kc: e.matmul(PS[bk][0:1, :], lhsT=css[:, kc:kc + 1], rhs=aw[s][:, kc, :],
                                               start=(kc == 0), stop=(kc == 15)),
               reads=[csb, awb[s]], writes=[PSB[bk]], inc=(kc == 15))
        op("dve", lambda e: e.tensor_tensor(out=mrow[s][:], in0=PS[bk][0:1, :], in1=abr[s][:], op=ALU.add),
           reads=[PSB[bk], abrb[s]], writes=[mrowb[s]])
        dma("sp", modrow[:, cg * 512:(cg + 1) * 512], mrow[s][:], reads=[mrowb[s]], writes=[modb])
    with nc.allow_non_contiguous_dma(reason="small vector relayout"):
        dma("sp", modcol[:], modrow.rearrange("o (m kc p) -> p (o m kc)", p=128, kc=16), reads=[modb], writes=[colb])
        dma("sp", npre[:], norm_pre.rearrange("s (kc p) -> p (s kc)", p=128), writes=[colb])
    for s in range(3):
        sc = modcol[:, (3 * s + 1) * 16:(3 * s + 2) * 16]
        op("dve", lambda e, s=s, sc=sc: e.scalar_tensor_tensor(
            out=Acol[:, s * 16:(s + 1) * 16], in0=sc, scalar=1.0, in1=npre[:, s * 16:(s + 1) * 16],
            op0=ALU.add, op1=ALU.mult), reads=[colb], writes=[colb])
    kb.release(mk)

    def Bcol(s, kc):
        return modcol[:, 3 * s * 16 + kc:3 * s * 16 + kc + 1]

    def rstd_from_ss(ss_ap, out_ap, tmp_ap, rb, n):
        op("dve", lambda e: e.tensor_scalar(out=tmp_ap, in0=ss_ap, scalar1=1.0 / n, scalar2=EPS,
                                            op0=ALU.mult, op1=ALU.add), reads=[rb], writes=[rb])
        op("act", lambda e: e.activation(out=tmp_ap, in_=tmp_ap, func=AF.Sqrt), reads=[rb], writes=[rb])
        op("dve", lambda e: e.reciprocal(out=out_ap, in_=tmp_ap), reads=[rb], writes=[rb])

    def ffn_stage(si, layer, xsrc, xsrc_bufs, add_pos, xdst):
        mk = kb.mark()
        hT = kb.sb("hT", [128, 16, TB], BF16)
        hTb = [[Buf() for _ in range(4)] for _ in range(16)]
        aT = kb.sb("aT", [128, 44, TB], BF16)
        aTb = bufs(44, "aT")
        wg_s = [kb.sb("wg%d" % i, [128, 16, 256], BF16) for i in range(2)]
        wu_s = [kb.sb("wu%d" % i, [128, 16, 256], BF16) for i in range(2)]
        wgb, wub = bufs(2, "wg"), bufs(2, "wu")
        wd_s = [kb.sb("wd%d" % i, [128, 2, 1024], BF16) for i in range(2)]
        wdb = bufs(2, "wd")
        xin = [kb.sb("xin%d" % i, [128, D], F32) for i in range(4)]
        xinb = bufs(4, "xin")
        xn = kb.sb("xn", [128, D], F32)
        xnb = Buf("xn")
        junk = kb.sb("junk", [128, D], BF16)
        junkb = Buf("junk")
        yacc = kb.sb("yacc", [128, 4, 1024], F32)
        yaccb = bufs(4, "yacc")
        tmp = [kb.sb("tmp%d" % i, [128, 512], F32) for i in range(2)]
        tmpb = bufs(2, "tmp")
        sg = [kb.sb("sg%d" % i, [128, 512], F32) for i in range(2)]
        sgb = bufs(2, "sg")
        Gp = kb.sb("Gp", [128, D], F32)
        gpost = kb.sb("gpost", [128, D], F32)
        Gpb = Buf("Gp")
        st = kb.sb("st", [128, 32], F32)
        stb = bufs(8, "st")
        gi = 3 * si + 2
        dma("sp", Gp[:], modrow[0, gi * D:(gi + 1) * D].partition_broadcast(128), reads=[modb], writes=[Gpb])
        dma("sp", gpost[:], norm_post[si, :].partition_broadcast(128), writes=[Gpb])
        gsc = 1.0 if si == 1 else 0.5
        op("dve", lambda e: e.scalar_tensor_tensor(out=Gp[:], in0=Gp[:], scalar=gsc, in1=gpost[:],
                                                   op0=ALU.mult, op1=ALU.mult), reads=[Gpb], writes=[Gpb])
        evac_i = [0]
        for tb in range(T // TB):
            if FFN_PARTS < 0.15:
                break
            for m in range(4):
                t = tb * 4 + m
                dma("sp", xin[m][:], xsrc[t * 128:(t + 1) * 128, :], reads=[xsrc_bufs[t]], writes=[xinb[m]])
                if add_pos:
                    dma("sp", xn[:], pos_in[t * 128:(t + 1) * 128, :], writes=[xnb])
                    op("dve", lambda e, m=m: e.tensor_tensor(out=xin[m][:], in0=xin[m][:], in1=xn[:], op=ALU.add),
                       reads=[xnb, xinb[m]], writes=[xinb[m]])
                sb_ = stb[m]
                c0 = m * 4
                op("act", lambda e, m=m, c0=c0: e.activation(out=junk[:], in_=xin[m][:], func=AF.Square,
                                                            accum_out=st[:, c0:c0 + 1]),
                   reads=[xinb[m]], writes=[junkb, sb_])
                if FFN_PARTS < 0.25:
                    continue
                rstd_from_ss(st[:, c0:c0 + 1], st[:, c0 + 1:c0 + 2], st[:, c0 + 2:c0 + 3], sb_, D)
                op("dve", lambda e, m=m, c0=c0: e.tensor_scalar(out=xn[:], in0=xin[m][:], scalar1=st[:, c0 + 1:c0 + 2],
                                                               scalar2=None, op0=ALU.mult),
                   reads=[xinb[m], sb_], writes=[xnb])
                for q in range(4):
                    if FFN_PARTS < 0.35:
                        break
                    bk = q
                    for kk in range(4):
                        kc = q * 4 + kk
                        op("pe", lambda e, kc=kc, kk=kk, bk=bk: e.transpose(
                            out=PS[bk][:, kk * 128:(kk + 1) * 128], in_=xn[:, kc * 128:(kc + 1) * 128], identity=ident[:]),
                           reads=[xnb, identb], writes=[PSB[bk]], inc=(kk == 3))
                    for kk in range(4):
                        if FFN_PARTS < 0.45:
                            break
                        kc = q * 4 + kk
                        src = PS[bk][:, kk * 128:(kk + 1) * 128]
                        dst = hT[:, kc, m * 128:(m + 1) * 128]
                        a_ap = Acol[:, si * 16 + kc:si * 16 + kc + 1]
                        b_ap = Bcol(si, kc)
                        if evac_i[0] % 2 == 0 and EVAC_ACT:
                            op("act", lambda e, src=src, dst=dst, a_ap=a_ap, b_ap=b_ap: e.activation(
                                out=dst, in_=src, func=AF.Identity, scale=a_ap, bias=b_ap),
                               reads=[PSB[bk], colb], writes=[hTb[kc][m]])
                        else:
                            op("dve", lambda e, src=src, dst=dst, a_ap=a_ap, b_ap=b_ap: e.tensor_scalar(
                                out=dst, in0=src, scalar1=a_ap, scalar2=b_ap, op0=ALU.mult, op1=ALU.add),
                               reads=[PSB[bk], colb], writes=[hTb[kc][m]])
                        evac_i[0] += 1
            if FFN_PARTS < 2:
                break
            for jp in range(22):
                s = jp % 2
                dma("pool", wg_s[s][:], ffn_wg[layer, :, jp * 256:(jp + 1) * 256].rearrange("(kc p) n -> p kc n", p=128),
                    writes=[wgb[s]])
                dma("pool", wu_s[s][:], ffn_wu[layer, :, jp * 256:(jp + 1) * 256].rearrange("(kc p) n -> p kc n", p=128),
                    writes=[wub[s]])
                for jj in range(2):
                    j = jp * 2 + jj
                    gb, ub = 4 + (j % 2), 6 + (j % 2)
                    for (wt, wb, bk) in ((wg_s[s], wgb[s], gb), (wu_s[s], wub[s], ub)):
                        for kc in range(16):
                            op("pe", lambda e, wt=wt, bk=bk, kc=kc, jj=jj: e.matmul(
                                PS[bk][:, :], lhsT=wt[:, kc, jj * 128:(jj + 1) * 128], rhs=hT[:, kc, :],
                                start=(kc == 0), stop=(kc == 15)),
                               reads=[wb] + hTb[kc], writes=[PSB[bk]], inc=(kc == 15))
                    ss_ = j % 2
                    op("act", lambda e, ss_=ss_, gb=gb: e.activation(out=sg[ss_][:], in_=PS[gb][:, :], func=AF.Silu),
                       reads=[PSB[gb]], writes=[sgb[ss_]])
                    op("dve", lambda e, ss_=ss_, ub=ub, j=j: e.tensor_tensor(out=aT[:, j, :], in0=sg[ss_][:], in1=PS[ub][:, :],
                                                                          op=ALU.mult),
                       reads=[sgb[ss_], PSB[ub]], writes=[aTb[j]])
            if FFN_PARTS < 3:
                break
            for half in range(2):
                for jp in range(22):
                    s = jp % 2
                    dma("pool", wd_s[s][:],
                        ffn_wd[layer, jp * 256:(jp + 1) * 256, half * 1024:(half + 1) * 1024].rearrange("(jj p) n -> p jj n", p=128),
                        writes=[wdb[s]])
                    for jj in range(2):
                        j = jp * 2 + jj
                        for m in range(4):
                            for nb in range(2):
                                bk = m * 2 + nb
                                op("pe", lambda e, j=j, jj=jj, m=m, nb=nb, bk=bk, s=s: e.matmul(
                                    PS[bk][:, :], lhsT=aT[:, j, m * 128:(m + 1) * 128],
                                    rhs=wd_s[s][:, jj, nb * 512:(nb + 1) * 512], start=(j == 0), stop=(j == 43)),
                                   reads=[aTb[j], wdb[s]], writes=[PSB[bk]], inc=(j == 43 or (jj == 1 and m == 3 and nb == 1)))
                if half == 0:
                    for m in range(4):
                        for nb in range(2):
                            bk = m * 2 + nb
                            eng = "act" if nb == 0 else "dve"
                            if eng == "act":
                                op("act", lambda e, m=m, nb=nb, bk=bk: e.activation(
                                    out=yacc[:, m, nb * 512:(nb + 1) * 512], in_=PS[bk][:, :], func=AF.Identity),
                                   reads=[PSB[bk]], writes=[yaccb[m]])
                            else:
                                op("dve", lambda e, m=m, nb=nb, bk=bk: e.tensor_copy(
                                    out=yacc[:, m, nb * 512:(nb + 1) * 512], in_=PS[bk][:, :]),
                                   reads=[PSB[bk]], writes=[yaccb[m]])
            if FFN_PARTS < 4:
                break
            for m in range(4):
                t = tb * 4 + m
                sb_ = stb[4 + m]
                c0 = 16 + m * 4
                op("act", lambda e, m=m, c0=c0: e.activation(out=junk[:, 0:1024], in_=yacc[:, m, :], func=AF.Square,
                                                            accum_out=st[:, c0:c0 + 1]),
                   reads=[yaccb[m]], writes=[junkb, sb_])
                for nb in range(2):
                    bk = m * 2 + nb
                    op("act", lambda e, nb=nb, bk=bk, c0=c0: e.activation(out=junk[:, 0:512], in_=PS[bk][:, :], func=AF.Square,
                                                                        accum_out=st[:, c0 + 1 + nb:c0 + 2 + nb]),
                       reads=[PSB[bk]], writes=[junkb, sb_])
                op("dve", lambda e, c0=c0: e.tensor_tensor(out=st[:, c0:c0 + 1], in0=st[:, c0:c0 + 1], in1=st[:, c0 + 1:c0 + 2],
                                                          op=ALU.add), reads=[sb_], writes=[sb_])
                op("dve", lambda e, c0=c0: e.tensor_tensor(out=st[:, c0:c0 + 1], in0=st[:, c0:c0 + 1], in1=st[:, c0 + 2:c0 + 3],
                                                          op=ALU.add), reads=[sb_], writes=[sb_])
                rstd_from_ss(st[:, c0:c0 + 1], st[:, c0 + 3:c0 + 4], st[:, c0 + 1:c0 + 2], sb_, D)
                rs = st[:, c0 + 3:c0 + 4]
                for q in range(4):
                    ts_ = q % 2
                    cols = slice(q * 512, (q + 1) * 512)
                    if q < 2:
                        src, srcb = yacc[:, m, cols], yaccb[m]
                    else:
                        bk = m * 2 + (q - 2)
                        src, srcb = PS[bk][:, :], PSB[bk]
                    op("dve", lambda e, src=src, ts_=ts_, cols=cols: e.scalar_tensor_tensor(
                        out=tmp[ts_][:], in0=src, scalar=rs, in1=Gp[:, cols], op0=ALU.mult, op1=ALU.mult),
                       reads=[srcb, sb_, Gpb], writes=[tmpb[ts_]])
                    op("dve", lambda e, ts_=ts_, cols=cols, m=m: e.tensor_tensor(
                        out=xin[m][:, cols], in0=tmp[ts_][:], in1=xin[m][:, cols], op=ALU.add),
                       reads=[tmpb[ts_], xinb[m]], writes=[xinb[m]])
                dma("sp", xdst[t * 128:(t + 1) * 128, :], xin[m][:], reads=[xinb[m]], writes=[ybuf[t]])
        kb.release(mk)

    def mixer_stage():
        si = 1
        mk = kb.mark()
        hTa = kb.sb("hTa", [128, 16, T], BF16)
        hTab = [[Buf() for _ in range(NT)] for _ in range(16)]
        amk = kb.mark()
        xin = [kb.sb("mxin%d" % i, [128, D], F32) for i in range(2)]
        xinb = bufs(2, "mxin")
        xn = kb.sb("mxn", [128, D], F32)
        xnb = Buf("mxn")
        junk = kb.sb("mjunk", [128, D], BF16)
        junkb = Buf("mjunk")
        st = kb.sb("mst", [128, 64], F32)
        stb = bufs(16, "mst")
        for t in range(NT):
            s = t % 2
            dma("sp", xin[s][:], xres[t * 128:(t + 1) * 128, :], reads=[ybuf[t]], writes=[xinb[s]])
            sb_ = stb[t]
            c0 = t * 4
            op("act", lambda e: e.activation(out=junk[:], in_=xin[s][:], func=AF.Square, accum_out=st[:, c0:c0 + 1]),
               reads=[xinb[s]], writes=[junkb, sb_])
            rstd_from_ss(st[:, c0:c0 + 1], st[:, c0 + 1:c0 + 2], st[:, c0 + 2:c0 + 3], sb_, D)
            op("dve", lambda e: e.tensor_scalar(out=xn[:], in0=xin[s][:], scalar1=st[:, c0 + 1:c0 + 2], scalar2=None,
                                                op0=ALU.mult), reads=[xinb[s], sb_], writes=[xnb])
            for q in range(4):
                bk = q
                for kk in range(4):
                    kc = q * 4 + kk
                    op("pe", lambda e: e.transpose(out=PS[bk][:, kk * 128:(kk + 1) * 128], in_=xn[:, kc * 128:(kc + 1) * 128],
                                                   identity=ident[:]), reads=[xnb, identb], writes=[PSB[bk]], inc=(kk == 3))
                for kk in range(4):
                    kc = q * 4 + kk
                    op("dve", lambda e: e.tensor_scalar(out=hTa[:, kc, t * 128:(t + 1) * 128], in0=PS[bk][:, kk * 128:(kk + 1) * 128],
                                                        scalar1=Acol[:, si * 16 + kc:si * 16 + kc + 1], scalar2=Bcol(si, kc),
                                                        op0=ALU.mult, op1=ALU.add),
                       reads=[PSB[bk], colb], writes=[hTab[kc][t]])

        kb.release(amk)
        chk(1)
        wi = [kb.sb("wi%d" % i, [128, 16, 128], BF16) for i in range(2)]
        wib = bufs(2, "wi")
        wi_n = [0]

        def proj_chunk(c0, ncol=128):
            s = wi_n[0] % 2
            wi_n[0] += 1
            dma("pool", wi[s][:, :, 0:ncol], w_in[:, c0:c0 + ncol].rearrange("(kc p) n -> p kc n", p=128), writes=[wib[s]])
            for nb in range(4):
                for kc in range(16):
                    op("pe", lambda e: e.matmul(PS[nb][0:ncol, :], lhsT=wi[s][:, kc, 0:ncol], rhs=hTa[:, kc, nb * 512:(nb + 1) * 512],
                                                start=(kc == 0), stop=(kc == 15)),
                       reads=[wib[s]] + hTab[kc][nb * 4:(nb + 1) * 4], writes=[PSB[nb]], inc=(kc == 15))

        PSlo = PSALL[:, 0:2048]
        PSloB = PSB[0:4]

        op("dve", lambda e: e.memset(hyss[:], 0.0), writes=[hyssb])
        PI = math.pi
        hm = kb.mark()
        ones_h = kb.sb("ones_h", [128, 128], F32)
        hcb = Buf("hconst")
        op("dve", lambda e: e.memset(ones_h[:], 1.0), writes=[hcb])
        hcw = kb.sb("hcw", [128, 3, 24], F32)
        bmask_h = kb.sb("bmask_h", [128, 7], F32)
        hcol = kb.sb("hcol", [128, 64], F32)
        nrep = kb.sb("nrep", [128, 1], F32)
        with nc.allow_non_contiguous_dma(reason="small vector relayout"):
            dma("sp", hcw[:], hy_conv_w.rearrange("j (c p) -> p j c", p=128), writes=[hcb])
        dma("sp", bmask_h[:], bmask_in[0, :].partition_broadcast(128), writes=[hcb])
        dma("sp", hcol[:], hcol_in[:, :], writes=[hcb])
        dma("sp", nrep[:], nrep_in[:, :], writes=[hcb])
        t7h = kb.sb("t7h", [128, 16], F32)
        t7hb = Buf("t7h")

        invm = kb.mark()
        invn = kb.sb("invn", [128, 2048], F32)
        invb = Buf("invn")
        h1m = kb.mark()
        featsT = kb.sb("featsT", [33, T], F32)
        w1s = kb.sb("w1s", [33, 64], F32)
        w2s = kb.sb("w2s", [64, 64], F32)
        w3s = kb.sb("w3s", [64, 4096], F32)
        bcs = kb.sb("bcs", [64, 2], F32)
        fb = Buf("filt")
        dma("sp", featsT[:], feats_in[:, :], writes=[fb])
        dma("sp", w1s[:], hy_f_w1[:, :], writes=[fb])
        dma("sp", w2s[:], hy_f_w2[:, :], writes=[fb])
        dma("sp", w3s[:], hy_f_w3[:, :], writes=[fb])
        with nc.allow_non_contiguous_dma(reason="small vector relayout"):
            dma("sp", bcs[:, 0:1], hy_f_b1.rearrange("o p -> p o"), writes=[fb])
            dma("sp", bcs[:, 1:2], hy_f_b2.rearrange("o p -> p o"), writes=[fb])
        h1T = kb.sb("h1T", [64, T], F32)
        h2T = kb.sb("h2T", [64, T], F32)
        h1b, h2b = Buf("h1T"), Buf("h2T")
        rw1 = kb.sb("rw1", [64, T], F32)
        rw2 = kb.sb("rw2", [64, T], F32)
        rwb = Buf("rw")
        for (src, srcb, wts, kdim, dst, dstb, bi) in ((featsT, fb, w1s, 33, h1T, h1b, 0), (h1T, h1b, w2s, 64, h2T, h2b, 1)):
            for nb in range(4):
                sl = slice(nb * 512, (nb + 1) * 512)
                op("pe", lambda e: e.matmul(PS[nb][0:64, :], lhsT=wts[0:kdim, :], rhs=src[0:kdim, sl], start=True, stop=True),
                   reads=[srcb, fb], writes=[PSB[nb]])
                op("dve", lambda e: e.tensor_scalar(out=dst[:, sl], in0=PS[nb][0:64, :], scalar1=bcs[:, bi:bi + 1], scalar2=None, op0=ALU.add),
                   reads=[PSB[nb], fb], writes=[dstb])
            op("dve", lambda e: e.tensor_scalar(out=rw1[:], in0=dst[:], scalar1=PI, scalar2=-2 * PI, op0=ALU.is_gt, op1=ALU.mult),
               reads=[dstb], writes=[rwb])
            op("dve", lambda e: e.tensor_scalar(out=rw2[:], in0=dst[:], scalar1=-PI, scalar2=2 * PI, op0=ALU.is_lt, op1=ALU.mult),
               reads=[dstb], writes=[rwb])
            op("dve", lambda e: e.tensor_tensor(out=dst[:], in0=dst[:], in1=rw1[:], op=ALU.add), reads=[dstb, rwb], writes=[dstb])
            op("dve", lambda e: e.tensor_tensor(out=dst[:], in0=dst[:], in1=rw2[:], op=ALU.add), reads=[dstb, rwb], writes=[dstb])
            op("dve", lambda e: e.tensor_scalar(out=dst[:], in0=dst[:], scalar1=-PI, scalar2=PI, op0=ALU.max, op1=ALU.min),
               reads=[dstb], writes=[dstb])
            op("act", lambda e: e.activation(out=dst[:], in_=dst[:], func=AF.Sin), reads=[dstb], writes=[dstb])
        absdec = kb.sb("absdec", [128, 2048], F32)
        adb = Buf("absdec")
        dma("sp", absdec[:], hy_decay.rearrange("o c -> (o c)").partition_broadcast(128), writes=[adb])
        op("act", lambda e: e.activation(out=absdec[:], in_=absdec[:], func=AF.Abs), reads=[adb], writes=[adb])
        absacc = kb.sb("absacc", [128, 2048], F32)
        aab = bufs(4, "absacc")
        op("dve", lambda e: e.memset(absacc[:], 0.0), writes=aab)
        wn = kb.sb("wn", [128, 2048], F32)
        wnb = Buf("wn")
        tA = [kb.sb("tA%d" % i, [128, 512], F32) for i in range(2)]
        tB = [kb.sb("tB%d" % i, [128, 512], F32) for i in range(2)]
        tAb, tBb = bufs(2, "tA"), bufs(2, "tB")
        tS = [kb.sb("tS%d" % i, [128, 2048], BF16) for i in range(2)]
        tD = [kb.sb("tD%d" % i, [128, 2048], BF16) for i in range(2)]
        tSb, tDb = bufs(2, "tS"), bufs(2, "tD")
        for lt in range(16):
            op("dve", lambda e: e.tensor_scalar(out=wn[:], in0=absdec[:], scalar1=hcol[:, lt:lt + 1], scalar2=None, op0=ALU.mult),
               reads=[adb, hcb], writes=[wnb])
            op("act", lambda e: e.activation(out=wn[:], in_=wn[:], func=AF.Exp), reads=[wnb], writes=[wnb])
            op("dve", lambda e: e.tensor_scalar(out=wn[:], in0=wn[:], scalar1=0.05, scalar2=None, op0=ALU.add), reads=[wnb], writes=[wnb])
            s2 = lt % 2
            for cb4 in range(4):
                sl = slice(cb4 * 512, (cb4 + 1) * 512)
                k_ = cb4 % 2
                pa, pb = 4 + 2 * k_, 5 + 2 * k_
                op("pe", lambda e: e.matmul(PS[pa][:, :], lhsT=h2T[:, lt * 128:(lt + 1) * 128], rhs=w3s[:, cb4 * 512:(cb4 + 1) * 512],
                                            start=True, stop=True), reads=[h2b, fb], writes=[PSB[pa]])
                op("pe", lambda e: e.matmul(PS[pb][:, :], lhsT=h2T[:, lt * 128:(lt + 1) * 128], rhs=w3s[:, 2048 + cb4 * 512:2048 + (cb4 + 1) * 512],
                                            start=True, stop=True), reads=[h2b, fb], writes=[PSB[pb]])
                op("dve", lambda e: e.tensor_tensor(out=tA[k_][:], in0=PS[pa][:, :], in1=wn[:, sl], op=ALU.mult),
                   reads=[PSB[pa], wnb], writes=[tAb[k_]])
                op("dve", lambda e: e.scalar_tensor_tensor(out=tB[k_][:], in0=PS[pb][:, :], scalar=hcol[:, 16 + lt:17 + lt], in1=wn[:, sl],
                                                           op0=ALU.mult, op1=ALU.mult), reads=[PSB[pb], wnb, hcb], writes=[tBb[k_]])
                op("pool", lambda e: e.tensor_tensor(out=tS[s2][:, sl], in0=tA[k_][:], in1=tB[k_][:], op=ALU.add),
                   reads=[tAb[k_], tBb[k_]], writes=[tSb[s2]])
                op("pool", lambda e: e.tensor_tensor(out=tD[s2][:, sl], in0=tA[k_][:], in1=tB[k_][:], op=ALU.subtract),
                   reads=[tAb[k_], tBb[k_]], writes=[tDb[s2]])
                op("act", lambda e: e.activation(out=tA[k_][:], in_=tA[k_][:], func=AF.Abs), reads=[tAb[k_]], writes=[tAb[k_]])
                op("act", lambda e: e.activation(out=tB[k_][:], in_=tB[k_][:], func=AF.Abs), reads=[tBb[k_]], writes=[tBb[k_]])
                op("pool", lambda e: e.tensor_tensor(out=absacc[:, sl], in0=absacc[:, sl], in1=tA[k_][:], op=ALU.add),
                   reads=[tAb[k_], aab[cb4]], writes=[aab[cb4]])
                op("pool", lambda e: e.tensor_tensor(out=absacc[:, sl], in0=absacc[:, sl], in1=tB[k_][:], op=ALU.add),
                   reads=[tBb[k_], aab[cb4]], writes=[aab[cb4]])
            dma("sp", tapsS[lt * 128:(lt + 1) * 128, :], tS[s2][:], reads=[tSb[s2]], writes=[tapsb])
            dma("sp", tapsD[lt * 128:(lt + 1) * 128, :], tD[s2][:], reads=[tDb[s2]], writes=[tapsb])
        for cb4 in range(4):
            sl = slice(cb4 * 512, (cb4 + 1) * 512)
            op("pe", lambda e: e.matmul(PS[cb4][:, :], lhsT=ones_h[:], rhs=absacc[:, sl], start=True, stop=True),
               reads=[aab[cb4], hcb], writes=[PSB[cb4]])
            op("dve", lambda e: e.reciprocal(out=invn[:, sl], in_=PS[cb4][:, :]), reads=[PSB[cb4]], writes=[invb])
        op("dve", lambda e: e.tensor_scalar(out=invn[:], in0=invn[:], scalar1=nrep[:, 0:1], scalar2=None, op0=ALU.mult),
           reads=[invb, hcb], writes=[invb])
        kb.release(h1m)
        h1m = kb.mark()
        tSk = kb.sb("tSk", [128, 16, 512], BF16)
        tDk = kb.sb("tDk", [128, 16, 512], BF16)
        tkb = Buf("tk")
        dfa = [kb.sb("dfa%d" % i, [128, 16, 128], BF16) for i in range(2)]
        dfb = [kb.sb("dfb%d" % i, [128, 16, 128], BF16) for i in range(2)]
        dfab, dfbb = bufs(2, "dfa"), bufs(2, "dfb")
        sp_t = [kb.sb("sp_t%d" % i, [128, 512], F32) for i in range(8)]
        sp_b = bufs(8, "sp_t")
        for cb4 in range(4):
            sl = slice(cb4 * 512, (cb4 + 1) * 512)
            dma("sp", tSk[:], tapsS[:, sl].rearrange("(lt p) n -> p lt n", p=128), reads=[tapsb], writes=[tkb])
            dma("sp", tDk[:], tapsD[:, sl].rearrange("(lt p) n -> p lt n", p=128), reads=[tapsb], writes=[tkb])
            for ft in range(16):
                s2 = ft % 2
                dma("sp", dfa[s2][:], CF_t[ft], writes=[dfab[s2]])
                dma("sp", dfb[s2][:], SF_t[ft], writes=[dfbb[s2]])
                b0 = 4 * s2
                for lt in range(16):
                    op("pe", lambda e: e.matmul(PS[b0][:, :], lhsT=dfa[s2][:, lt, :], rhs=tSk[:, lt, :], start=(lt == 0), stop=(lt == 15)),
                       reads=[dfab[s2], tkb], writes=[PSB[b0]], inc=(lt == 15))
                for lt in range(16):
                    op("pe", lambda e: e.matmul(PS[b0 + 1][:, :], lhsT=dfb[s2][:, lt, :], rhs=tDk[:, lt, :], start=(lt == 0), stop=(lt == 15)),
                       reads=[dfbb[s2], tkb], writes=[PSB[b0 + 1]], inc=(lt == 15))
                if ft % 2 == 0:
                    for lt in range(16):
                        op("pe", lambda e: e.matmul(PS[b0 + 2][:, :], lhsT=dfb[s2][:, lt, :], rhs=tSk[:, lt, :], start=(lt == 0), stop=(lt == 15)),
                           reads=[dfbb[s2], tkb], writes=[PSB[b0 + 2]], inc=(lt == 15))
                o4 = 4 * s2
                sre, sim, nq, srb = sp_t[o4], sp_t[o4 + 1], sp_t[o4 + 2], sp_t[o4 + 3]
                op("dve", lambda e: e.tensor_tensor(out=sre[:], in0=PS[b0][:, :], in1=invn[:, sl], op=ALU.mult),
                   reads=[PSB[b0], invb], writes=[sp_b[o4]])
                op("dve", lambda e: e.scalar_tensor_tensor(out=sim[:], in0=PS[b0 + 1][:, :], scalar=hcol[:, 48 + ft:49 + ft], in1=invn[:, sl],
                                                           op0=ALU.mult, op1=ALU.mult), reads=[PSB[b0 + 1], invb, hcb], writes=[sp_b[o4 + 1]])
                dma("sp", spec_re[ft * 128:(ft + 1) * 128, sl], sre[:], reads=[sp_b[o4]], writes=[specb])
                dma("sp", spec_im[ft * 128:(ft + 1) * 128, sl], sim[:], reads=[sp_b[o4 + 1]], writes=[specb])
                if ft % 2 == 0:
                    op("dve", lambda e: e.tensor_tensor(out=nq[:], in0=PS[b0 + 2][:, :], in1=invn[:, sl], op=ALU.mult),
                       reads=[PSB[b0 + 2], invb], writes=[sp_b[o4 + 2]])
                    op("pool", lambda e: e.tensor_tensor(out=nq[:], in0=nq[:], in1=sre[:], op=ALU.subtract),
                       reads=[sp_b[o4 + 2], sp_b[o4]], writes=[sp_b[o4 + 2]])
                    op("dve", lambda e: e.scalar_tensor_tensor(out=srb[:], in0=nq[:], scalar=hcol[:, 32 + ft:33 + ft], in1=sre[:],
                                                               op0=ALU.mult, op1=ALU.add), reads=[sp_b[o4 + 2], sp_b[o4], hcb], writes=[sp_b[o4 + 3]])
                    dma("sp", spec_rb[ft * 128:(ft + 1) * 128, sl], srb[:], reads=[sp_b[o4 + 3]], writes=[specb])
                else:
                    dma("sp", spec_rb[ft * 128:(ft + 1) * 128, sl], sre[:], reads=[sp_b[o4]], writes=[specb])
        kb.release(h1m)
        kb.release(invm)
        chk(20)

        CW = 256
        u32 = kb.sb("u32", [128, 16, CW], F32)
        ubf = kb.sb("ubf", [128, 16, CW], BF16)
        x1t = kb.sb("x1t", [128, 16, CW], F32)
        x2t = kb.sb("x2t", [128, 16, CW], F32)
        u32b, ubfb, x1tb, x2tb = Buf("u32"), Buf("ubf"), Buf("x1t"), Buf("x2t")
        sp3 = [kb.sb("sp3_%d" % i, [128, 3, CW], F32) for i in range(2)]
        sp3b = bufs(2, "sp3")
        Yre = kb.sb("Yre", [128, 16, CW], BF16)
        Yim = kb.sb("Yim", [128, 16, CW], BF16)
        Yb = Buf("Yf")
        dfa = [kb.sb("dga%d" % i, [128, 16, 128], BF16) for i in range(3)]
        dfb = [kb.sb("dgb%d" % i, [128, 16, 128], BF16) for i in range(3)]
        dfab, dfbb = bufs(3, "dga"), bufs(3, "dgb")
        tq = [kb.sb("tq%d" % i, [128, CW], F32) for i in range(8)]
        tqb = bufs(8, "tq")
        dbt = kb.sb("dbt", [128, CW], F32)
        hnt = kb.sb("hnt", [128, CW], F32)
        dbb = Buf("dbt")
        ztok = kb.sb("ztok", [128, 16, CW], F32)
        ztb_ = Buf("ztok")
        fmt = ztok[:, 0:8, :].rearrange("p a b -> p (a b)")
        fmtb = ztb_
        yoh = kb.sb("yoh", [128, T], BF16)
        yohb = Buf("yoh")
        ssq = kb.sb("ssq", [128, 2], F32)
        ssqb = Buf("ssq")
        jk = kb.sb("jk", [128, CW], BF16)
        jkb = Buf("jk")

        def conv3(dst, dstb, ci):
            w0, w1, w2 = (hcw[:, j, ci:ci + 1] for j in range(3))
            op("dve", lambda e: e.tensor_scalar(out=dst[:], in0=PSlo, scalar1=w1, scalar2=None, op0=ALU.mult),
               reads=PSloB + [hcb], writes=[dstb])
            op("dve", lambda e: e.scalar_tensor_tensor(out=dst[:, 1:T], in0=PSALL[:, 0:T - 1], scalar=w0, in1=dst[:, 1:T],
                                                       op0=ALU.mult, op1=ALU.add), reads=PSloB + [hcb], writes=[dstb])
            op("dve", lambda e: e.scalar_tensor_tensor(out=dst[:, 0:T - 1], in0=PSALL[:, 1:T], scalar=w2, in1=dst[:, 0:T - 1],
                                                       op0=ALU.mult, op1=ALU.add), reads=PSloB + [hcb], writes=[dstb])
            d3 = dst.rearrange("p (s t) -> p s t", t=256)
            p3 = PSlo.rearrange("p (s t) -> p s t", t=256)
            op("dve", lambda e: e.scalar_tensor_tensor(out=t7h[:, 0:7], in0=p3[:, 0:7, 255], scalar=w0, in1=bmask_h[:], op0=ALU.mult,
                                                       op1=ALU.mult), reads=PSloB + [hcb], writes=[t7hb])
            op("dve", lambda e: e.tensor_tensor(out=d3[:, 1:8, 0], in0=d3[:, 1:8, 0], in1=t7h[:, 0:7], op=ALU.subtract),
               reads=[t7hb], writes=[dstb])
            op("dve", lambda e: e.scalar_tensor_tensor(out=t7h[:, 8:15], in0=p3[:, 1:8, 0], scalar=w2, in1=bmask_h[:], op0=ALU.mult,
                                                       op1=ALU.mult), reads=PSloB + [hcb], writes=[t7hb])
            op("dve", lambda e: e.tensor_tensor(out=d3[:, 0:7, 255], in0=d3[:, 0:7, 255], in1=t7h[:, 8:15], op=ALU.subtract),
               reads=[t7hb], writes=[dstb])

        for cgp in range(0 if not SKIP_H2 else 4, 4):
            for g2 in range(2):
                cg = 2 * cgp + g2
                gs = slice(g2 * 128, (g2 + 1) * 128)
                for (which, dst, dstb) in ((0, u32, u32b), (1, x1t, x1tb), (2, x2t, x2tb)):
                    proj_chunk(which * 1024 + cg * 128)
                    conv3(fmt, fmtb, which * 8 + cg)
                    for q in range(4):
                        pbk = 4 + q
                        for kk in range(4):
                            tt = q * 4 + kk
                            op("pe", lambda e: e.transpose(out=PS[pbk][:, kk * 128:(kk + 1) * 128], in_=fmt[:, tt * 128:(tt + 1) * 128],
                                                           identity=ident[:]), reads=[fmtb, identb], writes=[PSB[pbk]], inc=(kk == 3))
                        src3 = PS[pbk].rearrange("p (a b) -> p a b", b=128)
                        op("act", lambda e: e.activation(out=dst[:, q * 4:(q + 1) * 4, gs], in_=src3, func=AF.Copy),
                           reads=[PSB[pbk]], writes=[dstb])
                        if which == 0:
                            op("act", lambda e: e.activation(out=ubf[:, q * 4:(q + 1) * 4, gs], in_=src3, func=AF.Copy),
                               reads=[PSB[pbk]], writes=[ubfb])
            csl0 = slice(cgp * CW, (cgp + 1) * CW)
            dma("sp", hnt[:], hy_out_norm[0, csl0].partition_broadcast(128), writes=[dbb])
            for o in range(2):
                csl = slice(o * 1024 + cgp * CW, o * 1024 + (cgp + 1) * CW)
                dma("sp", dbt[:], hy_bias[o, csl0].partition_broadcast(128), writes=[dbb])
                xg, xgb = (x1t, x1tb) if o == 0 else (x2t, x2tb)
                for ft in range(16):
                    s2 = ft % 2
                    s3 = ft % 3
                    dma("sp", dfa[s3][:], CF_t[ft], writes=[dfab[s3]])
                    dma("sp", dfb[s3][:], SF_t[ft], writes=[dfbb[s3]])
                    fsl = slice(ft * 128, (ft + 1) * 128)
                    dma("sp", sp3[s2][:, 0, :], spec_re[fsl, csl], reads=[specb], writes=[sp3b[s2]])
                    dma("sp", sp3[s2][:, 1, :], spec_im[fsl, csl], reads=[specb], writes=[sp3b[s2]])
                    dma("sp", sp3[s2][:, 2, :], spec_rb[fsl, csl], reads=[specb], writes=[sp3b[s2]])
                    pk = s2
                    for lt in range(16):
                        op("pe", lambda e: e.matmul(PS[pk][:, 0:CW], lhsT=dfa[s3][:, lt, :], rhs=ubf[:, lt, :], start=(lt == 0), stop=(lt == 15)),
                           reads=[dfab[s3], ubfb], writes=[PSB[pk]], inc=(lt == 15))
                    for lt in range(16):
                        op("pe", lambda e: e.matmul(PS[2 + pk][:, 0:CW], lhsT=dfb[s3][:, lt, :], rhs=ubf[:, lt, :], start=(lt == 0), stop=(lt == 15)),
                           reads=[dfbb[s3], ubfb], writes=[PSB[2 + pk]], inc=(lt == 15))
                    q4 = 4 * s2
                    ure, uim = PS[pk][:, 0:CW], PS[2 + pk][:, 0:CW]
                    S_re, S_im, S_rb = sp3[s2][:, 0, :], sp3[s2][:, 1, :], sp3[s2][:, 2, :]
                    op("dve", lambda e: e.tensor_tensor(out=tq[q4][:], in0=ure, in1=S_re, op=ALU.mult), reads=[PSB[pk], sp3b[s2]], writes=[tqb[q4]])
                    op("dve", lambda e: e.tensor_tensor(out=tq[q4 + 1][:], in0=uim, in1=S_im, op=ALU.mult), reads=[PSB[2 + pk], sp3b[s2]], writes=[tqb[q4 + 1]])
                    op("dve", lambda e: e.tensor_tensor(out=tq[q4 + 2][:], in0=ure, in1=S_im, op=ALU.mult), reads=[PSB[pk], sp3b[s2]], writes=[tqb[q4 + 2]])
                    op("dve", lambda e: e.tensor_tensor(out=tq[q4 + 3][:], in0=uim, in1=S_rb, op=ALU.mult), reads=[PSB[2 + pk], sp3b[s2]], writes=[tqb[q4 + 3]])
                    op("pool", lambda e: e.tensor_tensor(out=Yre[:, ft, :], in0=tq[q4][:], in1=tq[q4 + 1][:], op=ALU.subtract),
                       reads=[tqb[q4], tqb[q4 + 1]], writes=[Yb])
                    op("pool", lambda e: e.tensor_tensor(out=Yim[:, ft, :], in0=tq[q4 + 2][:], in1=tq[q4 + 3][:], op=ALU.add),
                       reads=[tqb[q4 + 2], tqb[q4 + 3]], writes=[Yb])
                for tt in range(16):
                    s2 = tt % 2
                    s3 = (tt + 1) % 3
                    dma("sp", dfa[s3][:], GC_t[tt], writes=[dfab[s3]])
                    dma("sp", dfb[s3][:], GS_t[tt], writes=[dfbb[s3]])
                    pk = 4 + s2
                    for ft in range(16):
                        op("pe", lambda e: e.matmul(PS[pk][:, 0:CW], lhsT=dfa[s3][:, ft, :], rhs=Yre[:, ft, :], start=(ft == 0), stop=False),
                           reads=[dfab[s3], Yb], writes=[PSB[pk]], inc=False)
                        op("pe", lambda e: e.matmul(PS[pk][:, 0:CW], lhsT=dfb[s3][:, ft, :], rhs=Yim[:, ft, :], start=False, stop=(ft == 15)),
                           reads=[dfbb[s3], Yb], writes=[PSB[pk]], inc=(ft == 15))
                    q4 = 4 * s2
                    op("dve", lambda e: e.tensor_tensor(out=tq[q4][:], in0=u32[:, tt, :], in1=dbt[:], op=ALU.mult), reads=[u32b, dbb], writes=[tqb[q4]])
                    op("dve", lambda e: e.tensor_tensor(out=tq[q4][:], in0=tq[q4][:], in1=PS[pk][:, 0:CW], op=ALU.add), reads=[PSB[pk], tqb[q4]], writes=[tqb[q4]])
                    if o == 0:
                        op("dve", lambda e: e.tensor_tensor(out=u32[:, tt, :], in0=tq[q4][:], in1=xg[:, tt, :], op=ALU.mult),
                           reads=[tqb[q4], xgb, u32b], writes=[u32b])
                        op("pool", lambda e: e.tensor_copy(out=ubf[:, tt, :], in_=u32[:, tt, :]), reads=[u32b], writes=[ubfb])
                    else:
                        op("dve", lambda e: e.tensor_tensor(out=tq[q4 + 1][:], in0=tq[q4][:], in1=xg[:, tt, :], op=ALU.mult),
                           reads=[tqb[q4], xgb], writes=[tqb[q4 + 1]])
                        op("act", lambda e: e.activation(out=jk[:], in_=tq[q4 + 1][:], func=AF.Square, accum_out=ssq[:, 0:1]),
                           reads=[tqb[q4 + 1]], writes=[jkb, ssqb])
                        op("dve", lambda e: e.tensor_tensor(out=hyss[:, tt:tt + 1], in0=hyss[:, tt:tt + 1], in1=ssq[:, 0:1], op=ALU.add),
                           reads=[ssqb, hyssb], writes=[hyssb])
                        op("pool", lambda e: e.tensor_tensor(out=ztok[:, tt, :], in0=tq[q4 + 1][:], in1=hnt[:], op=ALU.mult),
                           reads=[tqb[q4 + 1], dbb], writes=[ztb_])
            for g2 in range(2):
                gs = slice(g2 * 128, (g2 + 1) * 128)
                for q in range(4):
                    pbk = q
                    for kk in range(4):
                        tt = q * 4 + kk
                        op("pe", lambda e: e.transpose(out=PS[pbk][:, kk * 128:(kk + 1) * 128], in_=ztok[:, tt, gs], identity=ident[:]),
                           reads=[ztb_, identb], writes=[PSB[pbk]], inc=(kk == 3))
                    op("act", lambda e: e.activation(out=yoh[:, q * 512:(q + 1) * 512], in_=PS[pbk][:, :], func=AF.Copy), reads=[PSB[pbk]], writes=[yohb])
                dma("sp", ycatT[2 * cgp + g2], yoh[:], reads=[yohb], writes=[ycb[2 * cgp + g2]])
        kb.release(hm)
        chk(2)
        gm = kb.mark()
        gmask = kb.sb("gmask", [128, 7, 128], F32)
        ones = kb.sb("ones", [128, 128], F32)
        cb = Buf("gconst")
        dma("sp", gmask[:], gmask_in.rearrange("m p f -> p m f"), writes=[cb])
        op("dve", lambda e: e.memset(ones[:], 1.0), writes=[cb])
        keepc = kb.sb("keepc", [128, 8], F32)
        bmask = kb.sb("bmask", [128, 7], F32)
        dma("sp", keepc[:], keep_in[0, :].partition_broadcast(128), writes=[cb])
        dma("sp", bmask[:], bmask_in[0, :].partition_broadcast(128), writes=[cb])
        gcw = kb.sb("gcw", [128, 3, 24], F32)
        hcw = kb.sb("hcw", [128, 3, 24], F32)
        with nc.allow_non_contiguous_dma(reason="small vector relayout"):
            dma("sp", gcw[:], gdn_conv_w.rearrange("j (c p) -> p j c", p=128), writes=[cb])
            dma("sp", hcw[:], hy_conv_w.rearrange("j (c p) -> p j c", p=128), writes=[cb])
        onorm = kb.sb("onorm", [128, 1], F32)
        with nc.allow_non_contiguous_dma(reason="small vector relayout"):
            dma("sp", onorm[:], gdn_o_norm.rearrange("o p -> p o"), writes=[cb])
        chk(3)
        wab = kb.sb("wab", [128, 16, 32], BF16)
        wabb = Buf("wab")
        dma("pool", wab[:], w_in[:, 7168:7200].rearrange("(kc p) n -> p kc n", p=128), writes=[wabb])
        for tt in range(NT):
            for kc in range(16):
                op("pe", lambda e: e.matmul(PS[4][:, tt * 32:(tt + 1) * 32], lhsT=hTa[:, kc, tt * 128:(tt + 1) * 128], rhs=wab[:, kc, :],
                                            start=(kc == 0), stop=(kc == 15)),
                   reads=[wabb, hTab[kc][tt]], writes=[PSB[4]], inc=(kc == 15))
        chk(4)
        W = 256
        gv = kb.sb("gv", [128, 12, W], F32)
        gvb = Buf("gv")
        GX, GG, BETA, NEA, GCUM, TOT, NBETA, BG, KD, DTB, ALOG = range(11)
        ab4 = PS[4].rearrange("p (t d k h) -> p t d k h", t=16, d=2, k=2, h=8)

        def v4(i):
            return gv[:, i, :].rearrange("p (t d h) -> p t d h", t=16, d=2, h=8)
        dma("sp", gv[:, DTB, :], dtb_in[0, :].partition_broadcast(128), writes=[gvb])
        dma("sp", gv[:, ALOG, :], alog_in[0, :].partition_broadcast(128), writes=[gvb])
        op("dve", lambda e: e.tensor_tensor(out=v4(GX), in0=ab4[:, :, :, 0, :], in1=v4(DTB), op=ALU.add), reads=[PSB[4], gvb], writes=[gvb])
        op("act", lambda e: e.activation(out=gv[:, GX, :], in_=gv[:, GX, :], func=AF.Exp), reads=[gvb], writes=[gvb])
        op("act", lambda e: e.activation(out=gv[:, GX, :], in_=gv[:, GX, :], func=AF.Ln, bias=1.0), reads=[gvb], writes=[gvb])
        op("act", lambda e: e.activation(out=gv[:, NEA, :], in_=gv[:, ALOG, :], func=AF.Exp), reads=[gvb], writes=[gvb])
        op("dve", lambda e: e.scalar_tensor_tensor(out=gv[:, GG, :], in0=gv[:, GX, :], scalar=-1.0, in1=gv[:, NEA, :],
                                                   op0=ALU.mult, op1=ALU.mult), reads=[gvb], writes=[gvb])
        op("act", lambda e: e.activation(out=v4(BETA), in_=ab4[:, :, :, 1, :], func=AF.Sigmoid), reads=[PSB[4]], writes=[gvb])
        chk(5)
        for tt in range(NT):
            for d in range(2):
                cs_ = (tt * 2 + d) * 8
                op("pe", lambda e: e.matmul(PS[5][:, cs_:cs_ + 8], lhsT=gmask[:, 1 if d == 0 else 3, :], rhs=gv[:, GG, cs_:cs_ + 8],
                                            start=True, stop=True), reads=[gvb, cb], writes=[PSB[5]], inc=(tt == NT - 1 and d == 1))
        op("pe", lambda e: e.matmul(PS[6][:, 0:W], lhsT=ones[:], rhs=gv[:, GG, :], start=True, stop=True), reads=[gvb, cb], writes=[PSB[6]])
        op("dve", lambda e: e.tensor_copy(out=gv[:, GCUM, :], in_=PS[5][:, 0:W]), reads=[PSB[5]], writes=[gvb])
        op("dve", lambda e: e.tensor_copy(out=gv[:, TOT, :], in_=PS[6][:, 0:W]), reads=[PSB[6]], writes=[gvb])
        op("dve", lambda e: e.tensor_scalar(out=gv[:, NBETA, :], in0=gv[:, BETA, :], scalar1=-1.0, scalar2=None, op0=ALU.mult),
           reads=[gvb], writes=[gvb])
        op("act", lambda e: e.activation(out=gv[:, BG, :], in_=gv[:, GCUM, :], func=AF.Exp), reads=[gvb], writes=[gvb])
        op("dve", lambda e: e.tensor_tensor(out=gv[:, BG, :], in0=gv[:, BG, :], in1=gv[:, BETA, :], op=ALU.mult), reads=[gvb], writes=[gvb])
        op("dve", lambda e: e.tensor_tensor(out=gv[:, KD, :], in0=gv[:, TOT, :], in1=gv[:, GCUM, :], op=ALU.subtract), reads=[gvb], writes=[gvb])
        op("act", lambda e: e.activation(out=gv[:, KD, :], in_=gv[:, KD, :], func=AF.Exp), reads=[gvb], writes=[gvb])

        chk(6)
        if DEBUG:
            dma("sp", dbg_gv[:, :], gv[:].rearrange("p a b -> p (a b)"), reads=[gvb], writes=[dbgb])

        def col(i, tt, d, h):
            c = (tt * 2 + d) * 8 + h
            return gv[:, i, c:c + 1]

        FM = [kb.sb("fm%d" % i, [128, T], F32) for i in range(6)]
        FMB = bufs(6, "fm")
        QT, KT, ZT, OA, SQ, VT = range(6)
        ktok = kb.sb("ktok", [128, 16, 128], F32)
        vtok = kb.sb("vtok", [128, 16, 128], F32)
        ktokb, vtokb = Buf("ktok"), Buf("vtok")
        tl2 = [kb.sb("tl%d" % i, [128, 38, 128], F32) for i in range(2)]
        tlb2 = [bufs(38, "tl%d_" % i) for i in range(2)]
        OAd = [FM[OA], kb.sb("oab", [128, T], F32)]
        OAdb = [Buf("oaf"), Buf("oab")]
        psl = {b_: bufs(4, "psl%d_" % b_) for b_ in range(8)}
        (DG, ND, MM, EG, ML, MU, NA, NTA, NB_, NTB, RTA, RTB, VB, KBG, UU, WT, ATT, QD, KDE, VN, SS, S2, T1, T2, T3, T4,
         NBD, NTBD, E1, E1T, E2, DD, PP, XX, DD2, DT2, RTF, T5) = range(38)
        yo = kb.sb("yo", [128, T], BF16)
        yob = Buf("yo")
        t7 = kb.sb("t7", [128, 16], F32)
        t7b = Buf("t7")

        def conv_silu(dst, dstb, wtile, ci, do_conv=True, do_silu=True):
            if not do_conv:
                op("act", lambda e: e.activation(out=dst[:], in_=PSlo, func=AF.Silu), reads=PSloB, writes=[dstb])
                return
            w0, w1, w2 = (wtile[:, j, ci:ci + 1] for j in range(3))
            op("dve", lambda e: e.tensor_scalar(out=dst[:], in0=PSlo, scalar1=w1, scalar2=None, op0=ALU.mult),
               reads=PSloB + [cb], writes=[dstb])
            op("dve", lambda e: e.scalar_tensor_tensor(out=dst[:, 1:T], in0=PSALL[:, 0:T - 1], scalar=w0, in1=dst[:, 1:T],
                                                       op0=ALU.mult, op1=ALU.add), reads=PSloB + [cb], writes=[dstb])
            op("dve", lambda e: e.scalar_tensor_tensor(out=dst[:, 0:T - 1], in0=PSALL[:, 1:T], scalar=w2, in1=dst[:, 0:T - 1],
                                                       op0=ALU.mult, op1=ALU.add), reads=PSloB + [cb], writes=[dstb])
            d3 = dst.rearrange("p (s t) -> p s t", t=256)
            p3 = PSlo.rearrange("p (s t) -> p s t", t=256)
            op("dve", lambda e: e.scalar_tensor_tensor(out=t7[:, 0:7], in0=p3[:, 0:7, 255], scalar=w0, in1=bmask[:], op0=ALU.mult,
                                                       op1=ALU.mult), reads=PSloB + [cb], writes=[t7b])
            op("dve", lambda e: e.tensor_tensor(out=d3[:, 1:8, 0], in0=d3[:, 1:8, 0], in1=t7[:, 0:7], op=ALU.subtract),
               reads=[t7b], writes=[dstb])
            op("dve", lambda e: e.scalar_tensor_tensor(out=t7[:, 8:15], in0=p3[:, 1:8, 0], scalar=w2, in1=bmask[:], op0=ALU.mult,
                                                       op1=ALU.mult), reads=PSloB + [cb], writes=[t7b])
            op("dve", lambda e: e.tensor_tensor(out=d3[:, 0:7, 255], in0=d3[:, 0:7, 255], in1=t7[:, 8:15], op=ALU.subtract),
               reads=[t7b], writes=[dstb])
            if do_silu:
                op("act", lambda e: e.activation(out=dst[:], in_=dst[:], func=AF.Silu), reads=[dstb], writes=[dstb])

        def partnorm(src, srcb, scale_const):
            op("dve", lambda e: e.tensor_tensor(out=FM[SQ][:], in0=src[:], in1=src[:], op=ALU.mult), reads=[srcb], writes=[FMB[SQ]])
            for nb in range(4):
                op("pe", lambda e: e.matmul(PS[4 + nb][:, :], lhsT=ones[:], rhs=FM[SQ][:, nb * 512:(nb + 1) * 512], start=True, stop=True),
                   reads=[FMB[SQ], cb], writes=[PSB[4 + nb]])
            return

        def partnorm_apply(src, srcb, nscale, eps, extra=None):
            for nb in range(4):
                sl = slice(nb * 512, (nb + 1) * 512)
                op("dve", lambda e: e.tensor_scalar(out=FM[SQ][:, sl], in0=PS[4 + nb][:, :], scalar1=nscale, scalar2=eps,
                                                    op0=ALU.mult, op1=ALU.add), reads=[PSB[4 + nb]], writes=[FMB[SQ]])
            op("act", lambda e: e.activation(out=FM[SQ][:], in_=FM[SQ][:], func=AF.Ln), reads=[FMB[SQ]], writes=[FMB[SQ]])
            op("act", lambda e: e.activation(out=FM[SQ][:], in_=FM[SQ][:], func=AF.Exp, scale=-0.5), reads=[FMB[SQ]], writes=[FMB[SQ]])
            op("dve", lambda e: e.tensor_tensor(out=src[:], in0=src[:], in1=FM[SQ][:], op=ALU.mult), reads=[FMB[SQ], srcb], writes=[srcb])

        def T_(i):
            return tl[:, i, :]

        for hd in range(8):
            for (ci, dsti, conv) in ((hd, QT, True), (8 + hd, KT, True), (16 + hd, VT, True), (None, ZT, False)):
                c0 = 3072 + ci * 128 if ci is not None else 6144 + hd * 128
                proj_chunk(c0)
                conv_silu(FM[dsti], FMB[dsti], gcw, ci if ci is not None else 0, do_conv=conv)
            chk(7)
            for (dsti, scl) in ((QT, 128.0 ** -0.5), (KT, 1.0)):
                partnorm(FM[dsti], FMB[dsti], None)
                partnorm_apply(FM[dsti], FMB[dsti], 1.0, EPS)
                if scl != 1.0:
                    op("dve", lambda e: e.tensor_scalar(out=FM[dsti][:], in0=FM[dsti][:], scalar1=scl, scalar2=None, op0=ALU.mult),
                       reads=[FMB[dsti]], writes=[FMB[dsti]])
            chk(8)
            for (srci, dst, dstb) in ((KT, ktok, ktokb), (VT, vtok, vtokb)):
                for q in range(4):
                    for kk in range(4):
                        tt = q * 4 + kk
                        op("pe", lambda e: e.transpose(out=PS[q][:, kk * 128:(kk + 1) * 128], in_=FM[srci][:, tt * 128:(tt + 1) * 128],
                                                       identity=ident[:]), reads=[FMB[srci], identb], writes=[PSB[q]], inc=(kk == 3))
                    op("dve", lambda e: e.tensor_copy(out=dst[:, q * 4:(q + 1) * 4, :].rearrange("p a b -> p (a b)"), in_=PS[q][:, :]),
                       reads=[PSB[q]], writes=[dstb])
            chk(9)
            def mm_(e, out, lhsT, rhs, **kw):
                if GDN_F32R:
                    lhsT, rhs = lhsT.bitcast(mybir.dt.float32r), rhs.bitcast(mybir.dt.float32r)
                return e.matmul(out, lhsT=lhsT, rhs=rhs, **kw)

            def gdn_dir(d):
                tl_, tlb = tl2[d], tlb2[d]
                T_ = lambda i: tl_[:, i, :]
                bx, by = 4 + 2 * d, 5 + 2 * d
                keys_ = ((4, 0), (5, 0), (5, 1), (6, 0), (6, 1), (7, 0), (7, 1))
                SL = {k_: ((k_[0] if d == 0 else k_[0] - 4), k_[1]) for k_ in keys_}
                def P_(ob, oslot):
                    b_, s_ = SL[(ob, oslot)]
                    return PS[b_][:, s_ * 128:(s_ + 1) * 128]
                def PB_(ob, oslot):
                    b_, s_ = SL[(ob, oslot)]
                    return psl[b_][s_]
                order = list(range(NT)) if d == 0 else list(range(NT - 1, -1, -1))
                mA, mB = (0, 1) if d == 0 else (2, 3)
                dma("sp", T_(SS), s0_in[d, hd], writes=[tlb[SS]])
                for ci_, c in enumerate(order):
                    tsl = slice(c * 128, (c + 1) * 128)
                    seg = c // 2
                    seg_start = (c % 2 == 0) if d == 0 else (c % 2 == 1)
                    seg_end = not seg_start
                    if seg_start and ci_ > 0:
                        kcol = keepc[:, seg:seg + 1] if d == 0 else keepc[:, seg + 1:seg + 2]
                        yield op("dve", lambda e: e.tensor_scalar(out=T_(SS), in0=T_(SS), scalar1=kcol, scalar2=None, op0=ALU.mult),
                           reads=[tlb[SS], cb], writes=[tlb[SS]])
                    gc = col(GCUM, c, d, hd)
                    yield op("dve", lambda e: e.tensor_scalar(out=T_(DG), in0=ident[:], scalar1=gc, scalar2=None, op0=ALU.mult),
                       reads=[identb, gvb], writes=[tlb[DG]])
                    yield op("pe", lambda e: mm_(e, P_(4, 0), lhsT=ones[:], rhs=T_(DG), start=True, stop=True),
                       reads=[tlb[DG], cb], writes=[PB_(4, 0)])
                    yield op("dve", lambda e: e.tensor_scalar(out=T_(ND), in0=P_(4, 0), scalar1=gc, scalar2=None, op0=ALU.subtract),
                       reads=[PB_(4, 0), gvb], writes=[tlb[ND]])
                    yield op("act", lambda e: e.activation(out=T_(ND), in_=T_(ND), func=AF.Abs), reads=[tlb[ND]], writes=[tlb[ND]])
                    yield op("act", lambda e: e.activation(out=T_(MM), in_=T_(ND), func=AF.Exp, scale=-1.0), reads=[tlb[ND]], writes=[tlb[MM]])
                    yield op("act", lambda e: e.activation(out=T_(EG), in_=P_(4, 0), func=AF.Exp), reads=[PB_(4, 0)], writes=[tlb[EG]])
                    yield op("dve", lambda e: e.tensor_tensor(out=T_(ML), in0=T_(MM), in1=gmask[:, mA, :], op=ALU.mult),
                       reads=[tlb[MM], cb], writes=[tlb[ML]])
                    yield op("dve", lambda e: e.tensor_tensor(out=T_(MU), in0=T_(MM), in1=gmask[:, mB, :], op=ALU.mult),
                       reads=[tlb[MM], cb], writes=[tlb[MU]])
                    chk(10)
                    yield op("pe", lambda e: mm_(e, P_(5, 0), lhsT=FM[KT][:, tsl], rhs=FM[KT][:, tsl], start=True, stop=True),
                       reads=[FMB[KT]], writes=[PB_(5, 0)])
                    yield op("dve", lambda e: e.scalar_tensor_tensor(out=T_(NA), in0=P_(5, 0), scalar=col(NBETA, c, d, hd), in1=T_(ML),
                                                               op0=ALU.mult, op1=ALU.mult), reads=[PB_(5, 0), gvb, tlb[ML]], writes=[tlb[NA]])
                    yield op("pe", lambda e: e.transpose(out=P_(5, 1), in_=T_(NA), identity=ident[:]), reads=[tlb[NA], identb], writes=[PB_(5, 1)])
                    yield op("dve", lambda e: e.tensor_copy(out=T_(NTA), in_=P_(5, 1)), reads=[PB_(5, 1)], writes=[tlb[NTA]])
                    for (dst_, src_, mi) in ((NBD, NA, 4), (NTBD, NTA, 4), (E1, NA, 5), (E1T, NTA, 5), (E2, NA, 6)):
                        yield op("pool", lambda e: e.tensor_tensor(out=T_(dst_), in0=T_(src_), in1=gmask[:, mi, :], op=ALU.mult),
                           reads=[tlb[src_], cb], writes=[tlb[dst_]])
                    yield op("dve", lambda e: e.tensor_tensor(out=T_(RTA), in0=T_(NTBD), in1=ident[:], op=ALU.add), reads=[tlb[NTBD], identb], writes=[tlb[RTA]])
                    cur = (NBD, NTBD, RTA)
                    nxt = (NB_, NTB, RTB)
                    for lvl in range(4):
                        n_, nt_, rt_ = cur
                        n2, nt2, rt2 = nxt
                        yield op("pe", lambda e: mm_(e, P_(6, 0), lhsT=T_(nt_), rhs=T_(n_), start=True, stop=True),
                           reads=[tlb[nt_], tlb[n_]], writes=[PB_(6, 0)])
                        yield op("act", lambda e: e.activation(out=T_(n2), in_=P_(6, 0), func=AF.Copy), reads=[PB_(6, 0)], writes=[tlb[n2]])
                        if lvl < 3:
                            yield op("pe", lambda e: mm_(e, P_(7, 0), lhsT=T_(n_), rhs=T_(nt_), start=True, stop=True),
                               reads=[tlb[nt_], tlb[n_]], writes=[PB_(7, 0)])
                            yield op("act", lambda e: e.activation(out=T_(nt2), in_=P_(7, 0), func=AF.Copy), reads=[PB_(7, 0)], writes=[tlb[nt2]])
                        yield op("pe", lambda e: mm_(e, P_(5, 0), lhsT=T_(n2), rhs=T_(rt_), start=True, stop=True),
                           reads=[tlb[n2], tlb[rt_]], writes=[PB_(5, 0)])
                        yield op("dve", lambda e: e.tensor_tensor(out=T_(rt2), in0=P_(5, 0), in1=T_(rt_), op=ALU.add),
                           reads=[PB_(5, 0), tlb[rt_]], writes=[tlb[rt2]])
                        cur, nxt = nxt, cur
                    DT0 = cur[2]
                    yield op("pe", lambda e: e.transpose(out=P_(5, 1), in_=T_(DT0), identity=ident[:]), reads=[tlb[DT0], identb], writes=[PB_(5, 1)])
                    yield op("act", lambda e: e.activation(out=T_(DD), in_=P_(5, 1), func=AF.Copy), reads=[PB_(5, 1)], writes=[tlb[DD]])
                    yield op("pe", lambda e: mm_(e, P_(6, 0), lhsT=T_(E1T), rhs=T_(DD), start=True, stop=True),
                       reads=[tlb[E1T], tlb[DD]], writes=[PB_(6, 0)])
                    yield op("act", lambda e: e.activation(out=T_(PP), in_=P_(6, 0), func=AF.Copy), reads=[PB_(6, 0)], writes=[tlb[PP]])
                    yield op("pe", lambda e: mm_(e, P_(7, 0), lhsT=T_(DT0), rhs=T_(PP), start=True, stop=True),
                       reads=[tlb[DT0], tlb[PP]], writes=[PB_(7, 0)])
                    yield op("dve", lambda e: e.tensor_tensor(out=T_(DD2), in0=P_(7, 0), in1=T_(DD), op=ALU.add),
                       reads=[PB_(7, 0), tlb[DD]], writes=[tlb[DD2]])
                    yield op("pe", lambda e: mm_(e, P_(6, 1), lhsT=T_(E1), rhs=T_(DT0), start=True, stop=True),
                       reads=[tlb[E1], tlb[DT0]], writes=[PB_(6, 1)])
                    yield op("act", lambda e: e.activation(out=T_(XX), in_=P_(6, 1), func=AF.Copy), reads=[PB_(6, 1)], writes=[tlb[XX]])
                    yield op("pe", lambda e: mm_(e, P_(7, 1), lhsT=T_(DD), rhs=T_(XX), start=True, stop=True),
                       reads=[tlb[DD], tlb[XX]], writes=[PB_(7, 1)])
                    yield op("dve", lambda e: e.tensor_tensor(out=T_(DT2), in0=P_(7, 1), in1=T_(DT0), op=ALU.add),
                       reads=[PB_(7, 1), tlb[DT0]], writes=[tlb[DT2]])
                    yield op("pe", lambda e: mm_(e, P_(6, 0), lhsT=T_(E2), rhs=T_(DT2), start=True, stop=True),
                       reads=[tlb[E2], tlb[DT2]], writes=[PB_(6, 0)])
                    yield op("act", lambda e: e.activation(out=T_(XX), in_=P_(6, 0), func=AF.Copy), reads=[PB_(6, 0)], writes=[tlb[XX]])
                    yield op("pe", lambda e: mm_(e, P_(7, 0), lhsT=T_(DD2), rhs=T_(XX), start=True, stop=True),
                       reads=[tlb[DD2], tlb[XX]], writes=[PB_(7, 0)])
                    yield op("dve", lambda e: e.tensor_tensor(out=T_(RTF), in0=P_(7, 0), in1=T_(DT2), op=ALU.add),
                       reads=[PB_(7, 0), tlb[DT2]], writes=[tlb[RTF]])
                    chk(11)
                    RT = RTF
                    yield op("dve", lambda e: e.tensor_scalar(out=T_(VB), in0=vtok[:, c, :], scalar1=col(BETA, c, d, hd), scalar2=None, op0=ALU.mult),
                       reads=[vtokb, gvb], writes=[tlb[VB]])
                    yield op("dve", lambda e: e.tensor_scalar(out=T_(KBG), in0=ktok[:, c, :], scalar1=col(BG, c, d, hd), scalar2=None, op0=ALU.mult),
                       reads=[ktokb, gvb], writes=[tlb[KBG]])
                    yield op("dve", lambda e: e.tensor_scalar(out=T_(KDE), in0=ktok[:, c, :], scalar1=col(KD, c, d, hd), scalar2=None, op0=ALU.mult),
                       reads=[ktokb, gvb], writes=[tlb[KDE]])
                    yield op("pe", lambda e: mm_(e, P_(6, 0), lhsT=T_(RT), rhs=T_(VB), start=True, stop=True),
                       reads=[tlb[RT], tlb[VB]], writes=[PB_(6, 0)])
                    yield op("act", lambda e: e.activation(out=T_(UU), in_=P_(6, 0), func=AF.Copy), reads=[PB_(6, 0)], writes=[tlb[UU]])
                    yield op("pe", lambda e: mm_(e, P_(7, 0), lhsT=T_(KBG), rhs=T_(RT), start=True, stop=True),
                       reads=[tlb[RT], tlb[KBG]], writes=[PB_(7, 0)])
                    yield op("act", lambda e: e.activation(out=T_(WT), in_=P_(7, 0), func=AF.Copy), reads=[PB_(7, 0)], writes=[tlb[WT]])
                    yield op("pe", lambda e: mm_(e, P_(5, 0), lhsT=FM[KT][:, tsl], rhs=FM[QT][:, tsl], start=True, stop=True),
                       reads=[FMB[KT], FMB[QT]], writes=[PB_(5, 0)])
                    yield op("dve", lambda e: e.tensor_tensor(out=T_(ATT), in0=P_(5, 0), in1=T_(MU), op=ALU.mult),
                       reads=[PB_(5, 0), tlb[MU]], writes=[tlb[ATT]])
                    yield op("dve", lambda e: e.tensor_tensor(out=T_(QD), in0=FM[QT][:, tsl], in1=T_(EG), op=ALU.mult),
                       reads=[FMB[QT], tlb[EG]], writes=[tlb[QD]])
                    chk(12)
                    yield op("pe", lambda e: mm_(e, P_(6, 0), lhsT=T_(WT), rhs=T_(SS), start=True, stop=True),
                       reads=[tlb[WT], tlb[SS]], writes=[PB_(6, 0)])
                    yield op("dve", lambda e: e.tensor_tensor(out=T_(VN), in0=T_(UU), in1=P_(6, 0), op=ALU.subtract),
                       reads=[PB_(6, 0), tlb[UU]], writes=[tlb[VN]])
                    op("pe", lambda e: mm_(e, P_(7, 0), lhsT=T_(SS), rhs=T_(QD), start=True, stop=False),
                       reads=[tlb[SS], tlb[QD]], writes=[PB_(7, 0)], inc=False)
                    yield op("pe", lambda e: mm_(e, P_(7, 0), lhsT=T_(VN), rhs=T_(ATT), start=False, stop=True),
                       reads=[tlb[VN], tlb[ATT]], writes=[PB_(7, 0)])
                    yield op("act", lambda e: e.activation(out=OAd[d][:, tsl], in_=P_(7, 0), func=AF.Copy), reads=[PB_(7, 0)], writes=([OAdb[d], FMB[OA]] if d == 0 else [OAdb[d]]))
                    yield op("pe", lambda e: mm_(e, P_(6, 1), lhsT=T_(KDE), rhs=T_(VN), start=True, stop=True),
                       reads=[tlb[KDE], tlb[VN]], writes=[PB_(6, 1)])
                    egl = tl_[:, EG, 127:128] if d == 0 else tl_[:, EG, 0:1]
                    yield op("dve", lambda e: e.scalar_tensor_tensor(out=T_(SS), in0=T_(SS), scalar=egl, in1=P_(6, 1), op0=ALU.mult,
                                                               op1=ALU.add), reads=[PB_(6, 1), tlb[EG], tlb[SS]], writes=[tlb[SS]])
                    if seg_end:
                        dma("sp", st_out[seg, d, hd], T_(SS), reads=[tlb[SS]], writes=[stoutb[(seg * 2 + d) * 8 + hd]])
            for b_ in range(8):
                for s_ in range(4):
                    psl[b_][s_].w = dict(PSB[b_].w)
                    psl[b_][s_].r = dict(PSB[b_].r)
            gens = [gdn_dir(0), gdn_dir(1)]
            if not GDN_INTERLEAVE:
                for g_ in gens:
                    for _ in g_:
                        pass
                gens = []
            while gens:
                for g_ in list(gens):
                    try:
                        next(g_)
                    except StopIteration:
                        gens.remove(g_)
            for b_ in range(8):
                for s_ in range(4):
                    for (dst_, src_) in ((PSB[b_].w, psl[b_][s_].w), (PSB[b_].r, psl[b_][s_].r)):
                        for k_, v_ in src_.items():
                            if k_ not in dst_ or dst_[k_][1] < v_[1]:
                                dst_[k_] = v_
            op("dve", lambda e: e.tensor_tensor(out=FM[OA][:], in0=OAd[0][:], in1=OAd[1][:], op=ALU.add), reads=OAdb, writes=[FMB[OA]])
            chk(13)
            if DEBUG and hd == 0:
                dma("sp", dbg_q[:, :], FM[OA][:], reads=[FMB[OA]], writes=[dbgb])
            partnorm(FM[OA], FMB[OA], None)
            partnorm_apply(FM[OA], FMB[OA], 1.0 / 128.0, EPS)
            if DEBUG and hd == 0:
                dma("sp", dbg_k[:, :], FM[SQ][:], reads=[FMB[SQ]], writes=[dbgb])
            op("dve", lambda e: e.scalar_tensor_tensor(out=yo[:], in0=FM[OA][:], scalar=onorm[:, 0:1], in1=FM[ZT][:], op0=ALU.mult,
                                                       op1=ALU.mult), reads=[FMB[OA], FMB[ZT], cb], writes=[yob])
            dma("sp", ycatT[8 + hd], yo[:], reads=[yob], writes=[ycb[8 + hd]])
            if DEBUG and hd == 0:
                dma("sp", dbg_oa[:, :], FM[OA][:], reads=[FMB[OA]], writes=[dbgb])
                dma("sp", dbg_z[:, :], FM[ZT][:], reads=[FMB[ZT]], writes=[dbgb])
        kb.release(gm)
        kb.release(mk)

        chk(14)
        mk2 = kb.mark()
        wo = kb.sb("wo", [128, 16, D], BF16)
        wob = Buf("wo")
        for q in range(4):
            dma("pool", wo[:, :, q * 512:(q + 1) * 512], w_out[:, q * 512:(q + 1) * 512].rearrange("(c p) n -> p c n", p=128), writes=[wob])
        yT = [kb.sb("yT%d" % i, [128, 16, 128], BF16) for i in range(3)]
        yTb = bufs(3, "yT")
        xr = [kb.sb("xr%d" % i, [128, D], F32) for i in range(3)]
        xrb = bufs(3, "xr")
        hy = kb.sb("hyacc", [128, D], F32)
        hyb = Buf("hyacc")
        tmp = [kb.sb("mtmp%d" % i, [128, 512], F32) for i in range(2)]
        tmpb = bufs(2, "mtmp")
        junk2 = kb.sb("junk2", [128, 512], BF16)
        junk2b = Buf("junk2")
        st2 = kb.sb("st2", [128, 8], F32)
        st2b = Buf("st2")
        Gp = kb.sb("mGp", [128, D], F32)
        gpost = kb.sb("mgpost", [128, D], F32)
        Gpb = Buf("mGp")
        hyg = kb.sb("hyrs", [128, 16], F32)
        dma("sp", Gp[:], modrow[0, 5 * D:6 * D].partition_broadcast(128), reads=[modb], writes=[Gpb])
        dma("sp", gpost[:], norm_post[1, :].partition_broadcast(128), writes=[Gpb])
        op("dve", lambda e: e.tensor_tensor(out=Gp[:], in0=Gp[:], in1=gpost[:], op=ALU.mult), reads=[Gpb], writes=[Gpb])
        op("dve", lambda e: e.tensor_scalar(out=hyg[:], in0=hyss[:], scalar1=1.0 / 1024.0, scalar2=EPS, op0=ALU.mult, op1=ALU.add),
           reads=[hyssb], writes=[hyssb])
        op("act", lambda e: e.activation(out=hyg[:], in_=hyg[:], func=AF.Sqrt), reads=[hyssb], writes=[hyssb])
        op("dve", lambda e: e.reciprocal(out=hyg[:], in_=hyg[:]), reads=[hyssb], writes=[hyssb])
        for t in range(NT):
            s = t % 3
            with nc.allow_non_contiguous_dma(reason="256B rows"):
                dma("sp", yT[s][:], ycatT[:, :, t * 128:(t + 1) * 128].rearrange("c p n -> p c n"), reads=ycb, writes=[yTb[s]])
            dma("sp", xr[s][:], xres[t * 128:(t + 1) * 128, :], reads=[ybuf[t]], writes=[xrb[s]])
            for nbk in range(4):
                for c in range(8):
                    op("pe", lambda e: e.matmul(PS[nbk][:, :], lhsT=yT[s][:, c, :], rhs=wo[:, c, nbk * 512:(nbk + 1) * 512],
                                                start=(c == 0), stop=(c == 7)), reads=[yTb[s], wob], writes=[PSB[nbk]], inc=(c == 7))
                for c in range(8, 16):
                    op("pe", lambda e: e.matmul(PS[4 + nbk][:, :], lhsT=yT[s][:, c, :], rhs=wo[:, c, nbk * 512:(nbk + 1) * 512],
                                                start=(c == 8), stop=(c == 15)), reads=[yTb[s], wob], writes=[PSB[4 + nbk]], inc=(c == 15))
                sl = slice(nbk * 512, (nbk + 1) * 512)
                op("act", lambda e: e.activation(out=hy[:, sl], in_=PS[4 + nbk][:, :], func=AF.Copy), reads=[PSB[4 + nbk]], writes=[hyb])
                op("dve", lambda e: e.scalar_tensor_tensor(out=hy[:, sl], in0=PS[nbk][:, :], scalar=hyg[:, t:t + 1], in1=hy[:, sl],
                                                           op0=ALU.mult, op1=ALU.add), reads=[PSB[nbk], hyssb, hyb], writes=[hyb])
                op("act", lambda e: e.activation(out=junk2[:], in_=hy[:, sl], func=AF.Square, accum_out=st2[:, nbk:nbk + 1]),
                   reads=[hyb], writes=[junk2b, st2b])
            op("dve", lambda e: e.tensor_tensor(out=st2[:, 0:2], in0=st2[:, 0:2], in1=st2[:, 2:4], op=ALU.add), reads=[st2b], writes=[st2b])
            op("dve", lambda e: e.tensor_tensor(out=st2[:, 0:1], in0=st2[:, 0:1], in1=st2[:, 1:2], op=ALU.add), reads=[st2b], writes=[st2b])
            rstd_from_ss(st2[:, 0:1], st2[:, 4:5], st2[:, 5:6], st2b, D)
            for q in range(4):
                sl = slice(q * 512, (q + 1) * 512)
                ts_ = q % 2
                op("dve", lambda e: e.scalar_tensor_tensor(out=tmp[ts_][:], in0=hy[:, sl], scalar=st2[:, 4:5], in1=Gp[:, sl],
                                                           op0=ALU.mult, op1=ALU.mult), reads=[hyb, st2b, Gpb], writes=[tmpb[ts_]])
                op("dve", lambda e: e.tensor_tensor(out=xr[s][:, sl], in0=tmp[ts_][:], in1=xr[s][:, sl], op=ALU.add),
                   reads=[tmpb[ts_], xrb[s]], writes=[xrb[s]])
            dma("sp", (xres if STAGE >= 3 else y_out)[t * 128:(t + 1) * 128, :], xr[s][:], reads=[xrb[s]], writes=[ybuf[t]])
        kb.release(mk2)

    xinbufs = bufs(NT, "xsrc")
    if STAGE == 0:
        dma("sp", y_out[0:128, 0:144], modcol[:], reads=[colb], writes=[ybuf[0]])
        dma("sp", y_out[128:256, 0:48], Acol[:], reads=[colb], writes=[ybuf[1]])
    if STAGE >= 1:
        ffn_stage(0, 0, x_in, xinbufs, True, xres if STAGE >= 2 else y_out)
    if STAGE >= 2 and RUN_MIXER:
        try:
            mixer_stage()
        except StopMix:
            pass
    if STAGE >= 3:
        ffn_stage(2, 1, xres, ybuf, False, y_out)
    kb.barrier()
    return kb


def grid_pos():
    rows = T // 64
    r = np.broadcast_to(np.arange(rows, dtype=np.float32)[:, None], (rows, 64)).reshape(-1)
    col = np.broadcast_to(np.arange(64, dtype=np.float32)[None, :], (rows, 64)).reshape(-1)
    quarter = D // 4
    omega = (1.0 / (np.float32(10000.0) ** (np.arange(quarter, dtype=np.float32) / np.float32(quarter)))).astype(np.float32)
    ar = r[:, None] * omega[None]
    ac = col[:, None] * omega[None]
    return np.concatenate([np.sin(ar), np.cos(ar), np.sin(ac), np.cos(ac)], axis=-1).astype(np.float32)


_HC = {}


def hyena_consts(L):
    if L in _HC:
        return _HC[L]
    import ml_dtypes
    f = np.float32
    nrep = T // L
    idx = np.arange(L, dtype=np.float64)
    tt = (idx / (L - 1)).astype(f)
    bands = np.arange(1, 17, dtype=np.float64)
    ang = (2.0 * math.pi / L) * idx[:, None] * bands[None, :]
    feats = np.concatenate([tt[:, None].astype(np.float64), np.cos(ang), np.sin(ang)], axis=-1).astype(f)
    featsT = np.ascontiguousarray(np.tile(feats, (nrep, 1)).T)
    lagp = np.tile(np.arange(L), nrep)
    col = np.zeros((128, 64), f)
    lag2 = lagp.reshape(16, 128).T
    col[:, 0:16] = -(lag2 / (L - 1.0))
    col[:, 16:32] = (lag2 != 0)
    col[:, 32:48] = (lag2 == 0)
    col[:, 48:64] = (lag2 != 0)
    ph = math.pi * np.outer(idx, idx) / L
    sgn = np.where(idx % 2 == 0, 1.0, -1.0)
    cf = np.cos(ph)
    sf = -np.sin(ph)
    sf[:, 0] = sgn
    wf = np.full(L, 1.0 / L)
    wf[0] = 0.5 / L
    gc = (np.cos(ph) * wf[None, :]).T
    gs = (-np.sin(ph) / L).T
    gs[0, :] = 0.5 / L * sgn

    def tiles(blk):
        full = np.zeros((T, T), f)
        for r in range(nrep):
            full[r * L:(r + 1) * L, r * L:(r + 1) * L] = blk
        return np.ascontiguousarray(full.reshape(16, 128, 16, 128).transpose(2, 1, 0, 3)).astype(ml_dtypes.bfloat16)
    out = {"feats": featsT, "hcol": col, "nrep": np.full((128, 1), float(nrep), f),
           "CF_t": tiles(cf), "SF_t": tiles(sf), "GC_t": tiles(gc), "GS_t": tiles(gs)}
    _HC[L] = out
    return out


def core_inputs(core, inp):
    f = np.float32
    m = {}
    if core < 4:
        m["x"] = np.ascontiguousarray(inp["x_sample"][core])
        m["pos"] = grid_pos()
        m["cvec"] = np.ascontiguousarray(inp["c"][core:core + 1])
    else:
        xp = np.zeros((T, D), f)
        xp[:1024] = inp["x_prompt"][(core - 4) * 4:(core - 4) * 4 + 4].reshape(1024, D)
        m["x"] = xp
        m["pos"] = np.zeros((T, D), f)
        m["cvec"] = np.ascontiguousarray(inp["c_ctx"].reshape(1, D))
    m["ada_w"] = np.ascontiguousarray(inp["ada_w"][0])
    m["ada_b"] = np.ascontiguousarray(inp["ada_b"][0].reshape(1, -1))
    m["norm_pre"] = np.ascontiguousarray(inp["norm_pre"][0])
    m["norm_post"] = np.ascontiguousarray(inp["norm_post"][0])
    m["ffn_wg"] = np.ascontiguousarray(inp["ffn_wg"][0])
    m["ffn_wu"] = np.ascontiguousarray(inp["ffn_wu"][0])
    m["ffn_wd"] = np.ascontiguousarray(inp["ffn_wd"][0])
    m["ident"] = np.eye(128, dtype=f)
    m["w_in"] = np.ascontiguousarray(inp["w_in"][0])
    m["w_out"] = np.ascontiguousarray(inp["w_out"][0])
    m["gdn_conv_w"] = np.ascontiguousarray(inp["gdn_conv_w"][0])
    m["hy_conv_w"] = np.ascontiguousarray(inp["hy_conv_w"][0])
    m["gdn_o_norm"] = np.ascontiguousarray(inp["gdn_o_norm"][0].reshape(1, 128))
    p = np.arange(128)[:, None]
    q = np.arange(128)[None, :]
    def bm(b):
        return (p // b) == (q // b)
    m["gmask"] = np.stack([(p > q), (q >= p), (q > p), (q <= p), bm(32), bm(64) & ~bm(32), ~bm(64)]).astype(f)
    sample = core < 4
    keep = np.ones((1, 8), f) if sample else np.zeros((1, 8), f)
    m["keep"] = keep
    m["bmask"] = np.zeros((1, 7), f) if sample else np.ones((1, 7), f)
    m["dtb"] = np.ascontiguousarray(np.tile(inp["gdn_dt_bias"][0].reshape(1, 16), (1, 16)))
    m["alog"] = np.ascontiguousarray(np.tile(inp["gdn_a_log"][0].reshape(1, 16), (1, 16)))
    for nm in ("hy_f_w1", "hy_f_w2", "hy_f_w3", "hy_decay", "hy_bias"):
        m[nm] = np.ascontiguousarray(inp[nm][0])
    m["hy_f_b1"] = np.ascontiguousarray(inp["hy_f_b1"][0].reshape(1, 64))
    m["hy_f_b2"] = np.ascontiguousarray(inp["hy_f_b2"][0].reshape(1, 64))
    m["hy_out_norm"] = np.ascontiguousarray(inp["hy_out_norm"][0].reshape(1, 1024))
    m.update(hyena_consts(2048 if sample else 256))
    m["s0"] = np.ascontiguousarray(inp["state_gdn"][core, 0]) if sample else np.zeros((2, 8, 128, 128), f)
    return m


def kernel(**inputs):
    inp = {k: np.asarray(v) for k, v in inputs.items()}
    kb = build()
    in_maps = [core_inputs(c, inp) for c in range(8)]
    res = run_bass_kernel_spmd(kb.nc, in_maps, core_ids=list(range(8)))
    ys = [r["y"] for r in res.results]
    y_sample = np.stack(ys[:4], axis=0).astype(np.float32)
    y_prompt = np.concatenate([ys[c][:1024].reshape(4, 256, D) for c in range(4, 8)], axis=0).astype(np.float32)
    new_state = np.zeros((16, 1, 2, 8, 128, 128), np.float32)
    for c in range(4, 8):
        so = res.results[c]["st_out"]
        for sq in range(4):
            new_state[(c - 4) * 4 + sq, 0] = so[sq]
    return (y_prompt, y_sample, new_state)
```
